# Optimizing a Trainium2 kernel written in Bass

```python
import math
import jax
import jax.numpy as jnp
from jax import lax
import numpy as np

D_MODEL = 1024
BATCH = 8
SEQ = 2048
DEPTH = 1
DEC_BATCH = 128
DEC_SEQ = 4
PAST_LEN = 16384
PAGE_SIZE = 128

POOL_WINDOWS = (2, 4, 8, 16)
POOL_GROUPS = 4
POOL_WIDTH = D_MODEL // 2
POOL_GROUP_DIM = POOL_WIDTH // POOL_GROUPS
POOL_BUF = max(POOL_WINDOWS) - 1
M_HEADS = 4
M_WIDTH = D_MODEL
M_HEAD_DIM = M_WIDTH // M_HEADS
M_CHUNK = 64
D_FF = 4 * D_MODEL
N_ADA = 6
EPS = 1e-6
IN_SIZES = (POOL_WIDTH, M_WIDTH, M_WIDTH, M_WIDTH, M_WIDTH, M_HEADS, M_HEADS, D_MODEL, D_MODEL)
IN_WIDTH = POOL_WIDTH + 4 * M_WIDTH + 2 * M_HEADS + 2 * D_MODEL

kernel_name = "hybrid_pool_mlstm_decoder_step"


def rmsnorm(x, g):
    xf = x.astype(jnp.float32)
    y = xf * lax.rsqrt(jnp.mean(xf * xf, axis=-1, keepdims=True) + EPS)
    return (y * g.astype(jnp.float32)).astype(x.dtype)


def split_cols(z):
    outs, start = [], 0
    for s in IN_SIZES:
        outs.append(z[..., start:start + s])
        start += s
    return outs


def pool_mixer(u, buf, offset, w_group, scale):
    B, S, _ = u.shape
    full = jnp.concatenate([buf.astype(jnp.float32), u.astype(jnp.float32)], axis=1)
    cs = jnp.concatenate([jnp.zeros((B, 1, POOL_WIDTH), jnp.float32), jnp.cumsum(full, axis=1)], axis=1)
    end = cs[:, POOL_BUF + 1:]
    pos = offset + jnp.arange(S)
    outs = []
    for gi, w in enumerate(POOL_WINDOWS):
        lo, hi = gi * POOL_GROUP_DIM, (gi + 1) * POOL_GROUP_DIM
        start = POOL_BUF + 1 - w
        win = end[:, :, lo:hi] - cs[:, start:start + S, lo:hi]
        cnt = jnp.minimum(pos + 1, w).astype(jnp.float32)
        outs.append(win / cnt[None, :, None])
    pooled = jnp.concatenate(outs, axis=-1) - full[:, POOL_BUF:]
    r = pooled.reshape(B, S, POOL_GROUPS, POOL_GROUP_DIM)
    y = jnp.einsum('bsgc,gcd->bsgd', r, w_group.astype(jnp.float32)).reshape(B, S, POOL_WIDTH)
    y = y * scale.astype(jnp.float32)
    return y, full[:, -POOL_BUF:]


def mlstm(q, k, v, i_log, logf, C0, n0, m0):
    B, S, H, Dh = q.shape
    L = math.gcd(S, M_CHUNK)
    NC = S // L

    def to_chunks(a):
        return a.reshape((B, NC, L, H) + a.shape[3:]).swapaxes(2, 3).swapaxes(0, 1)

    xs = (to_chunks(q), to_chunks(k), to_chunks(v), to_chunks(i_log), to_chunks(logf))
    mask = jnp.tril(jnp.ones((L, L), dtype=bool))

    def step(carry, inp):
        C, n, m = carry
        qc, kc, vc, ic, fc = inp
        b = jnp.cumsum(fc, axis=-1)
        g = b + m[..., None]
        dmat = jnp.where(mask, b[..., :, None] - b[..., None, :] + ic[..., None, :], -jnp.inf)
        mt = jnp.maximum(g, jnp.max(dmat, axis=-1))
        w_inter = jnp.exp(g - mt)
        s = jnp.einsum('bhtd,bhsd->bhts', qc, kc) * jnp.exp(dmat - mt[..., None])
        num = w_inter[..., None] * jnp.einsum('bhtd,bhde->bhte', qc, C) + jnp.einsum('bhts,bhse->bhte', s, vc)
        den = w_inter * jnp.einsum('bhtd,bhd->bht', qc, n) + jnp.sum(s, axis=-1)
        h = num / jnp.maximum(jnp.abs(den), jnp.exp(-mt))[..., None]
        bL = b[..., -1]
        a = bL[..., None] - b + ic
        m_new = jnp.maximum(bL + m, jnp.max(a, axis=-1))
        decay = jnp.exp(bL + m - m_new)
        ws = jnp.exp(a - m_new[..., None])
        C_new = decay[..., None, None] * C + jnp.einsum('bhs,bhsd,bhse->bhde', ws, kc, vc)
        n_new = decay[..., None] * n + jnp.einsum('bhs,bhsd->bhd', ws, kc)
        return (C_new, n_new, m_new), h

    carry0 = (C0.astype(jnp.float32), n0.astype(jnp.float32), m0.astype(jnp.float32))
    (Cn, nn_, mn), hs = lax.scan(step, carry0, xs)
    h = hs.swapaxes(0, 1).swapaxes(2, 3).reshape(B, S, H, Dh)
    return h, Cn, nn_, mn


def hybrid_layer(x, c, pool_buf, C0, n0, m0, offset, g_mix, g_ffn, w_ada, b_ada, w_in, b_i, b_f,
                 w_pool_group, pool_scale, w_pool_out, m_norm, w_m_out, w_out, w1, w2):
    B, S, _ = x.shape
    ada = jax.nn.silu(c) @ w_ada + b_ada
    sh1, sc1, gt1, sh2, sc2, gt2 = jnp.split(ada, N_ADA, axis=-1)
    h = rmsnorm(x, g_mix) * (1 + sc1[:, None]) + sh1[:, None]
    z = h @ w_in
    u, q, k, v, o, i_pre, f_pre, g_a, g_b = split_cols(z)
    y_pool, pool_new = pool_mixer(u, pool_buf, offset, w_pool_group, pool_scale)
    q = q.reshape(B, S, M_HEADS, M_HEAD_DIM).astype(jnp.float32)
    k = k.reshape(B, S, M_HEADS, M_HEAD_DIM).astype(jnp.float32) * (M_HEAD_DIM ** -0.5)
    v = v.reshape(B, S, M_HEADS, M_HEAD_DIM).astype(jnp.float32)
    i_log = (i_pre + b_i).astype(jnp.float32)
    logf = jax.nn.log_sigmoid((f_pre + b_f).astype(jnp.float32))
    hm, Cn, nn_, mn = mlstm(q, k, v, i_log, logf, C0, n0, m0)
    hm = rmsnorm(hm, m_norm).reshape(B, S, M_WIDTH) * jax.nn.sigmoid(o.astype(jnp.float32))
    merged = jax.nn.sigmoid(g_a) * (y_pool @ w_pool_out) + jax.nn.sigmoid(g_b) * (hm @ w_m_out)
    x = x + gt1[:, None] * (merged @ w_out)
    h2 = rmsnorm(x, g_ffn) * (1 + sc2[:, None]) + sh2[:, None]
    x = x + gt2[:, None] * (jnp.square(jax.nn.relu(h2 @ w1)) @ w2)
    return x, pool_new, Cn, nn_, mn


def setup_inputs(seed: int = 0) -> dict:
    key = jax.random.key(seed)
    ks = jax.random.split(key, 32)
    f32 = jnp.float32
    nrm = lambda k_, shape, s: jax.random.normal(k_, shape, f32) * s
    d = {}
    d['x_prompt'] = nrm(ks[0], (BATCH, SEQ, D_MODEL), 1.0)
    d['x_sample'] = nrm(ks[1], (DEC_BATCH, DEC_SEQ, D_MODEL), 1.0)
    d['state_pool'] = nrm(ks[2], (DEPTH, DEC_BATCH, POOL_BUF, POOL_WIDTH), 1.0)
    d['state_C'] = nrm(ks[3], (DEPTH, DEC_BATCH, M_HEADS, M_HEAD_DIM, M_HEAD_DIM), 0.3)
    d['state_n'] = nrm(ks[4], (DEPTH, DEC_BATCH, M_HEADS, M_HEAD_DIM), 0.3)
    d['state_m'] = nrm(ks[5], (DEPTH, DEC_BATCH, M_HEADS), 1.0)
    d['c_prompt'] = nrm(ks[6], (BATCH, D_MODEL), 1.0)
    d['c_sample'] = nrm(ks[7], (DEC_BATCH, D_MODEL), 1.0)
    d['g_mix'] = 1.0 + nrm(ks[8], (DEPTH, D_MODEL), 0.05)
    d['g_ffn'] = 1.0 + nrm(ks[9], (DEPTH, D_MODEL), 0.05)
    d['w_ada'] = nrm(ks[10], (DEPTH, D_MODEL, N_ADA * D_MODEL), D_MODEL ** -0.5)
    d['b_ada'] = nrm(ks[11], (DEPTH, N_ADA * D_MODEL), 0.02)
    d['w_in'] = nrm(ks[12], (DEPTH, D_MODEL, IN_WIDTH), D_MODEL ** -0.5)
    d['b_i'] = nrm(ks[13], (DEPTH, M_HEADS), 0.1)
    d['b_f'] = 3.0 + nrm(ks[14], (DEPTH, M_HEADS), 0.5)
    d['w_pool_group'] = nrm(ks[15], (DEPTH, POOL_GROUPS, POOL_GROUP_DIM, POOL_GROUP_DIM), POOL_GROUP_DIM ** -0.5)
    d['pool_scale'] = 1.0 + nrm(ks[16], (DEPTH, POOL_WIDTH), 0.1)
    d['w_pool_out'] = nrm(ks[17], (DEPTH, POOL_WIDTH, D_MODEL), POOL_WIDTH ** -0.5)
    d['m_norm'] = 1.0 + nrm(ks[18], (DEPTH, M_HEADS, M_HEAD_DIM), 0.05)
    d['w_m_out'] = nrm(ks[19], (DEPTH, M_WIDTH, D_MODEL), M_WIDTH ** -0.5)
    d['w_out'] = nrm(ks[20], (DEPTH, D_MODEL, D_MODEL), D_MODEL ** -0.5)
    d['w1'] = nrm(ks[21], (DEPTH, D_MODEL, D_FF), D_MODEL ** -0.5)
    d['w2'] = nrm(ks[22], (DEPTH, D_FF, D_MODEL), D_FF ** -0.5)
    d['g_final'] = 1.0 + nrm(ks[23], (D_MODEL,), 0.05)
    return d


def reference(x_prompt, x_sample, state_pool, state_C, state_n, state_m, c_prompt, c_sample,
              g_mix, g_ffn, w_ada, b_ada, w_in, b_i, b_f, w_pool_group, pool_scale, w_pool_out,
              m_norm, w_m_out, w_out, w1, w2, g_final):
    xp, xs = x_prompt, x_sample
    pp_l, cp_l, np_l, mp_l = [], [], [], []
    ps_l, cs_l, ns_l, ms_l = [], [], [], []
    for l in range(DEPTH):
        wl = (g_mix[l], g_ffn[l], w_ada[l], b_ada[l], w_in[l], b_i[l], b_f[l], w_pool_group[l],
              pool_scale[l], w_pool_out[l], m_norm[l], w_m_out[l], w_out[l], w1[l], w2[l])
        Bp = xp.shape[0]
        buf0 = jnp.zeros((Bp, POOL_BUF, POOL_WIDTH), jnp.float32)
        C0 = jnp.zeros((Bp, M_HEADS, M_HEAD_DIM, M_HEAD_DIM), jnp.float32)
        n0 = jnp.zeros((Bp, M_HEADS, M_HEAD_DIM), jnp.float32)
        m0 = jnp.zeros((Bp, M_HEADS), jnp.float32)
        xp, pp, cp, npr, mp = hybrid_layer(xp, c_prompt, buf0, C0, n0, m0, 0, *wl)
        xs, ps, csm, ns, ms = hybrid_layer(xs, c_sample, state_pool[l], state_C[l], state_n[l], state_m[l], PAST_LEN, *wl)
        pp_l.append(pp); cp_l.append(cp); np_l.append(npr); mp_l.append(mp)
        ps_l.append(ps); cs_l.append(csm); ns_l.append(ns); ms_l.append(ms)
    y_prompt = rmsnorm(xp, g_final)
    y_sample = rmsnorm(xs, g_final)
    return (y_prompt, y_sample, jnp.stack(pp_l), jnp.stack(cp_l), jnp.stack(np_l), jnp.stack(mp_l),
            jnp.stack(ps_l), jnp.stack(cs_l), jnp.stack(ns_l), jnp.stack(ms_l))
```

```python
import numpy as np
from contextlib import ExitStack
import concourse.bass as bass
import concourse.mybir as mybir
from concourse.bass_utils import run_bass_kernel_spmd

F32 = mybir.dt.float32
BF16 = mybir.dt.bfloat16
AF = mybir.ActivationFunctionType
ALU = mybir.AluOpType

ENGS = ("pe", "act", "dve", "pool", "sp")
ATTACH = True


class Buf:
    __slots__ = ("name", "writer", "readers", "dsem", "dcount")

    def __init__(self, name):
        self.name = name
        self.writer = None
        self.readers = []
        self.dsem = None
        self.dcount = 0


class Ins:
    __slots__ = ("fn", "waits", "needed", "dma", "rank", "attach")

    def __init__(self, fn, waits, dma=None, attach=False):
        self.fn = fn
        self.waits = waits
        self.needed = False
        self.dma = dma
        self.rank = None
        self.attach = attach


class FW:
    def __init__(self, nc, stack, same_engine_sync=True):
        self.nc = nc
        self.stack = stack
        self.streams = {e: [] for e in ENGS}
        self.seen = {e: {} for e in ENGS}
        self.same_engine_sync = same_engine_sync
        self.esem = {}
        for e in ("pe", "act", "dve", "pool"):
            self.esem[e] = stack.enter_context(nc.semaphore("es_" + e))
        self.dsems = []
        self.out_events = []
        self.n_bufs = 0
        self.pend = {}

    def buf(self, name=None):
        self.n_bufs += 1
        return Buf(name or f"b{self.n_bufs}")

    def bufs(self, n, name="b"):
        return [self.buf(f"{name}{i}") for i in range(n)]

    def _collect(self, eng, reads, writes):
        deps = {}

        def add(ev):
            if ev is None:
                return
            key = (ev[0], ev[1])
            if key not in deps or deps[key] < ev[2]:
                deps[key] = ev[2]

        for b in reads:
            add(b.writer)
        for b in writes:
            add(b.writer)
            for r in b.readers:
                add(r)
        waits = []
        seen = self.seen[eng]
        for key, idx in deps.items():
            if key[0] == "e" and key[1] == eng:
                if eng in ("pe", "sp") or not self.same_engine_sync:
                    continue
            if seen.get(key, -1) >= idx:
                continue
            seen[key] = idx
            waits.append((key[0], key[1], idx))
            if key[0] == "e":
                self.streams[key[1]][idx].needed = True
        return waits

    def op(self, eng, fn, reads=(), writes=(), attach=False):
        waits = self._collect(eng, reads, writes)
        idx = len(self.streams[eng])
        self.streams[eng].append(Ins(fn, waits, attach=attach and ATTACH))
        ev = ("e", eng, idx)
        for b in reads:
            b.readers.append(ev)
        for b in writes:
            b.writer = ev
            b.readers = []
        return ev

    def dma(self, q, out, in_, reads=(), writes=(), track=None, is_output=False, **kw):
        if track is None:
            track = writes[0] if writes else reads[0]
        if track.dsem is None:
            track.dsem = len(self.dsems)
            self.dsems.append(self.stack.enter_context(self.nc.semaphore(f"ds{track.dsem}")))
        waits = self._collect(q, reads, writes)
        track.dcount += 16
        ev = ("d", track.dsem, track.dcount)
        fn = lambda e, out=out, in_=in_, kw=kw: e.dma_start(out=out, in_=in_, **kw)
        self.streams[q].append(Ins(fn, waits, dma=(track.dsem, track.dcount)))
        for b in reads:
            b.readers.append(ev)
        for b in writes:
            b.writer = ev
            b.readers = []
        if is_output:
            self.out_events.append(ev)
        if q == "sp":
            self.pend[track.dsem] = track.dcount
        return ev

    def barrier(self, engines=("act", "dve", "sp")):
        last = {}
        for e in ("pe", "act", "dve"):
            s = self.streams[e]
            for i in range(len(s) - 1, -1, -1):
                if s[i].dma is None and s[i].fn is not None:
                    last[e] = i
                    break
        for e in engines:
            waits = []
            for src, idx in last.items():
                if src == e and (e == "pe" or not self.same_engine_sync):
                    continue
                key = ("e", src)
                if self.seen[e].get(key, -1) >= idx:
                    continue
                self.seen[e][key] = idx
                self.streams[src][idx].needed = True
                waits.append(("e", src, idx))
            if e != "sp":
                for sem, cnt in self.pend.items():
                    key = ("d", sem)
                    if self.seen[e].get(key, -1) >= cnt:
                        continue
                    self.seen[e][key] = cnt
                    waits.append(("d", sem, cnt))
            if waits:
                self.streams[e].append(Ins(None, waits))

    def finish(self):
        fin = {}
        for ev in self.out_events:
            key = (ev[0], ev[1])
            fin[key] = max(fin.get(key, 0), ev[2])
        self.streams["sp"].append(Ins(None, [(k[0], k[1], v) for k, v in fin.items()]))
        for e in ("pe", "act", "dve", "pool"):
            r = 0
            for ins in self.streams[e]:
                if ins.needed:
                    r += 1
                    ins.rank = r
        engmap = {"pe": "tensor", "act": "scalar", "dve": "vector", "pool": "gpsimd", "sp": "sync"}

        def replay(eobj, e):
            for ins in self.streams[e]:
                ws = ins.waits
                last = None
                if ins.attach and ins.fn is not None and ws:
                    last = ws[-1]
                    ws = ws[:-1]
                for w in ws:
                    if w[0] == "e":
                        eobj.wait_ge(self.esem[w[1]], self.streams[w[1]][w[2]].rank)
                    else:
                        eobj.wait_ge(self.dsems[w[1]], w[2])
                if ins.fn is None:
                    continue
                bi = ins.fn(eobj)
                if last is not None:
                    if last[0] == "e":
                        bi._wait_ge(self.esem[last[1]], self.streams[last[1]][last[2]].rank)
                    else:
                        bi._wait_ge(self.dsems[last[1]], last[2])
                if ins.dma is not None:
                    bi.then_inc(self.dsems[ins.dma[0]], 16)
                elif ins.needed:
                    bi.then_inc(self.esem[e], 1)

        with self.nc.Block() as block:
            for e in ENGS:
                if self.streams[e]:
                    getattr(block, engmap[e])(lambda eobj, e=e: replay(eobj, e))

    def stats(self):
        return {e: (len(s), sum(1 for i in s if i.needed), sum(len(i.waits) for i in s))
                for e, s in self.streams.items()}


P = 128
D = 1024
NKC = 8
SEQ = 2048
NS_CORE = 16
ST = 4
SS_CORE = NS_CORE * ST
NS = 4
SS = NS * ST
HD = 256
NH = 4
NBH = NS * NH
DFF = 4096
COL_U, COL_Q, COL_K, COL_V, COL_O, COL_I, COL_F, COL_GA, COL_GB = 0, 512, 1536, 2560, 3584, 4608, 4612, 4616, 5640
INW = 6664
EPS = 1e-6
POOL_W = (2, 4, 8, 16)

NT_BLK = 4
NBLK = SEQ // (NT_BLK * P)
TB = NT_BLK * P
TT = TB + SS
NTL = NT_BLK + 1
NWB = 4


def _consts():
    c = {}
    c["identf"] = np.eye(P, dtype=np.float32)
    s = np.arange(P)
    c["maskp"] = (s[:, None] <= s[None, :]).astype(np.float32)
    t = np.arange(SS)
    c["masks"] = ((t[:, None] // ST == t[None, :] // ST) & (t[:, None] <= t[None, :])).astype(np.float32)
    cm = (np.arange(NS)[:, None] == (t[None, :] // ST)).astype(np.float32)
    c["cm"] = np.broadcast_to(cm[None], (P, NS, SS)).reshape(P, NS * SS).copy()
    c["rm"] = (t[:, None] // ST == np.arange(NS)[None, :]).astype(np.float32)
    ic = np.zeros((P, 4, 15), np.float32)
    for g, w in enumerate(POOL_W):
        ic[:, g, :] = 1.0 / np.minimum(np.arange(15) + 1, w)
    c["icnt"] = ic.reshape(P, 60)
    c["ones4"] = np.ones((4, P), np.float32)
    c["onesrow"] = np.ones((4, TT), np.float32)
    return c


class MK:
    def __init__(self, dbg=()):
        self.dbg = set(dbg)
        self.nc = bass.Bass("TRN2", target_bir_lowering=False)
        self.st = ExitStack()
        self.fw = FW(self.nc, self.st)
        self.dbg_out = {}
        self.build()

    def din(self, name, shape):
        return self.nc.dram_tensor(name, list(shape), F32, kind="ExternalInput").ap()

    def dout(self, name, shape):
        return self.nc.dram_tensor(name, list(shape), F32, kind="ExternalOutput").ap()

    def sb(self, name, shape, dt=F32):
        return self.st.enter_context(self.nc.sbuf_tensor("s_" + name, list(shape), dt))

    def carve(self, nbytes):
        off = self.aoff
        self.aoff += (nbytes + 63) // 64 * 64
        assert self.aoff <= self.asize, (self.aoff, self.asize)
        return off

    def aview(self, off, n, dt, parts=P):
        if dt == BF16:
            return self.arena[0:parts, off // 2: off // 2 + n]
        return self.arena[0:parts, off // 2: off // 2 + 2 * n].bitcast(F32)

    def act(self, out, in_, func, r=(), w=(), **kw):
        return self.fw.op("act", lambda e: e.activation(out=out, in_=in_, func=func, **kw), r, w, attach=("accum_out" not in kw))

    def mm(self, out, lhsT, rhs, start, stop, r=(), w=()):
        return self.fw.op("pe", lambda e: e.matmul(out, lhsT=lhsT, rhs=rhs, start=start, stop=stop), r, w)

    def tp(self, out, in_, ident, r=(), w=()):
        return self.fw.op("pe", lambda e: e.transpose(out=out, in_=in_, identity=ident), r, w)

    def tt(self, out, in0, in1, op, r=(), w=(), eng="dve"):
        return self.fw.op(eng, lambda e: e.tensor_tensor(out=out, in0=in0, in1=in1, op=op), r, w, attach=True)

    def ts(self, out, in0, s1, s2, op0, op1=None, r=(), w=(), eng="dve"):
        if op1 is None:
            return self.fw.op(eng, lambda e: e.tensor_scalar(out=out, in0=in0, scalar1=s1, scalar2=None, op0=op0), r, w, attach=True)
        return self.fw.op(eng, lambda e: e.tensor_scalar(out=out, in0=in0, scalar1=s1, scalar2=s2, op0=op0, op1=op1), r, w, attach=True)

    def stt(self, out, in0, scalar, in1, op0, op1, r=(), w=()):
        return self.fw.op("dve", lambda e: e.scalar_tensor_tensor(out=out, in0=in0, scalar=scalar, in1=in1, op0=op0, op1=op1), r, w, attach=True)

    def cp(self, out, in_, r=(), w=(), eng="dve"):
        return self.fw.op(eng, lambda e: e.tensor_copy(out=out, in_=in_), r, w, attach=True)

    def recip(self, out, in_, r=(), w=()):
        return self.fw.op("dve", lambda e: e.reciprocal(out=out, in_=in_), r, w, attach=True)

    def memset(self, ap, val, w=(), eng="dve"):
        return self.fw.op(eng, lambda e: e.memset(ap, val), (), w)

    def load(self, out, in_, w, q="sp", **kw):
        return self.fw.dma(q, out, in_, writes=w, **kw)

    def store(self, out, in_, r, track=None):
        return self.fw.dma("sp", out, in_, reads=r, writes=[self.fw.buf("o")], track=track or r[0], is_output=True)

    def bank(self):
        i = self.gen_banks[self.gen_ptr % len(self.gen_banks)]
        self.gen_ptr += 1
        return self.ps[:, i, :], self.pb[i]

    def bankN(self):
        i = (3, 6)[self.n_ptr % 2]
        self.n_ptr += 1
        return self.ps[:, i, :], self.pb[i]

    def pair(self, which=0):
        i = (4, 6)[which]
        return self.ps[:, i:i + 2, :], [self.pb[i], self.pb[i + 1]]

    def wload(self, src, shape):
        a, b = shape
        n = a * b
        nh = 2 * NWB
        if n <= 2048:
            i = self.w_ptr % nh
            self.w_ptr += 1
            bufs = [self.wb[i]]
        else:
            if self.w_ptr % 2:
                self.w_ptr += 1
            i = self.w_ptr % nh
            self.w_ptr += 2
            bufs = [self.wb[i], self.wb[i + 1]]
        view = self.wring[:, i * 2048:i * 2048 + n].rearrange("p (a b) -> p a b", a=a)
        self.fw.dma("pool", view, src, writes=bufs, track=bufs[0])
        return view, bufs

    def wpanel(self, w_ap, c0, ncols):
        src = w_ap[:, c0:c0 + ncols].rearrange("(kc p) n -> p kc n", p=P)
        return self.wload(src, (NKC, ncols))

    def dump(self, name, ap, bufs, shape, parts=P):
        if name not in self.dbg:
            return
        o = self.nc.dram_tensor("dbg_" + name, [parts] + list(shape[1:]), ap.dtype, kind="ExternalOutput").ap()
        self.fw.dma("sp", o, ap, reads=bufs, writes=[self.fw.buf()], track=bufs[0], is_output=True)
        self.dbg_out[name] = "dbg_" + name

    def build(self):
        nc, fw = self.nc, self.fw
        d = self.d = {}
        for name, shape in [("xp", (SEQ, D)), ("xs", (SS_CORE, D)), ("call", (17, D)), ("spool", (NS_CORE * 15, 512)),
                            ("sC", (NS_CORE * NH, HD, HD)), ("sn", (NS_CORE * NH, HD)), ("smT", (NH, NS_CORE)),
                            ("vecs", (68, P)), ("bif", (NH, 2)), ("m_norm", (NH, HD)), ("g_final", (D,)),
                            ("w_ada", (D, 6 * D)), ("w_in", (D, INW)), ("w_pg", (4, P, P)), ("w_po", (512, D)),
                            ("w_mo", (D, D)), ("w_out", (D, D)), ("w1", (D, DFF)), ("w2", (DFF, D)),
                            ("identf", (P, P)), ("maskp", (P, P)), ("masks", (SS, SS)), ("cm", (P, NS * SS)),
                            ("rm", (SS, NS)), ("icnt", (P, 60)), ("ones4", (4, P)), ("onesrow", (4, TT))]:
            d[name] = self.din(name, shape)
        o = self.o = {}
        for name, shape in [("yp", (SEQ, D)), ("ys", (SS_CORE, D)), ("npp", (15, 512)), ("ncp", (NH, HD, HD)),
                            ("nnp", (8, P)), ("nmp", (NH, 1)), ("nps", (NS_CORE, 15, 512)), ("ncs", (NS_CORE * NH, HD, HD)),
                            ("nns", (NS_CORE * NH, HD)), ("nmsT", (NH, NS_CORE))]:
            o[name] = self.dout(name, shape)

        self.ps = self.st.enter_context(nc.psum_tensor("ps", [P, 8, 512], F32))
        self.pb = fw.bufs(8, "bank")
        self.gen_banks = (0, 1, 2)
        self.gen_ptr = 0
        self.n_ptr = 0

        self.xb = self.sb("xb", [P, NTL, D]); self.b_xb = fw.bufs(NTL, "xb")
        self.hT = self.sb("hT", [P, NKC, TT], BF16); self.b_hT = fw.bufs(NTL, "hT")
        self.mT = self.sb("mT", [P, NKC, TT], BF16)
        self.wring = self.sb("wring", [P, NWB * 4096], BF16); self.wb = fw.bufs(2 * NWB, "wr"); self.w_ptr = 0
        self.Cst = self.sb("Cst", [P, NH, 2, 257]); self.b_C = fw.bufs(NH, "C")
        self.Csb = self.sb("Csb", [P, NH, 2, 258], BF16); self.b_Csb = fw.bufs(NH, "Csb")
        self.asize = 56 * 1024
        self.arena = self.sb("arena", [P, self.asize // 2], BF16)

        self.final_pending = None
        self.setup_consts()
        self.gen_banks = (0, 1, 2, 3)
        self.spool_load(0)
        gen0 = self.prep_gen(0)
        next(gen0)
        self.setup_ada()
        self.gen_banks = (0, 1, 2, 3)
        for _ in gen0:
            pass
        for b in range(NBLK):
            self.block(b)
        fw.finish()

    def setup_consts(self):
        fw, d = self.fw, self.d
        self.identf = self.sb("identf", [P, P]); self.b_idf = fw.buf()
        self.load(self.identf[:], d["identf"], [self.b_idf])
        self.identb = self.sb("identb", [P, P], BF16); self.b_idb = fw.buf()
        self.cp(self.identb[:], self.identf[:], r=[self.b_idf], w=[self.b_idb])
        self.maskp = self.sb("maskp", [P, P]); self.b_mk = fw.buf()
        self.load(self.maskp[:], d["maskp"], [self.b_mk])
        self.masks = self.sb("masks", [SS, SS])
        self.load(self.masks[:], d["masks"], [self.b_mk])
        cmf = self.aview(0, NS * SS, F32)
        b_cmf = fw.buf()
        self.load(cmf, d["cm"], [b_cmf])
        self.cmb = self.sb("cmb", [P, NS, SS], BF16); self.b_cm = fw.buf()
        self.cp(self.cmb[:].rearrange("p a b -> p (a b)"), cmf, r=[b_cmf], w=[self.b_cm])
        self.rm = self.sb("rm", [SS, NS])
        self.load(self.rm[:], d["rm"], [self.b_cm])
        self.icnt = self.sb("icnt", [P, 4, 15])
        self.load(self.icnt[:].rearrange("p a b -> p (a b)"), d["icnt"], [self.b_cm])
        self.ones4 = self.sb("ones4", [4, P])
        self.load(self.ones4[:], d["ones4"], [self.b_cm])
        self.onesrow = self.sb("onesrow", [4, TT])
        self.load(self.onesrow[:], d["onesrow"], [self.b_cm])
        self.b_const = self.b_cm
        self.cols = self.sb("cols", [P, 4]); self.b_cols = fw.buf()
        self.memset(self.cols[:, 0:1], 1.0, w=[self.b_cols])
        self.memset(self.cols[:, 1:2], EPS, w=[self.b_cols])
        self.onec = self.cols[:, 0:1]
        vec_sb = self.sb("vec_sb", [68, P]); b_v = fw.buf()
        self.load(vec_sb[:], d["vecs"], [b_v])
        self.vecT = self.sb("vecT", [P, 68]); self.b_vecT = fw.buf()
        pst, pbk = self.bank()
        self.tp(pst[:, 0:68], vec_sb[:], self.identf[0:68, 0:68], r=[b_v, self.b_idf], w=[pbk])
        self.cp(self.vecT[:], pst[:, 0:68], r=[pbk], w=[self.b_vecT])
        self.gmixT = self.vecT[:, 0:8]
        self.gffnT = self.vecT[:, 8:16]
        self.badaT = self.vecT[:, 16:64]
        self.pscT = self.vecT[:, 64:68]
        self.bif = self.sb("bif", [NH, 4]); self.b_bif = fw.buf()
        self.load(self.bif[:, 0:2], d["bif"], [self.b_bif])
        self.ts(self.bif[:, 2:3], self.bif[:, 1:2], -1.0, None, ALU.mult, r=[self.b_bif], w=[self.b_bif])
        self.mnh = self.sb("mnh", [P, D]); self.b_mnh = fw.buf()
        self.load(self.mnh[:], d["m_norm"].rearrange("a b -> (a b)").partition_broadcast(P), [self.b_mnh])
        self.ts(self.mnh[:], self.mnh[:], 0.5, None, ALU.mult, r=[self.b_mnh], w=[self.b_mnh])
        self.gfin = self.sb("gfin", [P, D]); self.b_gfin = fw.buf()
        self.load(self.gfin[:], d["g_final"].partition_broadcast(P), [self.b_gfin])
        self.wg = self.sb("wg", [P, NKC, 8], BF16); self.b_wg = fw.buf()
        fw.dma("pool", self.wg[:], d["w_in"][:, COL_I:COL_I + 8].rearrange("(kc p) n -> p kc n", p=P), writes=[self.b_wg])
        self.wpg = self.sb("wpg", [P, 4, P], BF16); self.b_wpg = fw.buf()
        fw.dma("pool", self.wpg[:], d["w_pg"].rearrange("g c d -> c g d"), writes=[self.b_wpg])
        self.memset(self.Cst[:].rearrange("p a b c -> p (a b c)"), 0.0, w=self.b_C)
        self.memset(self.Csb[:].rearrange("p a b c -> p (a b c)"), 0.0, w=self.b_Csb)
        self.uhist = self.sb("uhist", [P, 4, 15]); self.b_uh = fw.buf()
        self.memset(self.uhist[:].rearrange("p a b -> p (a b)"), 0.0, w=[self.b_uh])
        self.mprev = self.sb("mprev", [NH, 2]); self.b_mprev = fw.buf()
        self.memset(self.mprev[:], 0.0, w=[self.b_mprev])
        self.m0s = self.sb("m0s", [NH, NS_CORE]); self.b_m0s = fw.buf()
        self.load(self.m0s[:], d["smT"], [self.b_m0s])
        self.g_i = self.sb("g_i", [NH, TT]); self.g_lf = self.sb("g_lf", [NH, TT])
        self.g_B = self.sb("g_B", [NH, TT]); self.g_m = self.sb("g_m", [NH, TT])
        self.g_nbk = self.sb("g_nbk", [NH, TT]); self.g_wa = self.sb("g_wa", [NH, TT])
        self.g_small = self.sb("g_small", [NH, 128])
        self.b_g = fw.buf("gates")
        self.Rm = self.sb("Rm", [NH, NH, NT_BLK + NS])
        self.decbc = self.sb("decbc", [P, NH, NT_BLK + NS]); self.b_dec = fw.buf()
        self.wsk = self.sb("wsk", [P, NTL, NH]); self.thrT = self.sb("thrT", [P, NTL, NH]); self.b_wt = fw.bufs(NTL, "wt")
        self.ss = self.sb("ss", [P, 16]); self.b_ss = fw.buf()
        self.memset(self.ss[:], 1.0, w=[self.b_ss])
        self.rstd = self.sb("rstd", [P, 16]); self.b_rstd = fw.buf()
        self.junk = self.sb("junk", [P, D], BF16); self.b_junk = fw.buf()
        self.xn = self.sb("xn", [P, 2, D]); self.b_xn = fw.bufs(2, "xn"); self.xn_ptr = 0
        self.xr = self.sb("xr", [P, 2, D]); self.b_xr = fw.bufs(2, "xr"); self.xr_ptr = 0
        self.tmpn = self.sb("tmpn", [P, NKC, P]); self.b_tmpn = fw.buf()
        self.gmv = self.sb("gmv", [P, 3, 512]); self.b_gmv = fw.bufs(3, "gmv"); self.gmv_ptr = 0
        self.gmvs = self.sb("gmvs", [P, 3, SS]); self.b_gmvs = fw.bufs(3, "gmvs"); self.gmvs_ptr = 0
        self.tg = self.sb("tg", [P, 2, 512]); self.b_tg = fw.bufs(2, "tg"); self.tg_ptr = 0
        self.t2 = self.sb("t2", [P, 512]); self.b_t2 = fw.buf()
        self.sm = self.sb("sm", [P, NT_BLK, P], BF16); self.b_sm = fw.bufs(NT_BLK, "sm"); self.sm_ptr = 0
        self.rr = self.sb("rr", [P, 4]); self.b_rr = fw.bufs(4, "rr"); self.rr_ptr = 0
        self.ssq = self.sb("ssq", [P, NTL]); self.b_ssq = fw.buf()
        self.memset(self.ssq[:], 1.0, w=[self.b_ssq])
        self.rsq = self.sb("rsq", [P, NTL]); self.b_rsq = fw.buf()
        self.small_out = self.sb("small_out", [P, 512]); self.b_so = fw.buf()
        self.adaT = self.sb("adaT", [P, 48, 17]); self.b_ada = fw.buf()
        self.gsc1 = self.sb("gsc1", [P, NKC, 17]); self.gsc2 = self.sb("gsc2", [P, NKC, 17])
        self.hgt1 = self.sb("hgt1", [P, NKC, 17]); self.b_der = fw.buf()
        self.sh1 = self.adaT[:, 0:8, :]
        self.sh2 = self.adaT[:, 24:32, :]
        self.gt2 = self.adaT[:, 40:48, :]

    def rot(self, name):
        ptr = getattr(self, name + "_ptr")
        setattr(self, name + "_ptr", ptr + 1)
        t = getattr(self, name)
        bl = getattr(self, "b_" + name)
        i = ptr % len(bl)
        return t, i, bl[i]

    def setup_ada(self):
        fw, d = self.fw, self.d
        call = self.aview(8192, D, F32, parts=17); b_c = fw.buf()
        self.load(call, d["call"], [b_c])
        th = self.aview(16384, D, F32, parts=17); b_th = fw.buf()
        self.act(th, call, AF.Tanh, r=[b_c], w=[b_th], scale=0.5)
        self.stt(th, th, 1.0, call, ALU.add, ALU.mult, r=[b_th, b_c], w=[b_th])
        pst, pbk = self.bank()
        for kc in range(NKC):
            self.tp(pst[:, kc * 17:(kc + 1) * 17], th[:, kc * P:(kc + 1) * P], self.identf[0:17, 0:17],
                    r=[b_th, self.b_idf], w=[pbk])
        self.cT = self.sb("cT", [P, NKC, 17], BF16); b_cT = fw.buf()
        self.act(self.cT[:].rearrange("p a b -> p (a b)"), pst[:, 0:NKC * 17], AF.Copy, r=[pbk], w=[b_cT], scale=0.5)
        for pnl in range(12):
            wv, wbuf = self.wpanel(d["w_ada"], pnl * 512, 512)
            for c4 in range(4):
                ch = pnl * 4 + c4
                pst, pbk = self.bank()
                for kc in range(NKC):
                    self.mm(pst[:, 0:17], wv[:, kc, c4 * P:(c4 + 1) * P], self.cT[:, kc, :], kc == 0, kc == NKC - 1,
                            r=wbuf + [b_cT], w=[pbk])
                self.act(self.adaT[:, ch, :], pst[:, 0:17], AF.Identity, r=[pbk, self.b_vecT], w=[self.b_ada],
                         bias=self.badaT[:, ch:ch + 1])
        A = self.adaT
        self.stt(self.gsc1[:], A[:, 8:16, :], 1.0, self.gmixT.unsqueeze(2).broadcast_to([P, NKC, 17]), ALU.add, ALU.mult,
                 r=[self.b_ada, self.b_vecT], w=[self.b_der])
        self.stt(self.gsc2[:], A[:, 32:40, :], 1.0, self.gffnT.unsqueeze(2).broadcast_to([P, NKC, 17]), ALU.add, ALU.mult,
                 r=[self.b_ada, self.b_vecT], w=[self.b_der])
        self.ts(self.hgt1[:], A[:, 16:24, :], 0.5, None, ALU.mult, r=[self.b_ada], w=[self.b_der])
        self.dump("adaT", self.adaT[:], [self.b_ada], [P, 48, 17])
        fw.barrier()

    def tile_list(self, has_s):
        tl = [(tt, P, tt * P) for tt in range(NT_BLK)]
        if has_s:
            tl.append((NT_BLK, SS, TB))
        return tl

    def segs(self, has_s):
        sg = [(0, TB, list(range(NT_BLK)))]
        if has_s:
            sg.append((TB, TB + SS, [NT_BLK]))
        return sg

    def x_src(self, b, tt, pc):
        if tt < NT_BLK:
            return self.d["xp"][(b * NT_BLK + tt) * P:(b * NT_BLK + tt + 1) * P, :]
        return self.d["xs"][SS * b:SS * (b + 1), :]

    def norm_gen(self, tl, gsc, sh, b, stream=False):
        fw = self.fw
        def fetch(tt, pc):
            if not stream:
                return self.xb[0:pc, tt, :], self.b_xb[tt]
            xr, ri, rb = self.rot("xr")
            self.load(xr[0:pc, ri, :], self.x_src(b, tt, pc), [rb])
            return xr[0:pc, ri, :], rb
        for (tt, pc, c0) in tl:
            src, sb_ = fetch(tt, pc)
            self.act(self.junk[0:pc, :], src, AF.Square, r=[sb_], w=[self.b_junk, self.b_ss],
                     accum_out=self.ss[0:pc, tt:tt + 1])
        n = len(tl)
        self.ts(self.rstd[:, 0:n], self.ss[:, 0:n], 1.0 / D, EPS, ALU.mult, ALU.add, r=[self.b_ss], w=[self.b_rstd])
        self.act(self.rstd[:, 0:n], self.rstd[:, 0:n], AF.Sqrt, r=[self.b_rstd], w=[self.b_rstd])
        self.recip(self.rstd[:, 0:n], self.rstd[:, 0:n], r=[self.b_rstd], w=[self.b_rstd])
        yield
        for i, (tt, pc, c0) in enumerate(tl):
            src, sb_ = fetch(tt, pc)
            xn, xi, xnb = self.rot("xn")
            self.act(xn[0:pc, xi, :], src, AF.Copy, r=[sb_, self.b_rstd], w=[xnb],
                     scale=self.rstd[0:pc, tt:tt + 1])
            pp, ppb = self.pair(i % 2)
            pv = pp.rearrange("p a b -> p (a b)").rearrange("p (k t) -> p k t", t=P)
            for kc in range(NKC):
                self.tp(pv[:, kc, 0:pc], xn[0:pc, xi, kc * P:(kc + 1) * P], self.identf[0:pc, 0:pc],
                        r=[xnb, self.b_idf], w=ppb)
            if pc == P:
                self.tt(self.tmpn[:], pv, gsc[:, :, 0:1].broadcast_to([P, NKC, P]), ALU.mult,
                        r=ppb + [self.b_der, self.b_ada], w=[self.b_tmpn])
                self.tt(self.hT[:, :, c0:c0 + P], self.tmpn[:], sh[:, :, 0:1].broadcast_to([P, NKC, P]), ALU.add,
                        r=[self.b_tmpn, self.b_ada], w=[self.b_hT[tt]])
            else:
                tv = self.tmpn[:, :, 0:SS].rearrange("p k (s t) -> p k s t", t=ST)
                self.tt(tv, pv[:, :, 0:SS].rearrange("p k (s t) -> p k s t", t=ST),
                        gsc[:, :, 1 + NS * b:1 + NS * (b + 1)].unsqueeze(3).broadcast_to([P, NKC, NS, ST]), ALU.mult,
                        r=ppb + [self.b_der, self.b_ada], w=[self.b_tmpn])
                self.tt(self.hT[:, :, c0:c0 + SS].rearrange("p k (s t) -> p k s t", t=ST), tv,
                        sh[:, :, 1 + NS * b:1 + NS * (b + 1)].unsqueeze(3).broadcast_to([P, NKC, NS, ST]), ALU.add,
                        r=[self.b_tmpn, self.b_ada], w=[self.b_hT[tt]])
            yield

    def spool_load(self, b):
        nrow = NS * 15
        self.load(self.t2[0:nrow, :], self.d["spool"][b * nrow:(b + 1) * nrow, :], [self.b_t2])

    def prep_gen(self, b):
        tl = self.tile_list(True)
        sg = self.segs(True)
        yield from self.norm_gen(tl, self.gsc1, self.sh1, b, stream=True)
        self.gates(b, True, sg, TB + SS)
        yield

    def hT_bufs(self, tiles):
        return [self.b_hT[t] for t in tiles]

    def block(self, b):
        fw, d, o = self.fw, self.d, self.o
        has_s = True
        self.cb = b
        self.gen_banks = (0, 1, 2, 3)
        tl = self.tile_list(has_s)
        sg = self.segs(has_s)
        t0 = b * NT_BLK
        ntl = len(tl)
        ncol = TB + (SS if has_s else 0)

        fw.barrier()
        self.aoff = 0
        self.gen_banks = (0, 1, 2, 3, 4, 5, 6, 7)
        self.pool_branch(b, has_s, sg, tl)

        fw.barrier()
        self.aoff = 0
        self.gen_banks = (0, 1, 2)
        self.heads(b, has_s, sg, tl)

        self.mixer_out(b, has_s, sg, tl)

        fw.barrier()
        self.aoff = 0
        self.ffn(b, has_s, sg, tl, t0)

    def gates(self, b, has_s, sg, ncol):
        fw, d, o = self.fw, self.d, self.o
        G = [self.b_g]
        gi, lf, gB, gm, nbk, wa = self.g_i, self.g_lf, self.g_B, self.g_m, self.g_nbk, self.g_wa
        for (c0, c1, tiles) in sg:
            w = c1 - c0
            pi, pib = self.bank()
            for kc in range(NKC):
                self.mm(pi[0:4, 0:w], self.wg[:, kc, 0:4], self.hT[:, kc, c0:c1], kc == 0, kc == NKC - 1,
                        r=[self.b_wg] + self.hT_bufs(tiles), w=[pib])
            pf, pfb = self.bank()
            for kc in range(NKC):
                self.mm(pf[0:4, 0:w], self.wg[:, kc, 4:8], self.hT[:, kc, c0:c1], kc == 0, kc == NKC - 1,
                        r=[self.b_wg] + self.hT_bufs(tiles), w=[pfb])
            self.act(gi[:, c0:c1], pi[0:4, 0:w], AF.Identity, r=[pib, self.b_bif], w=G, bias=self.bif[:, 0:1])
            self.act(lf[:, c0:c1], pf[0:4, 0:w], AF.Exp, r=[pfb, self.b_bif], w=G, bias=self.bif[:, 2:3], scale=-1.0)
        self.act(lf[:, 0:ncol], lf[:, 0:ncol], AF.Ln, r=G + [self.b_cols], w=G, bias=self.onec[0:4, :])
        self.ts(lf[:, 0:ncol], lf[:, 0:ncol], -1.0, None, ALU.mult, r=G, w=G)
        NTB = NT_BLK
        sml = self.g_small
        Kx = sml[:, 0:NTB + 1]
        dk = sml[:, 8:8 + NTB]
        dec = sml[:, 16:16 + NTB]
        Ks = sml[:, 32:32 + NS]
        dks = sml[:, 48:48 + NS]
        decs = sml[:, 64:64 + NS]
        cur = sml[:, 80:80 + NS]
        tmp = sml[:, 96:96 + NS]
        fw.op("dve", lambda e: e.tensor_tensor_scan(out=gB[:, 0:TB], data0=self.onesrow[:, 0:TB], data1=lf[:, 0:TB],
                                                    initial=0.0, op0=ALU.mult, op1=ALU.add), G + [self.b_const], G)
        fw.op("dve", lambda e: e.tensor_tensor_scan(out=gm[:, 0:TB], data0=lf[:, 0:TB], data1=gi[:, 0:TB],
                                                    initial=self.mprev[:, 0:1], op0=ALU.add, op1=ALU.max),
              G + [self.b_mprev], G)
        Bv = gB[:, 0:TB].rearrange("p (c t) -> p c t", t=P)
        mv = gm[:, 0:TB].rearrange("p (c t) -> p c t", t=P)
        self.tt(Kx[:, 1:NTB + 1], Bv[:, :, P - 1], mv[:, :, P - 1], ALU.subtract, r=G, w=G)
        self.ts(Kx[:, 0:1], self.mprev[:, 0:1], -1.0, None, ALU.mult, r=[self.b_mprev], w=G)
        self.tt(dk, Kx[:, 1:NTB + 1], Kx[:, 0:NTB], ALU.subtract, r=G, w=G)
        self.act(dec, dk, AF.Exp, r=G, w=G)
        self.tt(nbk[:, 0:TB].rearrange("p (c t) -> p c t", t=P), Kx[:, 1:NTB + 1].unsqueeze(2).broadcast_to([NH, NTB, P]),
                Bv, ALU.subtract, r=G, w=G)
        self.cp(self.mprev[:, 0:1], gm[:, TB - 1:TB], r=G, w=[self.b_mprev])
        nR = NTB
        if has_s:
            l3 = lf[:, TB:TB + SS].rearrange("p (s t) -> p s t", t=ST)
            i3 = gi[:, TB:TB + SS].rearrange("p (s t) -> p s t", t=ST)
            B3 = gB[:, TB:TB + SS].rearrange("p (s t) -> p s t", t=ST)
            m3 = gm[:, TB:TB + SS].rearrange("p (s t) -> p s t", t=ST)
            n3 = nbk[:, TB:TB + SS].rearrange("p (s t) -> p s t", t=ST)
            self.cp(B3[:, :, 0], l3[:, :, 0], r=G, w=G)
            for t in range(1, ST):
                self.tt(B3[:, :, t], B3[:, :, t - 1], l3[:, :, t], ALU.add, r=G, w=G)
            prev = self.m0s[:, NS * b:NS * (b + 1)]
            for t in range(ST):
                self.tt(tmp, l3[:, :, t], prev, ALU.add, r=G + [self.b_m0s], w=G)
                self.tt(m3[:, :, t], tmp, i3[:, :, t], ALU.max, r=G, w=G)
                prev = m3[:, :, t]
            self.tt(Ks, B3[:, :, ST - 1], m3[:, :, ST - 1], ALU.subtract, r=G, w=G)
            self.tt(dks, Ks, self.m0s[:, NS * b:NS * (b + 1)], ALU.add, r=G + [self.b_m0s], w=G)
            self.act(decs, dks, AF.Exp, r=G, w=G)
            self.tt(n3, Ks.unsqueeze(2).broadcast_to([NH, NS, ST]), B3, ALU.subtract, r=G, w=G)
            self.cp(self.small_out[0:4, 0:NS], m3[:, :, ST - 1], r=G, w=[self.b_so])
            self.store(o["nmsT"][:, NS * b:NS * (b + 1)], self.small_out[0:4, 0:NS], [self.b_so])
            nR = NTB + NS
        if b == NBLK - 1:
            self.cp(self.small_out[0:4, 32:33], gm[:, TB - 1:TB], r=G, w=[self.b_so])
            self.store(o["nmp"], self.small_out[0:4, 32:33], [self.b_so])
        self.tt(wa[:, 0:ncol], gi[:, 0:ncol], nbk[:, 0:ncol], ALU.add, r=G, w=G)
        self.act(nbk[:, 0:ncol], nbk[:, 0:ncol], AF.Exp, r=G, w=G)
        self.act(wa[:, 0:ncol], wa[:, 0:ncol], AF.Exp, r=G, w=G)
        idb = self.identf[0:4, 0:4].unsqueeze(2)
        self.tt(self.Rm[:, :, 0:NTB], idb.broadcast_to([NH, NH, NTB]), dec.unsqueeze(1).broadcast_to([NH, NH, NTB]),
                ALU.mult, r=G + [self.b_idf], w=G)
        if has_s:
            self.tt(self.Rm[:, :, NTB:NTB + NS], idb.broadcast_to([NH, NH, NS]), decs.unsqueeze(1).broadcast_to([NH, NH, NS]),
                    ALU.mult, r=G + [self.b_idf], w=G)
        self.gates_pending = (b, has_s)

    def gates_b(self):
        fw = self.fw
        b, has_s = self.gates_pending
        G = [self.b_g]
        wa, nbk = self.g_wa, self.g_nbk
        pr, prb = self.bank()
        nRt = NT_BLK + NS
        self.mm(pr[:, 0:NH * nRt], self.ones4[:, :], self.Rm[:].rearrange("p a b -> p (a b)"), True, True,
                r=G + [self.b_const], w=[prb])
        self.cp(self.decbc[:].rearrange("p a b -> p (a b)"), pr[:, 0:NH * nRt], r=[prb], w=[self.b_dec])
        for (tt, pc, c0) in self.tile_list(has_s):
            pt, ptb = self.bank()
            self.tp(pt[0:pc, 0:4], wa[:, c0:c0 + pc], self.identf[0:4, 0:4], r=G + [self.b_idf], w=[ptb])
            self.tp(pt[0:pc, 4:8], nbk[:, c0:c0 + pc], self.identf[0:4, 0:4], r=G + [self.b_idf], w=[ptb])
            self.ts(self.wsk[0:pc, tt, :], pt[0:pc, 0:4], 1.0 / 16.0, None, ALU.mult, r=[ptb], w=[self.b_wt[tt]])
            self.cp(self.thrT[0:pc, tt, :], pt[0:pc, 4:8], r=[ptb], w=[self.b_wt[tt]])
        if b == 0:
            self.dump("wsk0", self.wsk[:, 0:NT_BLK, :], self.b_wt[:NT_BLK], [P, NT_BLK, NH])
            self.dump("thr0", self.thrT[:, 0:NT_BLK, :], self.b_wt[:NT_BLK], [P, NT_BLK, NH])
            self.dump("dec0", self.decbc[:], [self.b_dec], [P, NH, NT_BLK + NS])
        if has_s:
            self.dump("wskL", self.wsk[:, :, :], self.b_wt, [P, NTL, NH])
            self.dump("thrL", self.thrT[:, :, :], self.b_wt, [P, NTL, NH])
            self.dump("decL", self.decbc[:], [self.b_dec], [P, NH, NT_BLK + NS])

    def pool_branch(self, b, has_s, sg, tl):
        fw, d, o = self.fw, self.d, self.o
        NU = 15 + TB
        uf = self.aview(self.carve(4 * NU * 4), 4 * NU, F32).rearrange("p (g n) -> p g n", g=4)
        b_uf = fw.bufs(4, "uf")
        sA = self.aview(self.carve(NU * 4), NU, F32); b_sA = fw.buf()
        sB = self.aview(self.carve(NU * 4), NU, F32); b_sB = fw.buf()
        pl = self.aview(self.carve(4 * TT * 2), 4 * TT, BF16).rearrange("p (g n) -> p g n", g=4)
        b_pl = [[fw.buf() for _ in sg] for g in range(4)]
        ypT = self.aview(self.carve(4 * TT * 2), 4 * TT, BF16).rearrange("p (g n) -> p g n", g=4)
        b_yp = [[fw.buf() for _ in sg] for g in range(4)]
        t15 = self.aview(self.carve(64), 15, F32); b_t15 = fw.buf()
        if has_s:
            fs = self.aview(self.carve(4 * NS * 19 * 4), 4 * NS * 19, F32).rearrange("p (g s n) -> p g s n", g=4, s=NS)
            b_fs = fw.bufs(4, "fs")
            s3A = self.aview(self.carve(NS * 19 * 4), NS * 19, F32).rearrange("p (s n) -> p s n", s=NS); b_s3A = fw.buf()
            s3B = self.aview(self.carve(NS * 19 * 4), NS * 19, F32).rearrange("p (s n) -> p s n", s=NS); b_s3B = fw.buf()
            us = self.aview(self.carve(4 * SS * 4), 4 * SS, F32).rearrange("p (g n) -> p g n", g=4); b_us = fw.bufs(4, "us")
            nrow = NS * 15
            spsb = self.t2[0:nrow, :]
            b_sp = self.b_t2
            ups = self.aview(self.carve(512 * 4), 512, F32, parts=SS); b_ups = fw.buf()
            b_np = fw.buf()
            fw.dma("sp", o["nps"][NS * b:NS * (b + 1), 0:11, :], d["spool"].rearrange("(s j) c -> s j c", j=15)[NS * b:NS * (b + 1), 4:15, :],
                   writes=[b_np], track=b_np, is_output=True)
        self.cp(uf[:, :, 0:15], self.uhist[:], r=[self.b_uh], w=b_uf)
        wu, wub = self.wpanel(d["w_in"], COL_U, 512)
        for g in range(4):
            for si, (c0, c1, tiles) in enumerate(sg):
                w = c1 - c0
                pu, pub = self.bank()
                for kc in range(NKC):
                    self.mm(pu[:, 0:w], wu[:, kc, g * P:(g + 1) * P], self.hT[:, kc, c0:c1], kc == 0, kc == NKC - 1,
                            r=wub + self.hT_bufs(tiles), w=[pub])
                if c0 < TB:
                    self.act(uf[:, g, 15 + c0:15 + c1], pu[:, 0:w], AF.Copy, r=[pub], w=[b_uf[g]])
                else:
                    self.act(fs[:, g, :, 15:19], pu[:, 0:SS].rearrange("p (s t) -> p s t", t=ST), AF.Copy, r=[pub], w=[b_fs[g]])
                    self.act(us[:, g, :], pu[:, 0:SS], AF.Copy, r=[pub], w=[b_us[g]])
        self.cp(self.uhist[:], uf[:, :, TB:TB + 15], r=b_uf, w=[self.b_uh])
        self.final_b()
        t0 = b * NT_BLK
        for tt in range(NT_BLK):
            self.load(self.xb[:, tt, :], d["xp"][(t0 + tt) * P:(t0 + tt + 1) * P, :], [self.b_xb[tt]])
        self.load(self.xb[0:SS, NT_BLK, :], d["xs"][SS * b:SS * (b + 1), :], [self.b_xb[NT_BLK]])
        for g in range(4):
            f = uf[:, g, :]
            lv = [(sA, b_sA), (sB, b_sB)]
            src, srcb = f, b_uf[g]
            sh = 1
            for lvl in range(g + 1):
                dst, dstb = lv[lvl % 2]
                lo = 2 * sh - 1
                self.tt(dst[:, lo:NU], src[:, lo:NU], src[:, lo - sh:NU - sh], ALU.add, r=[srcb], w=[dstb])
                src, srcb = dst, dstb
                sh *= 2
            wsz = float(POOL_W[g])
            self.stt(pl[:, g, 0:TB], src[:, 15:NU], 1.0 / wsz, f[:, 15:NU], ALU.mult, ALU.subtract,
                     r=[srcb, b_uf[g]], w=[b_pl[g][0]])
            if b == 0:
                self.tt(t15, src[:, 15:30], self.icnt[:, g, :], ALU.mult, r=[srcb, self.b_const], w=[b_t15])
                self.tt(pl[:, g, 0:15], t15, f[:, 15:30], ALU.subtract, r=[b_t15, b_uf[g]], w=[b_pl[g][0]])
        if has_s:
            for g in range(4):
                pt, ptb = self.bank()
                self.tp(pt[:, 0:nrow], spsb[:, g * P:(g + 1) * P], self.identf[0:nrow, 0:nrow], r=[b_sp, self.b_idf], w=[ptb])
                self.cp(fs[:, g, :, 0:15], pt[:, 0:nrow].rearrange("p (s n) -> p s n", n=15), r=[ptb], w=[b_fs[g]])
            for g in range(4):
                f3 = fs[:, g]
                lv = [(s3A, b_s3A), (s3B, b_s3B)]
                src, srcb = f3, b_fs[g]
                sh = 1
                for lvl in range(g + 1):
                    dst, dstb = lv[lvl % 2]
                    lo = 2 * sh - 1
                    self.tt(dst[:, :, lo:19], src[:, :, lo:19], src[:, :, lo - sh:19 - sh], ALU.add, r=[srcb], w=[dstb])
                    src, srcb = dst, dstb
                    sh *= 2
                wsz = float(POOL_W[g])
                self.stt(pl[:, g, TB:TB + SS].rearrange("p (s t) -> p s t", t=ST), src[:, :, 15:19], 1.0 / wsz, f3[:, :, 15:19],
                         ALU.mult, ALU.subtract, r=[srcb, b_fs[g]], w=[b_pl[g][1]])
            pt, ptb = self.bank()
            for g in range(4):
                self.tp(pt[0:SS, g * P:(g + 1) * P], us[:, g, :], self.identf[:, :], r=[b_us[g], self.b_idf], w=[ptb])
            self.cp(ups, pt[0:SS, :], r=[ptb], w=[b_ups])
            for t in range(ST):
                fw.dma("sp", o["nps"][NS * b:NS * (b + 1), 11 + t, :], ups[t:SS:ST, :], reads=[b_ups],
                       writes=[fw.buf()], track=b_ups, is_output=True)
        if b == NBLK - 1:
            pt, ptb = self.bank()
            for g in range(4):
                self.tp(pt[0:15, g * P:(g + 1) * P], uf[:, g, TB:TB + 15], self.identf[:, :], r=[b_uf[g], self.b_idf], w=[ptb])
            self.cp(self.small_out[0:15, :], pt[0:15, :], r=[ptb], w=[self.b_so])
            self.store(o["npp"], self.small_out[0:15, :], [self.b_so])
        if b == 0:
            self.dump("pl0", pl[:, :, 0:TB], [x[0] for x in b_pl], [P, 4, TB])
        for g in range(4):
            for si, (c0, c1, tiles) in enumerate(sg):
                w = c1 - c0
                pg, pgb = self.bank()
                self.mm(pg[:, 0:w], self.wpg[:, g, :], pl[:, g, c0:c1], True, True, r=[self.b_wpg, b_pl[g][si]], w=[pgb])
                self.act(ypT[:, g, c0:c1], pg[:, 0:w], AF.Copy, r=[pgb, self.b_vecT], w=[b_yp[g][si]], scale=self.pscT[:, g:g + 1])
        wpo, wpob = self.wload(d["w_po"].rearrange("(g p) n -> p g n", p=P), (4, D))
        self.b_mT = [[fw.buf() for _ in sg] for fc in range(NKC)]
        wga = None
        for fc in range(NKC):
            if fc % 4 == 0:
                wga, wgab = self.wpanel(d["w_in"], COL_GA + (fc // 4) * 512, 512)
            for si, (c0, c1, tiles) in enumerate(sg):
                w = c1 - c0
                pa, pab = self.bank()
                for g in range(4):
                    self.mm(pa[:, 0:w], wpo[:, g, fc * P:(fc + 1) * P], ypT[:, g, c0:c1], g == 0, g == 3,
                            r=wpob + [b_yp[g][si]], w=[pab])
                pg, pgb = self.bank()
                for kc in range(NKC):
                    self.mm(pg[:, 0:w], wga[:, kc, (fc % 4) * P:(fc % 4 + 1) * P], self.hT[:, kc, c0:c1], kc == 0, kc == NKC - 1,
                            r=wgab + self.hT_bufs(tiles), w=[pgb])
                tg, ti, tgb = self.rot("tg")
                self.act(tg[:, ti, 0:w], pg[:, 0:w], AF.Tanh, r=[pgb], w=[tgb], scale=0.5)
                self.stt(self.mT[:, fc, c0:c1], tg[:, ti, 0:w], 1.0, pa[:, 0:w], ALU.add, ALU.mult, r=[tgb, pab], w=[self.b_mT[fc][si]])
        self.gates_b()

    def heads(self, b, has_s, sg, tl):
        fw, d, o = self.fw, self.d, self.o
        ntl = len(tl)
        qT = self.aview(self.carve(2 * TT * 2), 2 * TT, BF16).rearrange("p (a n) -> p a n", a=2)
        kT = self.aview(self.carve(2 * TT * 2), 2 * TT, BF16).rearrange("p (a n) -> p a n", a=2)
        ktil = self.aview(self.carve(NTL * HD * 2), NTL * HD, BF16).rearrange("p (a n) -> p a n", a=NTL)
        vext = self.aview(self.carve(NTL * 258 * 2), NTL * 258, BF16).rearrange("p (a n) -> p a n", a=NTL)
        og = self.aview(self.carve(NTL * HD * 2), NTL * HD, BF16).rearrange("p (a n) -> p a n", a=NTL)
        hq = self.aview(self.carve(NTL * HD * 4), NTL * HD, F32).rearrange("p (a n) -> p a n", a=NTL)
        hmh = self.aview(self.carve(NTL * HD * 2), NTL * HD, BF16).rearrange("p (a n) -> p a n", a=NTL)
        to = self.aview(self.carve(HD * 4), HD, F32); b_to = fw.buf()
        self.hmT = self.aview(self.carve(NKC * TT * 2), NKC * TT, BF16).rearrange("p (a n) -> p a n", a=NKC)
        self.b_hmT = [fw.bufs(NTL, "hmT") for _ in range(NKC)]
        b_qT = [[fw.buf() for _ in sg] for _ in range(2)]
        b_kT = [fw.bufs(NTL, "kT") for _ in range(2)]
        b_kt = fw.bufs(NTL, "ktil"); b_v = fw.bufs(NTL, "vext"); b_og = fw.bufs(NTL, "og"); b_hq = fw.bufs(NTL, "hq")
        b_hmh = fw.bufs(NTL, "hmh")
        if has_s:
            NCB = 4
            Cin = self.aview(self.carve(NCB * 2 * 258 * 4), NCB * 2 * 258, F32).rearrange("p (i a n) -> p i a n", i=NCB, a=2)
            Cout = self.aview(self.carve(NCB * 2 * 258 * 4), NCB * 2 * 258, F32).rearrange("p (i a n) -> p i a n", i=NCB, a=2)
            Cbs = self.aview(self.carve(NCB * 2 * 258 * 2), NCB * 2 * 258, BF16).rearrange("p (i a n) -> p i a n", i=NCB, a=2)
            b_Cin = fw.bufs(NCB, "Cin"); b_Cout = fw.bufs(NCB, "Cout"); b_Cbs = fw.bufs(NCB, "Cbs")
            kz = self.aview(self.carve(NS * HD * 2), NS * HD, BF16, parts=SS).rearrange("p (s n) -> p s n", s=NS); b_kz = fw.buf()
            qz = self.aview(self.carve(2 * NS * SS * 2), 2 * NS * SS, BF16).rearrange("p (a s n) -> p a s n", a=2, s=NS); b_qz = fw.buf()
            nT = self.aview(self.carve(2 * NBH * 4), 2 * NBH, F32).rearrange("p (a n) -> p a n", a=2); b_nT = fw.buf()
            nTn = self.aview(self.carve(2 * NBH * 4), 2 * NBH, F32).rearrange("p (a n) -> p a n", a=2); b_nTn = fw.buf()
            sn_sb = self.aview(self.carve(HD * 4), HD, F32, parts=NBH); b_sn = fw.buf()
            sms = self.aview(self.carve(SS * 2), SS, BF16, parts=SS); b_sms = fw.buf()
            self.load(sn_sb, d["sn"][NBH * b:NBH * (b + 1), :], [b_sn])
            pt, ptb = self.bank()
            for dc in range(2):
                self.tp(pt[:, dc * NBH:(dc + 1) * NBH], sn_sb[:, dc * P:(dc + 1) * P], self.identf[0:NBH, 0:NBH], r=[b_sn, self.b_idf], w=[ptb])
            self.cp(nT[:].rearrange("p a n -> p (a n)"), pt[:, 0:2 * NBH], r=[ptb], w=[b_nT])
        self.memset(vext[:, :, 256:257], 1.0, w=b_v)
        ci = 0
        def head_norm(hd):
            self.gen_banks_save = self.gen_banks
            self.ts(self.rsq[:, 0:ntl], self.ssq[:, 0:ntl], 1.0 / HD, EPS, ALU.mult, ALU.add, r=[self.b_ssq], w=[self.b_rsq])
            self.act(self.rsq[:, 0:ntl], self.rsq[:, 0:ntl], AF.Sqrt, r=[self.b_rsq], w=[self.b_rsq])
            self.recip(self.rsq[:, 0:ntl], self.rsq[:, 0:ntl], r=[self.b_rsq], w=[self.b_rsq])
            for (tt, pc, c0) in tl:
                self.stt(hmh[0:pc, tt, :], hq[0:pc, tt, :], self.rsq[0:pc, tt:tt + 1], og[0:pc, tt, :], ALU.mult, ALU.mult,
                         r=[b_hq[tt], self.b_rsq, b_og[tt]], w=[b_hmh[tt]])
            for (tt, pc, c0) in tl:
                ptr, ptrb = self.bank()
                pv = ptr.bitcast(BF16)
                for dc in range(2):
                    self.tp(pv[:, dc * P:dc * P + pc], hmh[0:pc, tt, dc * P:(dc + 1) * P], self.identb[0:pc, 0:pc],
                            r=[b_hmh[tt], self.b_idb], w=[ptrb])
                for dc in range(2):
                    self.cp(self.hmT[:, 2 * hd + dc, c0:c0 + pc], pv[:, dc * P:dc * P + pc], r=[ptrb], w=[self.b_hmT[2 * hd + dc][tt]])
            if b == 0 and hd == 0 and False:
                self.dump("hq0", hq[:, 0:NT_BLK, :], b_hq[:NT_BLK], [P, NT_BLK, HD])
                self.dump("hmh0", hmh[:, 0:NT_BLK, :], b_hmh[:NT_BLK], [P, NT_BLK, HD])
        assert NS == NT_BLK
        for hd in range(NH):
            self.gen_banks = (0, 1, 2, 3, 4, 5, 6, 7)
            for s_ in range(NS):
                bhg = (NS * b + s_) * NH + hd
                self.load(Cin[:, s_, :, 0:HD], d["sC"][bhg].rearrange("(a p) e -> p a e", p=P), [b_Cin[s_]])
            wq, wqb = self.wpanel(d["w_in"], COL_Q + hd * HD, HD)
            for dc in range(2):
                for si, (c0, c1, tiles) in enumerate(sg):
                    w = c1 - c0
                    pq, pqb = self.bank()
                    for kc in range(NKC):
                        self.mm(pq[:, 0:w], wq[:, kc, dc * P:(dc + 1) * P], self.hT[:, kc, c0:c1], kc == 0, kc == NKC - 1,
                                r=wqb + self.hT_bufs(tiles), w=[pqb])
                    self.act(qT[:, dc, c0:c1], pq[:, 0:w], AF.Copy, r=[pqb], w=[b_qT[dc][si]])
            wk, wkb = self.wpanel(d["w_in"], COL_K + hd * HD, HD)
            def k_tr(tt, pc, c0):
                ptr, ptrb = self.bank()
                pv = ptr.bitcast(BF16)
                for dc in range(2):
                    self.tp(pv[:, dc * P:dc * P + pc], ktil[0:pc, tt, dc * P:(dc + 1) * P], self.identb[0:pc, 0:pc],
                            r=[b_kt[tt], self.b_idb], w=[ptrb])
                for dc in range(2):
                    self.cp(kT[:, dc, c0:c0 + pc], pv[:, dc * P:dc * P + pc], r=[ptrb], w=[b_kT[dc][tt]])
            prev_k = None
            for (tt, pc, c0) in tl:
                pk, pkb = self.bank()
                for kc in range(NKC):
                    self.mm(pk[0:pc, 0:HD], self.hT[:, kc, c0:c0 + pc], wk[:, kc, :], kc == 0, kc == NKC - 1,
                            r=wkb + [self.b_hT[tt]], w=[pkb])
                self.act(ktil[0:pc, tt, :], pk[0:pc, 0:HD], AF.Copy, r=[pkb, self.b_wt[tt]], w=[b_kt[tt]],
                         scale=self.wsk[0:pc, tt, hd:hd + 1])
                if prev_k is not None:
                    k_tr(*prev_k)
                prev_k = (tt, pc, c0)
            k_tr(*prev_k)
            wv_, wvb = self.wpanel(d["w_in"], COL_V + hd * HD, HD)
            for (tt, pc, c0) in tl:
                pk, pkb = self.bank()
                for kc in range(NKC):
                    self.mm(pk[0:pc, 0:HD], self.hT[:, kc, c0:c0 + pc], wv_[:, kc, :], kc == 0, kc == NKC - 1,
                            r=wvb + [self.b_hT[tt]], w=[pkb])
                self.cp(vext[0:pc, tt, 0:HD], pk[0:pc, 0:HD], r=[pkb], w=[b_v[tt]])
            if hd > 0:
                head_norm(hd - 1)
            wo, wob = self.wpanel(d["w_in"], COL_O + hd * HD, HD)
            for (tt, pc, c0) in tl:
                pk, pkb = self.bank()
                for kc in range(NKC):
                    self.mm(pk[0:pc, 0:HD], self.hT[:, kc, c0:c0 + pc], wo[:, kc, :], kc == 0, kc == NKC - 1,
                            r=wob + [self.b_hT[tt]], w=[pkb])
                self.act(to[0:pc, :], pk[0:pc, 0:HD], AF.Tanh, r=[pkb], w=[b_to], scale=0.5)
                self.stt(og[0:pc, tt, :], to[0:pc, :], 1.0, self.mnh[0:pc, hd * HD:(hd + 1) * HD], ALU.add, ALU.mult,
                         r=[b_to, self.b_mnh], w=[b_og[tt]])
            if b == 0 and hd == 0:
                self.dump("qT0", qT[:, :, 0:TB], [x[0] for x in b_qT], [P, 2, TB])
                self.dump("kT0", kT[:, :, 0:TB], b_kT[0][:NT_BLK] + b_kT[1][:NT_BLK], [P, 2, TB])
                self.dump("vx0", vext[:, 0:NT_BLK, :], b_v[:NT_BLK], [P, NT_BLK, 258])
            self.act(self.Csb[:, hd, :, 0:257], self.Cst[:, hd, :, :], AF.Copy, r=[self.b_C[hd], self.b_dec], w=[self.b_Csb[hd]],
                     scale=self.decbc[:, hd, 0:1])
            self.gen_banks = (7,)
            tts = NT_BLK
            cs0 = TB
            for s_ in range(NS):
                bh = s_ * NH + hd
                self.cp(Cin[:, s_, :, 256:257], nT[:, :, bh:bh + 1], r=[b_nT], w=[b_Cin[s_]])
                dcol = self.decbc[:, hd, NT_BLK + s_:NT_BLK + s_ + 1]
                self.act(Cbs[:, s_, :, 0:257], Cin[:, s_, :, 0:257], AF.Copy, r=[b_Cin[s_], self.b_dec], w=[b_Cbs[s_]], scale=dcol)
            pS, pSb = self.ps[:, 7, :], self.pb[7]
            for dc in range(2):
                self.mm(pS[0:SS, 0:SS], kT[:, dc, cs0:cs0 + SS], qT[:, dc, cs0:cs0 + SS], dc == 0, dc == 1,
                        r=[b_kT[dc][tts], b_qT[dc][1]], w=[pSb])
            self.tt(sms, pS[0:SS, 0:SS], self.masks[:], ALU.mult, r=[pSb, self.b_mk], w=[b_sms])
            for dc in range(2):
                self.tt(qz[:, dc], qT[:, dc, cs0:cs0 + SS].unsqueeze(1).broadcast_to([P, NS, SS]), self.cmb[:], ALU.mult,
                        r=[b_qT[dc][1], self.b_cm], w=[b_qz])
            self.tt(kz, ktil[0:SS, tts, :].unsqueeze(1).broadcast_to([SS, NS, HD]),
                    self.rm[:, :].unsqueeze(2).broadcast_to([SS, NS, HD]), ALU.mult, r=[b_kt[tts], self.b_const], w=[b_kz])
            pNs, pNsb = self.ps[:, 6, :], self.pb[6]
            self.mm(pNs[0:SS, 0:257], sms, vext[0:SS, tts, 0:257], True, False, r=[b_sms, b_v[tts]], w=[pNsb])
            pS2, pS2b = self.ps[:, 2, :], self.pb[2]
            for tt in range(NT_BLK):
                c0 = tt * P
                for dc in range(2):
                    self.mm(pS2[:, tt * P:(tt + 1) * P], kT[:, dc, c0:c0 + P], qT[:, dc, c0:c0 + P], dc == 0, dc == 1,
                            r=[b_kT[dc][tt], b_qT[dc][0]], w=[pS2b])
            sm = self.sm
            for tt in range(NT_BLK):
                self.tt(sm[:, tt, :], pS2[:, tt * P:(tt + 1) * P], self.maskp[:], ALU.mult, r=[pS2b, self.b_mk], w=[self.b_sm[tt]])
            for tt in range(NT_BLK):
                c0 = tt * P
                smi, smb = tt, self.b_sm[tt]
                pC, pCb = self.pair(0)
                for dc in range(2):
                    self.mm(pC[:, dc, 0:257], ktil[:, tt, dc * P:(dc + 1) * P], vext[:, tt, 0:257], True, True,
                            r=[b_kt[tt], b_v[tt]], w=[pCb[dc]])
                pN, pNb = self.ps[:, 3, :], self.pb[3]
                self.mm(pN[:, 0:257], sm[:, smi, :], vext[:, tt, 0:257], True, False, r=[smb, b_v[tt]], w=[pNb])
                for dc in range(2):
                    self.mm(pN[:, 0:257], qT[:, dc, c0:c0 + P], self.Csb[:, hd, dc, 0:257], False, dc == 1,
                            r=[b_qT[dc][0], self.b_Csb[hd]], w=[pNb])
                self.chunk_post(pN, pNb, P, tt, hd, hq, b_hq)
                self.stt(self.Cst[:, hd, :, :], self.Cst[:, hd, :, :], self.decbc[:, hd, tt:tt + 1], pC[:, :, 0:257], ALU.mult, ALU.add,
                         r=[self.b_C[hd], self.b_dec] + pCb, w=[self.b_C[hd]])
                if tt + 1 < NT_BLK:
                    self.act(self.Csb[:, hd, :, 0:257], self.Cst[:, hd, :, :], AF.Copy, r=[self.b_C[hd], self.b_dec],
                             w=[self.b_Csb[hd]], scale=self.decbc[:, hd, tt + 1:tt + 2])
                s_ = tt
                bh = s_ * NH + hd
                bhg = (NS * b + s_) * NH + hd
                dcol = self.decbc[:, hd, NT_BLK + s_:NT_BLK + s_ + 1]
                for dc in range(2):
                    self.mm(pNs[0:SS, 0:257], qz[:, dc, s_, :], Cbs[:, s_, dc, 0:257], False, (s_ == NS - 1 and dc == 1),
                            r=[b_qz, b_Cbs[s_]], w=[pNsb])
                pC2 = self.ps[:, 0:2, :]
                pC2b = [self.pb[0], self.pb[1]]
                for dc in range(2):
                    self.mm(pC2[:, dc, 0:257], kz[:, s_, dc * P:(dc + 1) * P], vext[0:SS, tts, 0:257], True, True,
                            r=[b_kz, b_v[tts]], w=[pC2b[dc]])
                self.stt(Cout[:, s_, :, 0:257], Cin[:, s_, :, 0:257], dcol, pC2[:, :, 0:257], ALU.mult, ALU.add,
                         r=[b_Cin[s_], self.b_dec] + pC2b, w=[b_Cout[s_]])
                fw.dma("sp", o["ncs"][bhg].rearrange("(a p) e -> p a e", p=P), Cout[:, s_, :, 0:HD], reads=[b_Cout[s_]],
                       writes=[fw.buf()], track=b_Cout[s_], is_output=True)
                self.cp(nTn[:, :, bh:bh + 1], Cout[:, s_, :, 256:257], r=[b_Cout[s_]], w=[b_nTn])
            self.chunk_post(pNs, pNsb, SS, tts, hd, hq, b_hq)
            self.gen_banks = (0, 1, 2)
            pass
        head_norm(NH - 1)
        if b == NBLK - 1:
            for hd in range(NH):
                fw.dma("sp", o["ncp"][hd].rearrange("(a p) e -> p a e", p=P), self.Cst[:, hd, :, 0:HD], reads=[self.b_C[hd]],
                       writes=[fw.buf()], track=self.b_C[hd], is_output=True)
            nP = self.small_out[:, 256:264]
            self.cp(nP.rearrange("p (h a) -> p h a", a=2), self.Cst[:, :, :, 256], r=self.b_C, w=[self.b_so])
            pt, ptb = self.bank()
            self.tp(pt[0:8, 0:P], nP, self.identf[:, :], r=[self.b_so, self.b_idf], w=[ptb])
            self.cp(self.small_out[0:8, 300:300 + P], pt[0:8, 0:P], r=[ptb], w=[self.b_so])
            self.store(o["nnp"], self.small_out[0:8, 300:300 + P], [self.b_so])
        if has_s:
            pt, ptb = self.bank()
            for dc in range(2):
                self.tp(pt[0:NBH, dc * P:(dc + 1) * P], nTn[:, dc, :], self.identf[:, :], r=[b_nTn, self.b_idf], w=[ptb])
            self.cp(sn_sb, pt[0:NBH, 0:HD], r=[ptb], w=[b_sn])
            self.store(o["nns"][NBH * b:NBH * (b + 1), :], sn_sb, [b_sn])

    def chunk_post(self, pN, pNb, pc, tt, hd, hq, b_hq):
        rr, ri, rrb = self.rot("rr")
        self.act(rr[0:pc, ri:ri + 1], pN[0:pc, 256:257], AF.Abs, r=[pNb], w=[rrb])
        self.ts(rr[0:pc, ri:ri + 1], rr[0:pc, ri:ri + 1], self.thrT[0:pc, tt, hd:hd + 1], None, ALU.max,
                r=[rrb, self.b_wt[tt]], w=[rrb])
        self.recip(rr[0:pc, ri:ri + 1], rr[0:pc, ri:ri + 1], r=[rrb], w=[rrb])
        self.act(hq[0:pc, tt, :], pN[0:pc, 0:HD], AF.Copy, r=[pNb, rrb], w=[b_hq[tt]], scale=rr[0:pc, ri:ri + 1])
        self.act(self.junk[0:pc, 0:HD], pN[0:pc, 0:HD], AF.Square, r=[pNb, rrb], w=[self.b_junk, self.b_ssq],
                 scale=rr[0:pc, ri:ri + 1], accum_out=self.ssq[0:pc, tt:tt + 1])

    def xupdate_a(self, pM, pMb, c0, c1, tiles, fc, gtT):
        w = c1 - c0
        if c0 < TB:
            gmv, gi, gb = self.rot("gmv")
            g2 = gmv[:, gi, :]
            self.act(g2[:, 0:w], pM[:, 0:w], AF.Copy, r=[pMb, self.b_ada, self.b_der], w=[gb], scale=gtT[:, fc, 0:1])
        else:
            gmv, gi, gb = self.rot("gmvs")
            g2 = gmv[:, gi, :]
            self.tt(g2[:, 0:SS].rearrange("p (s t) -> p s t", t=ST), pM[:, 0:SS].rearrange("p (s t) -> p s t", t=ST),
                    gtT[:, fc, 1 + NS * self.cb:1 + NS * (self.cb + 1)].unsqueeze(2).broadcast_to([P, NS, ST]), ALU.mult, r=[pMb, self.b_ada, self.b_der], w=[gb])
        return (g2, gb, c0, tiles, fc)

    def xupdate_b(self, st):
        g2, gb, c0, tiles, fc = st
        pT, pTb = self.bankN()
        pv = pT.rearrange("p (j f) -> p j f", f=P)
        for j, tt in enumerate(tiles):
            pc = P if tt < NT_BLK else SS
            self.tp(pv[0:pc, j, :], g2[:, j * P:j * P + pc], self.identf[:, :], r=[gb, self.b_idf], w=[pTb])
        if c0 < TB:
            n = len(tiles)
            xv = self.xb[:, tiles[0]:tiles[0] + n, fc * P:(fc + 1) * P]
            self.tt(xv, xv, pv[:, 0:n, :], ALU.add, r=[pTb] + [self.b_xb[t] for t in tiles], w=[self.b_xb[t] for t in tiles])
        else:
            xv = self.xb[0:SS, NT_BLK, fc * P:(fc + 1) * P]
            self.tt(xv, xv, pv[0:SS, 0, :], ALU.add, r=[pTb, self.b_xb[NT_BLK]], w=[self.b_xb[NT_BLK]])

    def mixer_out(self, b, has_s, sg, tl):
        fw, d, o = self.fw, self.d, self.o
        self.gen_banks = (0, 1, 2, 3, 4, 5, 6, 7)
        for fc in range(NKC):
            if fc % 4 == 0:
                wmo, wmob = self.wpanel(d["w_mo"], (fc // 4) * 512, 512)
                wgb_, wgbb = self.wpanel(d["w_in"], COL_GB + (fc // 4) * 512, 512)
            for si, (c0, c1, tiles) in enumerate(sg):
                w = c1 - c0
                pB, pBb = self.bank()
                for kc in range(NKC):
                    self.mm(pB[:, 0:w], wmo[:, kc, (fc % 4) * P:(fc % 4 + 1) * P], self.hmT[:, kc, c0:c1], kc == 0, kc == NKC - 1,
                            r=wmob + [self.b_hmT[kc][t] for t in tiles], w=[pBb])
                pg, pgb = self.bank()
                for kc in range(NKC):
                    self.mm(pg[:, 0:w], wgb_[:, kc, (fc % 4) * P:(fc % 4 + 1) * P], self.hT[:, kc, c0:c1], kc == 0, kc == NKC - 1,
                            r=wgbb + self.hT_bufs(tiles), w=[pgb])
                tg, ti, tgb = self.rot("tg")
                self.act(tg[:, ti, 0:w], pg[:, 0:w], AF.Tanh, r=[pgb], w=[tgb], scale=0.5)
                self.stt(self.t2[:, 0:w], tg[:, ti, 0:w], 1.0, pB[:, 0:w], ALU.add, ALU.mult, r=[tgb, pBb], w=[self.b_t2])
                self.tt(self.mT[:, fc, c0:c1], self.t2[:, 0:w], self.mT[:, fc, c0:c1], ALU.add, r=[self.b_t2, self.b_mT[fc][si]],
                        w=[self.b_mT[fc][si]])
        if b == 0:
            self.dump("mT0", self.mT[:, :, 0:TB], [x[0] for x in self.b_mT], [P, NKC, TB])
        self.gen_banks = (0, 1, 2, 4, 5, 7)
        pend = []
        for fc in range(NKC):
            if fc % 4 == 0:
                wou, woub = self.wpanel(d["w_out"], (fc // 4) * 512, 512)
            cur = []
            for si, (c0, c1, tiles) in enumerate(sg):
                w = c1 - c0
                pM, pMb = self.bank()
                for kc in range(NKC):
                    self.mm(pM[:, 0:w], wou[:, kc, (fc % 4) * P:(fc % 4 + 1) * P], self.mT[:, kc, c0:c1], kc == 0, kc == NKC - 1,
                            r=woub + [self.b_mT[kc][si]], w=[pMb])
                cur.append(self.xupdate_a(pM, pMb, c0, c1, tiles, fc, self.hgt1))
            for st_ in pend:
                self.xupdate_b(st_)
            pend = cur
        for st_ in pend:
            self.xupdate_b(st_)
        if b == 0:
            self.dump("x10", self.xb[:, 0:NT_BLK, :], self.b_xb[:NT_BLK], [P, NT_BLK, D])
        for _ in self.norm_gen(tl, self.gsc2, self.sh2, b):
            pass

    def ffn(self, b, has_s, sg, tl, t0):
        fw, d, o = self.fw, self.d, self.o
        hidT = self.aview(self.carve(32 * TT * 2), 32 * TT, BF16).rearrange("p (j n) -> p j n", j=32)
        b_hid = [[fw.buf() for _ in sg] for _ in range(32)]
        rl = self.aview(self.carve(2 * 512 * 4), 2 * 512, F32).rearrange("p (i n) -> p i n", i=2); b_rl = fw.bufs(2, "rl")
        rli = 0
        self.gen_banks = (0, 1, 2, 3, 4, 5, 6, 7)
        for j in range(32):
            if j % 4 == 0:
                w1p, w1b = self.wpanel(d["w1"], (j // 4) * 512, 512)
            for si, (c0, c1, tiles) in enumerate(sg):
                w = c1 - c0
                pH, pHb = self.bank()
                for kc in range(NKC):
                    self.mm(pH[:, 0:w], w1p[:, kc, (j % 4) * P:(j % 4 + 1) * P], self.hT[:, kc, c0:c1], kc == 0, kc == NKC - 1,
                            r=w1b + self.hT_bufs(tiles), w=[pHb])
                ri = rli % 2
                rli += 1
                self.act(rl[:, ri, 0:w], pH[:, 0:w], AF.Relu, r=[pHb], w=[b_rl[ri]])
                self.tt(hidT[:, j, c0:c1], pH[:, 0:w], rl[:, ri, 0:w], ALU.mult, r=[pHb, b_rl[ri]], w=[b_hid[j][si]])
        self.gen_banks = (0, 1, 2)
        pend = []
        nxt = self.prep_gen(b + 1) if b + 1 < NBLK else iter(())
        if b + 1 < NBLK:
            self.spool_load(b + 1)
        next(nxt, None)
        for fc in range(NKC):
            w2p, w2b = self.wload(d["w2"][:, fc * P:(fc + 1) * P].rearrange("(j p) n -> p j n", p=P), (32, P))
            cur = []
            for si, (c0, c1, tiles) in enumerate(sg):
                w = c1 - c0
                pO, pOb = self.bank()
                for j in range(32):
                    self.mm(pO[:, 0:w], w2p[:, j, :], hidT[:, j, c0:c1], j == 0, j == 31, r=w2b + [b_hid[j][si]], w=[pOb])
                cur.append(self.xupdate_a(pO, pOb, c0, c1, tiles, fc, self.gt2))
            for st_ in pend:
                self.xupdate_b(st_)
            pend = cur
            if fc == NKC - 1:
                for st_ in pend:
                    self.xupdate_b(st_)
                pend = []
            if fc != 0:
                next(nxt, None)
        for st_ in pend:
            self.xupdate_b(st_)
        for _ in nxt:
            pass
        for (tt, pc, c0) in tl:
            self.act(self.junk[0:pc, :], self.xb[0:pc, tt, :], AF.Square, r=[self.b_xb[tt]], w=[self.b_junk, self.b_ss],
                     accum_out=self.ss[0:pc, tt:tt + 1])
        n = len(tl)
        self.ts(self.rstd[:, 0:n], self.ss[:, 0:n], 1.0 / D, EPS, ALU.mult, ALU.add, r=[self.b_ss], w=[self.b_rstd])
        self.act(self.rstd[:, 0:n], self.rstd[:, 0:n], AF.Sqrt, r=[self.b_rstd], w=[self.b_rstd])
        self.recip(self.rstd[:, 0:n], self.rstd[:, 0:n], r=[self.b_rstd], w=[self.b_rstd])
        self.final_pending = (b, tl, t0)
        if b == NBLK - 1:
            self.final_b()

    def final_b(self):
        if self.final_pending is None:
            return
        fw, o = self.fw, self.o
        b, tl, t0 = self.final_pending
        self.final_pending = None
        for i, (tt, pc, c0) in enumerate(tl):
            self.stt(self.xb[0:pc, tt, :], self.xb[0:pc, tt, :], self.rstd[0:pc, tt:tt + 1], self.gfin[0:pc, :], ALU.mult, ALU.mult,
                     r=[self.b_xb[tt], self.b_rstd, self.b_gfin], w=[self.b_xb[tt]])
            if tt < NT_BLK:
                dst = o["yp"][(t0 + tt) * P:(t0 + tt + 1) * P, :]
            else:
                dst = o["ys"][SS * b:SS * (b + 1), :]
            fw.dma("sp", dst, self.xb[0:pc, tt, :], reads=[self.b_xb[tt]], writes=[fw.buf()], track=self.b_xb[tt], is_output=True)


_CACHE = {}


def _program(dbg=()):
    key = tuple(sorted(dbg))
    if key not in _CACHE:
        _CACHE[key] = MK(dbg)
    return _CACHE[key]


def make_in_maps(inputs):
    f = lambda a: np.ascontiguousarray(np.asarray(a, dtype=np.float32))
    I = {k: f(v) for k, v in inputs.items()}
    consts = _consts()
    vec_common = [I["g_mix"][0].reshape(8, P), I["g_ffn"][0].reshape(8, P), I["b_ada"][0].reshape(48, P),
                  I["pool_scale"][0].reshape(4, P)]
    vecs = np.concatenate(vec_common, axis=0)
    bif = np.stack([I["b_i"][0], I["b_f"][0]], axis=1)
    shared = dict(vecs=vecs, bif=bif, m_norm=I["m_norm"][0], g_final=I["g_final"], w_ada=I["w_ada"][0], w_in=I["w_in"][0],
                  w_pg=I["w_pool_group"][0], w_po=I["w_pool_out"][0], w_mo=I["w_m_out"][0], w_out=I["w_out"][0],
                  w1=I["w1"][0], w2=I["w2"][0], **consts)
    maps = []
    for c in range(8):
        s0, s1 = c * NS_CORE, (c + 1) * NS_CORE
        m = dict(shared)
        m["xp"] = I["x_prompt"][c]
        m["xs"] = I["x_sample"][s0:s1].reshape(SS_CORE, D)
        m["call"] = np.concatenate([I["c_prompt"][c:c + 1], I["c_sample"][s0:s1]], axis=0)
        m["spool"] = I["state_pool"][0, s0:s1].reshape(NS_CORE * 15, 512)
        m["sC"] = I["state_C"][0, s0:s1].reshape(NS_CORE * NH, HD, HD)
        m["sn"] = I["state_n"][0, s0:s1].reshape(NS_CORE * NH, HD)
        m["smT"] = np.ascontiguousarray(I["state_m"][0, s0:s1].T)
        maps.append(m)
    return maps


def kernel(**inputs):
    prog = _program()
    maps = make_in_maps(inputs)
    res = run_bass_kernel_spmd(prog.nc, maps, core_ids=list(range(8)))
    R = res.results
    y_prompt = np.stack([R[c]["yp"] for c in range(8)], axis=0)
    y_sample = np.concatenate([R[c]["ys"].reshape(NS_CORE, ST, D) for c in range(8)], axis=0)
    npp = np.stack([R[c]["npp"] for c in range(8)], axis=0)[None]
    ncp = np.stack([R[c]["ncp"] for c in range(8)], axis=0)[None]
    nnp = np.stack([R[c]["nnp"].reshape(NH, HD) for c in range(8)], axis=0)[None]
    nmp = np.stack([R[c]["nmp"].reshape(NH) for c in range(8)], axis=0)[None]
    nps = np.concatenate([R[c]["nps"] for c in range(8)], axis=0)[None]
    ncs = np.concatenate([R[c]["ncs"].reshape(NS_CORE, NH, HD, HD) for c in range(8)], axis=0)[None]
    nns = np.concatenate([R[c]["nns"].reshape(NS_CORE, NH, HD) for c in range(8)], axis=0)[None]
    nms = np.concatenate([np.ascontiguousarray(R[c]["nmsT"].T) for c in range(8)], axis=0)[None]
    outs = (y_prompt, y_sample, npp, ncp, nnp, nmp, nps, ncs, nns, nms)
    return tuple(np.ascontiguousarray(a, dtype=np.float32) for a in outs)
```

```python
import numpy as np
from contextlib import ExitStack
import concourse.bass as bass
import concourse.mybir as mybir
from concourse.bass_utils import run_bass_kernel_spmd

F32 = mybir.dt.float32
BF16 = mybir.dt.bfloat16
AF = mybir.ActivationFunctionType
ALU = mybir.AluOpType

ENGS = ("pe", "act", "dve", "pool", "sp")
ATTACH = True


class Buf:
    __slots__ = ("name", "writer", "readers", "dsem", "dcount")

    def __init__(self, name):
        self.name = name
        self.writer = None
        self.readers = []
        self.dsem = None
        self.dcount = 0


class Ins:
    __slots__ = ("fn", "waits", "needed", "dma", "rank", "attach")

    def __init__(self, fn, waits, dma=None, attach=False):
        self.fn = fn
        self.waits = waits
        self.needed = False
        self.dma = dma
        self.rank = None
        self.attach = attach


class FW:
    def __init__(self, nc, stack, same_engine_sync=True):
        self.nc = nc
        self.stack = stack
        self.streams = {e: [] for e in ENGS}
        self.seen = {e: {} for e in ENGS}
        self.same_engine_sync = same_engine_sync
        self.esem = {}
        for e in ("pe", "act", "dve", "pool"):
            self.esem[e] = stack.enter_context(nc.semaphore("es_" + e))
        self.dsems = []
        self.out_events = []
        self.n_bufs = 0
        self.pend = {}

    def buf(self, name=None):
        self.n_bufs += 1
        return Buf(name or f"b{self.n_bufs}")

    def bufs(self, n, name="b"):
        return [self.buf(f"{name}{i}") for i in range(n)]

    def _collect(self, eng, reads, writes):
        deps = {}

        def add(ev):
            if ev is None:
                return
            key = (ev[0], ev[1])
            if key not in deps or deps[key] < ev[2]:
                deps[key] = ev[2]

        for b in reads:
            add(b.writer)
        for b in writes:
            add(b.writer)
            for r in b.readers:
                add(r)
        waits = []
        seen = self.seen[eng]
        for key, idx in deps.items():
            if key[0] == "e" and key[1] == eng:
                if eng in ("pe", "sp") or not self.same_engine_sync:
                    continue
            if seen.get(key, -1) >= idx:
                continue
            seen[key] = idx
            waits.append((key[0], key[1], idx))
            if key[0] == "e":
                self.streams[key[1]][idx].needed = True
        return waits

    def op(self, eng, fn, reads=(), writes=(), attach=False):
        waits = self._collect(eng, reads, writes)
        idx = len(self.streams[eng])
        self.streams[eng].append(Ins(fn, waits, attach=attach and ATTACH))
        ev = ("e", eng, idx)
        for b in reads:
            b.readers.append(ev)
        for b in writes:
            b.writer = ev
            b.readers = []
        return ev

    def dma(self, q, out, in_, reads=(), writes=(), track=None, is_output=False, **kw):
        if track is None:
            track = writes[0] if writes else reads[0]
        if track.dsem is None:
            track.dsem = len(self.dsems)
            self.dsems.append(self.stack.enter_context(self.nc.semaphore(f"ds{track.dsem}")))
        waits = self._collect(q, reads, writes)
        track.dcount += 16
        ev = ("d", track.dsem, track.dcount)
        fn = lambda e, out=out, in_=in_, kw=kw: e.dma_start(out=out, in_=in_, **kw)
        self.streams[q].append(Ins(fn, waits, dma=(track.dsem, track.dcount)))
        for b in reads:
            b.readers.append(ev)
        for b in writes:
            b.writer = ev
            b.readers = []
        if is_output:
            self.out_events.append(ev)
        if q == "sp":
            self.pend[track.dsem] = track.dcount
        return ev

    def barrier(self, engines=("act", "dve", "sp")):
        last = {}
        for e in ("pe", "act", "dve"):
            s = self.streams[e]
            for i in range(len(s) - 1, -1, -1):
                if s[i].dma is None and s[i].fn is not None:
                    last[e] = i
                    break
        for e in engines:
            waits = []
            for src, idx in last.items():
                if src == e and (e == "pe" or not self.same_engine_sync):
                    continue
                key = ("e", src)
                if self.seen[e].get(key, -1) >= idx:
                    continue
                self.seen[e][key] = idx
                self.streams[src][idx].needed = True
                waits.append(("e", src, idx))
            if e != "sp":
                for sem, cnt in self.pend.items():
                    key = ("d", sem)
                    if self.seen[e].get(key, -1) >= cnt:
                        continue
                    self.seen[e][key] = cnt
                    waits.append(("d", sem, cnt))
            if waits:
                self.streams[e].append(Ins(None, waits))

    def finish(self):
        fin = {}
        for ev in self.out_events:
            key = (ev[0], ev[1])
            fin[key] = max(fin.get(key, 0), ev[2])
        self.streams["sp"].append(Ins(None, [(k[0], k[1], v) for k, v in fin.items()]))
        for e in ("pe", "act", "dve", "pool"):
            r = 0
            for ins in self.streams[e]:
                if ins.needed:
                    r += 1
                    ins.rank = r
        engmap = {"pe": "tensor", "act": "scalar", "dve": "vector", "pool": "gpsimd", "sp": "sync"}

        def replay(eobj, e):
            for ins in self.streams[e]:
                ws = ins.waits
                last = None
                if ins.attach and ins.fn is not None and ws:
                    last = ws[-1]
                    ws = ws[:-1]
                for w in ws:
                    if w[0] == "e":
                        eobj.wait_ge(self.esem[w[1]], self.streams[w[1]][w[2]].rank)
                    else:
                        eobj.wait_ge(self.dsems[w[1]], w[2])
                if ins.fn is None:
                    continue
                bi = ins.fn(eobj)
                if last is not None:
                    if last[0] == "e":
                        bi._wait_ge(self.esem[last[1]], self.streams[last[1]][last[2]].rank)
                    else:
                        bi._wait_ge(self.dsems[last[1]], last[2])
                if ins.dma is not None:
                    bi.then_inc(self.dsems[ins.dma[0]], 16)
                elif ins.needed:
                    bi.then_inc(self.esem[e], 1)

        with self.nc.Block() as block:
            for e in ENGS:
                if self.streams[e]:
                    getattr(block, engmap[e])(lambda eobj, e=e: replay(eobj, e))

    def stats(self):
        return {e: (len(s), sum(1 for i in s if i.needed), sum(len(i.waits) for i in s))
                for e, s in self.streams.items()}


P = 128
D = 1024
NKC = 8
SEQ = 2048
NS_CORE = 16
ST = 4
SS_CORE = NS_CORE * ST
NS = 4
SS = NS * ST
HD = 256
NH = 4
NBH = NS * NH
DFF = 4096
COL_U, COL_Q, COL_K, COL_V, COL_O, COL_I, COL_F, COL_GA, COL_GB = 0, 512, 1536, 2560, 3584, 4608, 4612, 4616, 5640
INW = 6664
EPS = 1e-6
POOL_W = (2, 4, 8, 16)

NT_BLK = 4
NBLK = SEQ // (NT_BLK * P)
TB = NT_BLK * P
TT = TB + SS
NTL = NT_BLK + 1
NWB = 4


def _consts():
    c = {}
    c["identf"] = np.eye(P, dtype=np.float32)
    s = np.arange(P)
    c["maskp"] = (s[:, None] <= s[None, :]).astype(np.float32)
    t = np.arange(SS)
    c["masks"] = ((t[:, None] // ST == t[None, :] // ST) & (t[:, None] <= t[None, :])).astype(np.float32)
    cm = (np.arange(NS)[:, None] == (t[None, :] // ST)).astype(np.float32)
    c["cm"] = np.broadcast_to(cm[None], (P, NS, SS)).reshape(P, NS * SS).copy()
    c["rm"] = (t[:, None] // ST == np.arange(NS)[None, :]).astype(np.float32)
    ic = np.zeros((P, 4, 15), np.float32)
    for g, w in enumerate(POOL_W):
        ic[:, g, :] = 1.0 / np.minimum(np.arange(15) + 1, w)
    c["icnt"] = ic.reshape(P, 60)
    c["ones4"] = np.ones((4, P), np.float32)
    c["onesrow"] = np.ones((4, TT), np.float32)
    return c


class MK:
    def __init__(self, dbg=()):
        self.dbg = set(dbg)
        self.nc = bass.Bass("TRN2", target_bir_lowering=False)
        self.st = ExitStack()
        self.fw = FW(self.nc, self.st)
        self.dbg_out = {}
        self.build()

    def din(self, name, shape):
        return self.nc.dram_tensor(name, list(shape), F32, kind="ExternalInput").ap()

    def dout(self, name, shape):
        return self.nc.dram_tensor(name, list(shape), F32, kind="ExternalOutput").ap()

    def sb(self, name, shape, dt=F32):
        return self.st.enter_context(self.nc.sbuf_tensor("s_" + name, list(shape), dt))

    def carve(self, nbytes):
        off = self.aoff
        self.aoff += (nbytes + 63) // 64 * 64
        assert self.aoff <= self.asize, (self.aoff, self.asize)
        return off

    def aview(self, off, n, dt, parts=P):
        if dt == BF16:
            return self.arena[0:parts, off // 2: off // 2 + n]
        return self.arena[0:parts, off // 2: off // 2 + 2 * n].bitcast(F32)

    def act(self, out, in_, func, r=(), w=(), **kw):
        return self.fw.op("act", lambda e: e.activation(out=out, in_=in_, func=func, **kw), r, w, attach=("accum_out" not in kw))

    def mm(self, out, lhsT, rhs, start, stop, r=(), w=()):
        return self.fw.op("pe", lambda e: e.matmul(out, lhsT=lhsT, rhs=rhs, start=start, stop=stop), r, w)

    def tp(self, out, in_, ident, r=(), w=()):
        return self.fw.op("pe", lambda e: e.transpose(out=out, in_=in_, identity=ident), r, w)

    def tt(self, out, in0, in1, op, r=(), w=(), eng="dve"):
        return self.fw.op(eng, lambda e: e.tensor_tensor(out=out, in0=in0, in1=in1, op=op), r, w, attach=True)

    def ts(self, out, in0, s1, s2, op0, op1=None, r=(), w=(), eng="dve"):
        if op1 is None:
            return self.fw.op(eng, lambda e: e.tensor_scalar(out=out, in0=in0, scalar1=s1, scalar2=None, op0=op0), r, w, attach=True)
        return self.fw.op(eng, lambda e: e.tensor_scalar(out=out, in0=in0, scalar1=s1, scalar2=s2, op0=op0, op1=op1), r, w, attach=True)

    def stt(self, out, in0, scalar, in1, op0, op1, r=(), w=()):
        return self.fw.op("dve", lambda e: e.scalar_tensor_tensor(out=out, in0=in0, scalar=scalar, in1=in1, op0=op0, op1=op1), r, w, attach=True)

    def cp(self, out, in_, r=(), w=(), eng="dve"):
        return self.fw.op(eng, lambda e: e.tensor_copy(out=out, in_=in_), r, w, attach=True)

    def recip(self, out, in_, r=(), w=()):
        return self.fw.op("dve", lambda e: e.reciprocal(out=out, in_=in_), r, w, attach=True)

    def memset(self, ap, val, w=(), eng="dve"):
        return self.fw.op(eng, lambda e: e.memset(ap, val), (), w)

    def load(self, out, in_, w, q="sp", **kw):
        return self.fw.dma(q, out, in_, writes=w, **kw)

    def store(self, out, in_, r, track=None):
        return self.fw.dma("sp", out, in_, reads=r, writes=[self.fw.buf("o")], track=track or r[0], is_output=True)

    def bank(self):
        i = self.gen_banks[self.gen_ptr % len(self.gen_banks)]
        self.gen_ptr += 1
        return self.ps[:, i, :], self.pb[i]

    def bankN(self):
        i = (3, 6)[self.n_ptr % 2]
        self.n_ptr += 1
        return self.ps[:, i, :], self.pb[i]

    def pair(self, which=0):
        i = (4, 6)[which]
        return self.ps[:, i:i + 2, :], [self.pb[i], self.pb[i + 1]]

    def wload(self, src, shape):
        a, b = shape
        n = a * b
        nh = 2 * NWB
        if n <= 2048:
            i = self.w_ptr % nh
            self.w_ptr += 1
            bufs = [self.wb[i]]
        else:
            if self.w_ptr % 2:
                self.w_ptr += 1
            i = self.w_ptr % nh
            self.w_ptr += 2
            bufs = [self.wb[i], self.wb[i + 1]]
        view = self.wring[:, i * 2048:i * 2048 + n].rearrange("p (a b) -> p a b", a=a)
        self.fw.dma("pool", view, src, writes=bufs, track=bufs[0])
        return view, bufs

    def wpanel(self, w_ap, c0, ncols):
        src = w_ap[:, c0:c0 + ncols].rearrange("(kc p) n -> p kc n", p=P)
        return self.wload(src, (NKC, ncols))

    def dump(self, name, ap, bufs, shape, parts=P):
        if name not in self.dbg:
            return
        o = self.nc.dram_tensor("dbg_" + name, [parts] + list(shape[1:]), ap.dtype, kind="ExternalOutput").ap()
        self.fw.dma("sp", o, ap, reads=bufs, writes=[self.fw.buf()], track=bufs[0], is_output=True)
        self.dbg_out[name] = "dbg_" + name

    def build(self):
        nc, fw = self.nc, self.fw
        d = self.d = {}
        for name, shape in [("xp", (SEQ, D)), ("xs", (SS_CORE, D)), ("call", (17, D)), ("spool", (NS_CORE * 15, 512)),
                            ("sC", (NS_CORE * NH, HD, HD)), ("sn", (NS_CORE * NH, HD)), ("smT", (NH, NS_CORE)),
                            ("vecs", (68, P)), ("bif", (NH, 2)), ("m_norm", (NH, HD)), ("g_final", (D,)),
                            ("w_ada", (D, 6 * D)), ("w_in", (D, INW)), ("w_pg", (4, P, P)), ("w_po", (512, D)),
                            ("w_mo", (D, D)), ("w_out", (D, D)), ("w1", (D, DFF)), ("w2", (DFF, D)),
                            ("identf", (P, P)), ("maskp", (P, P)), ("masks", (SS, SS)), ("cm", (P, NS * SS)),
                            ("rm", (SS, NS)), ("icnt", (P, 60)), ("ones4", (4, P)), ("onesrow", (4, TT))]:
            d[name] = self.din(name, shape)
        o = self.o = {}
        for name, shape in [("yp", (SEQ, D)), ("ys", (SS_CORE, D)), ("npp", (15, 512)), ("ncp", (NH, HD, HD)),
                            ("nnp", (8, P)), ("nmp", (NH, 1)), ("nps", (NS_CORE, 15, 512)), ("ncs", (NS_CORE * NH, HD, HD)),
                            ("nns", (NS_CORE * NH, HD)), ("nmsT", (NH, NS_CORE))]:
            o[name] = self.dout(name, shape)

        self.ps = self.st.enter_context(nc.psum_tensor("ps", [P, 8, 512], F32))
        self.pb = fw.bufs(8, "bank")
        self.gen_banks = (0, 1, 2)
        self.gen_ptr = 0
        self.n_ptr = 0

        self.xb = self.sb("xb", [P, NTL, D]); self.b_xb = fw.bufs(NTL, "xb")
        self.hT = self.sb("hT", [P, NKC, TT], BF16); self.b_hT = fw.bufs(NTL, "hT")
        self.mT = self.sb("mT", [P, NKC, TT], BF16)
        self.wring = self.sb("wring", [P, NWB * 4096], BF16); self.wb = fw.bufs(2 * NWB, "wr"); self.w_ptr = 0
        self.Cst = self.sb("Cst", [P, NH, 2, 257]); self.b_C = fw.bufs(NH, "C")
        self.Csb = self.sb("Csb", [P, NH, 2, 258], BF16); self.b_Csb = fw.bufs(NH, "Csb")
        self.asize = 56 * 1024
        self.arena = self.sb("arena", [P, self.asize // 2], BF16)

        self.final_pending = None
        self.setup_consts()
        self.gen_banks = (0, 1, 2, 3)
        self.spool_load(0)
        gen0 = self.prep_gen(0)
        next(gen0)
        self.setup_ada()
        self.gen_banks = (0, 1, 2, 3)
        for _ in gen0:
            pass
        for b in range(NBLK):
            self.block(b)
        fw.finish()

    def setup_consts(self):
        fw, d = self.fw, self.d
        self.identf = self.sb("identf", [P, P]); self.b_idf = fw.buf()
        self.load(self.identf[:], d["identf"], [self.b_idf])
        self.identb = self.sb("identb", [P, P], BF16); self.b_idb = fw.buf()
        self.cp(self.identb[:], self.identf[:], r=[self.b_idf], w=[self.b_idb])
        self.maskp = self.sb("maskp", [P, P]); self.b_mk = fw.buf()
        self.load(self.maskp[:], d["maskp"], [self.b_mk])
        self.masks = self.sb("masks", [SS, SS])
        self.load(self.masks[:], d["masks"], [self.b_mk])
        cmf = self.aview(0, NS * SS, F32)
        b_cmf = fw.buf()
        self.load(cmf, d["cm"], [b_cmf])
        self.cmb = self.sb("cmb", [P, NS, SS], BF16); self.b_cm = fw.buf()
        self.cp(self.cmb[:].rearrange("p a b -> p (a b)"), cmf, r=[b_cmf], w=[self.b_cm])
        self.rm = self.sb("rm", [SS, NS])
        self.load(self.rm[:], d["rm"], [self.b_cm])
        self.icnt = self.sb("icnt", [P, 4, 15])
        self.load(self.icnt[:].rearrange("p a b -> p (a b)"), d["icnt"], [self.b_cm])
        self.ones4 = self.sb("ones4", [4, P])
        self.load(self.ones4[:], d["ones4"], [self.b_cm])
        self.onesrow = self.sb("onesrow", [4, TT])
        self.load(self.onesrow[:], d["onesrow"], [self.b_cm])
        self.b_const = self.b_cm
        self.cols = self.sb("cols", [P, 4]); self.b_cols = fw.buf()
        self.memset(self.cols[:, 0:1], 1.0, w=[self.b_cols])
        self.memset(self.cols[:, 1:2], EPS, w=[self.b_cols])
        self.onec = self.cols[:, 0:1]
        vec_sb = self.sb("vec_sb", [68, P]); b_v = fw.buf()
        self.load(vec_sb[:], d["vecs"], [b_v])
        self.vecT = self.sb("vecT", [P, 68]); self.b_vecT = fw.buf()
        pst, pbk = self.bank()
        self.tp(pst[:, 0:68], vec_sb[:], self.identf[0:68, 0:68], r=[b_v, self.b_idf], w=[pbk])
        self.cp(self.vecT[:], pst[:, 0:68], r=[pbk], w=[self.b_vecT])
        self.gmixT = self.vecT[:, 0:8]
        self.gffnT = self.vecT[:, 8:16]
        self.badaT = self.vecT[:, 16:64]
        self.pscT = self.vecT[:, 64:68]
        self.bif = self.sb("bif", [NH, 4]); self.b_bif = fw.buf()
        self.load(self.bif[:, 0:2], d["bif"], [self.b_bif])
        self.ts(self.bif[:, 2:3], self.bif[:, 1:2], -1.0, None, ALU.mult, r=[self.b_bif], w=[self.b_bif])
        self.mnh = self.sb("mnh", [P, D]); self.b_mnh = fw.buf()
        self.load(self.mnh[:], d["m_norm"].rearrange("a b -> (a b)").partition_broadcast(P), [self.b_mnh])
        self.ts(self.mnh[:], self.mnh[:], 0.5, None, ALU.mult, r=[self.b_mnh], w=[self.b_mnh])
        self.gfin = self.sb("gfin", [P, D]); self.b_gfin = fw.buf()
        self.load(self.gfin[:], d["g_final"].partition_broadcast(P), [self.b_gfin])
        self.wg = self.sb("wg", [P, NKC, 8], BF16); self.b_wg = fw.buf()
        fw.dma("pool", self.wg[:], d["w_in"][:, COL_I:COL_I + 8].rearrange("(kc p) n -> p kc n", p=P), writes=[self.b_wg])
        self.wpg = self.sb("wpg", [P, 4, P], BF16); self.b_wpg = fw.buf()
        fw.dma("pool", self.wpg[:], d["w_pg"].rearrange("g c d -> c g d"), writes=[self.b_wpg])
        self.memset(self.Cst[:].rearrange("p a b c -> p (a b c)"), 0.0, w=self.b_C)
        self.memset(self.Csb[:].rearrange("p a b c -> p (a b c)"), 0.0, w=self.b_Csb)
        self.uhist = self.sb("uhist", [P, 4, 15]); self.b_uh = fw.buf()
        self.memset(self.uhist[:].rearrange("p a b -> p (a b)"), 0.0, w=[self.b_uh])
        self.mprev = self.sb("mprev", [NH, 2]); self.b_mprev = fw.buf()
        self.memset(self.mprev[:], 0.0, w=[self.b_mprev])
        self.m0s = self.sb("m0s", [NH, NS_CORE]); self.b_m0s = fw.buf()
        self.load(self.m0s[:], d["smT"], [self.b_m0s])
        self.g_i = self.sb("g_i", [NH, TT]); self.g_lf = self.sb("g_lf", [NH, TT])
        self.g_B = self.sb("g_B", [NH, TT]); self.g_m = self.sb("g_m", [NH, TT])
        self.g_nbk = self.sb("g_nbk", [NH, TT]); self.g_wa = self.sb("g_wa", [NH, TT])
        self.g_small = self.sb("g_small", [NH, 128])
        self.b_g = fw.buf("gates")
        self.Rm = self.sb("Rm", [NH, NH, NT_BLK + NS])
        self.decbc = self.sb("decbc", [P, NH, NT_BLK + NS]); self.b_dec = fw.buf()
        self.wsk = self.sb("wsk", [P, NTL, NH]); self.thrT = self.sb("thrT", [P, NTL, NH]); self.b_wt = fw.bufs(NTL, "wt")
        self.ss = self.sb("ss", [P, 16]); self.b_ss = fw.buf()
        self.memset(self.ss[:], 1.0, w=[self.b_ss])
        self.rstd = self.sb("rstd", [P, 16]); self.b_rstd = fw.buf()
        self.junk = self.sb("junk", [P, D], BF16); self.b_junk = fw.buf()
        self.xn = self.sb("xn", [P, 2, D]); self.b_xn = fw.bufs(2, "xn"); self.xn_ptr = 0
        self.xr = self.sb("xr", [P, 2, D]); self.b_xr = fw.bufs(2, "xr"); self.xr_ptr = 0
        self.tmpn = self.sb("tmpn", [P, NKC, P]); self.b_tmpn = fw.buf()
        self.gmv = self.sb("gmv", [P, 3, 512]); self.b_gmv = fw.bufs(3, "gmv"); self.gmv_ptr = 0
        self.gmvs = self.sb("gmvs", [P, 3, SS]); self.b_gmvs = fw.bufs(3, "gmvs"); self.gmvs_ptr = 0
        self.tg = self.sb("tg", [P, 2, 512]); self.b_tg = fw.bufs(2, "tg"); self.tg_ptr = 0
        self.t2 = self.sb("t2", [P, 512]); self.b_t2 = fw.buf()
        self.sm = self.sb("sm", [P, NT_BLK, P], BF16); self.b_sm = fw.bufs(NT_BLK, "sm"); self.sm_ptr = 0
        self.rr = self.sb("rr", [P, 4]); self.b_rr = fw.bufs(4, "rr"); self.rr_ptr = 0
        self.ssq = self.sb("ssq", [P, NTL]); self.b_ssq = fw.buf()
        self.memset(self.ssq[:], 1.0, w=[self.b_ssq])
        self.rsq = self.sb("rsq", [P, NTL]); self.b_rsq = fw.buf()
        self.small_out = self.sb("small_out", [P, 512]); self.b_so = fw.buf()
        self.adaT = self.sb("adaT", [P, 48, 17]); self.b_ada = fw.buf()
        self.gsc1 = self.sb("gsc1", [P, NKC, 17]); self.gsc2 = self.sb("gsc2", [P, NKC, 17])
        self.hgt1 = self.sb("hgt1", [P, NKC, 17]); self.b_der = fw.buf()
        self.sh1 = self.adaT[:, 0:8, :]
        self.sh2 = self.adaT[:, 24:32, :]
        self.gt2 = self.adaT[:, 40:48, :]

    def rot(self, name):
        ptr = getattr(self, name + "_ptr")
        setattr(self, name + "_ptr", ptr + 1)
        t = getattr(self, name)
        bl = getattr(self, "b_" + name)
        i = ptr % len(bl)
        return t, i, bl[i]

    def setup_ada(self):
        fw, d = self.fw, self.d
        call = self.aview(8192, D, F32, parts=17); b_c = fw.buf()
        self.load(call, d["call"], [b_c])
        th = self.aview(16384, D, F32, parts=17); b_th = fw.buf()
        self.act(th, call, AF.Tanh, r=[b_c], w=[b_th], scale=0.5)
        self.stt(th, th, 1.0, call, ALU.add, ALU.mult, r=[b_th, b_c], w=[b_th])
        pst, pbk = self.bank()
        for kc in range(NKC):
            self.tp(pst[:, kc * 17:(kc + 1) * 17], th[:, kc * P:(kc + 1) * P], self.identf[0:17, 0:17],
                    r=[b_th, self.b_idf], w=[pbk])
        self.cT = self.sb("cT", [P, NKC, 17], BF16); b_cT = fw.buf()
        self.act(self.cT[:].rearrange("p a b -> p (a b)"), pst[:, 0:NKC * 17], AF.Copy, r=[pbk], w=[b_cT], scale=0.5)
        for pnl in range(12):
            wv, wbuf = self.wpanel(d["w_ada"], pnl * 512, 512)
            for c4 in range(4):
                ch = pnl * 4 + c4
                pst, pbk = self.bank()
                for kc in range(NKC):
                    self.mm(pst[:, 0:17], wv[:, kc, c4 * P:(c4 + 1) * P], self.cT[:, kc, :], kc == 0, kc == NKC - 1,
                            r=wbuf + [b_cT], w=[pbk])
                self.act(self.adaT[:, ch, :], pst[:, 0:17], AF.Identity, r=[pbk, self.b_vecT], w=[self.b_ada],
                         bias=self.badaT[:, ch:ch + 1])
        A = self.adaT
        self.stt(self.gsc1[:], A[:, 8:16, :], 1.0, self.gmixT.unsqueeze(2).broadcast_to([P, NKC, 17]), ALU.add, ALU.mult,
                 r=[self.b_ada, self.b_vecT], w=[self.b_der])
        self.stt(self.gsc2[:], A[:, 32:40, :], 1.0, self.gffnT.unsqueeze(2).broadcast_to([P, NKC, 17]), ALU.add, ALU.mult,
                 r=[self.b_ada, self.b_vecT], w=[self.b_der])
        self.ts(self.hgt1[:], A[:, 16:24, :], 0.5, None, ALU.mult, r=[self.b_ada], w=[self.b_der])
        self.dump("adaT", self.adaT[:], [self.b_ada], [P, 48, 17])
        fw.barrier()

    def tile_list(self, has_s):
        tl = [(tt, P, tt * P) for tt in range(NT_BLK)]
        if has_s:
            tl.append((NT_BLK, SS, TB))
        return tl

    def segs(self, has_s):
        sg = [(0, TB, list(range(NT_BLK)))]
        if has_s:
            sg.append((TB, TB + SS, [NT_BLK]))
        return sg

    def x_src(self, b, tt, pc):
        if tt < NT_BLK:
            return self.d["xp"][(b * NT_BLK + tt) * P:(b * NT_BLK + tt + 1) * P, :]
        return self.d["xs"][SS * b:SS * (b + 1), :]

    def norm_gen(self, tl, gsc, sh, b, stream=False):
        fw = self.fw
        def fetch(tt, pc):
            if not stream:
                return self.xb[0:pc, tt, :], self.b_xb[tt]
            xr, ri, rb = self.rot("xr")
            self.load(xr[0:pc, ri, :], self.x_src(b, tt, pc), [rb])
            return xr[0:pc, ri, :], rb
        for (tt, pc, c0) in tl:
            src, sb_ = fetch(tt, pc)
            self.act(self.junk[0:pc, :], src, AF.Square, r=[sb_], w=[self.b_junk, self.b_ss],
                     accum_out=self.ss[0:pc, tt:tt + 1])
        n = len(tl)
        self.ts(self.rstd[:, 0:n], self.ss[:, 0:n], 1.0 / D, EPS, ALU.mult, ALU.add, r=[self.b_ss], w=[self.b_rstd])
        self.act(self.rstd[:, 0:n], self.rstd[:, 0:n], AF.Sqrt, r=[self.b_rstd], w=[self.b_rstd])
        self.recip(self.rstd[:, 0:n], self.rstd[:, 0:n], r=[self.b_rstd], w=[self.b_rstd])
        yield
        for i, (tt, pc, c0) in enumerate(tl):
            src, sb_ = fetch(tt, pc)
            xn, xi, xnb = self.rot("xn")
            self.act(xn[0:pc, xi, :], src, AF.Copy, r=[sb_, self.b_rstd], w=[xnb],
                     scale=self.rstd[0:pc, tt:tt + 1])
            pp, ppb = self.pair(i % 2)
            pv = pp.rearrange("p a b -> p (a b)").rearrange("p (k t) -> p k t", t=P)
            for kc in range(NKC):
                self.tp(pv[:, kc, 0:pc], xn[0:pc, xi, kc * P:(kc + 1) * P], self.identf[0:pc, 0:pc],
                        r=[xnb, self.b_idf], w=ppb)
            if pc == P:
                self.tt(self.tmpn[:], pv, gsc[:, :, 0:1].broadcast_to([P, NKC, P]), ALU.mult,
                        r=ppb + [self.b_der, self.b_ada], w=[self.b_tmpn])
                self.tt(self.hT[:, :, c0:c0 + P], self.tmpn[:], sh[:, :, 0:1].broadcast_to([P, NKC, P]), ALU.add,
                        r=[self.b_tmpn, self.b_ada], w=[self.b_hT[tt]])
            else:
                tv = self.tmpn[:, :, 0:SS].rearrange("p k (s t) -> p k s t", t=ST)
                self.tt(tv, pv[:, :, 0:SS].rearrange("p k (s t) -> p k s t", t=ST),
                        gsc[:, :, 1 + NS * b:1 + NS * (b + 1)].unsqueeze(3).broadcast_to([P, NKC, NS, ST]), ALU.mult,
                        r=ppb + [self.b_der, self.b_ada], w=[self.b_tmpn])
                self.tt(self.hT[:, :, c0:c0 + SS].rearrange("p k (s t) -> p k s t", t=ST), tv,
                        sh[:, :, 1 + NS * b:1 + NS * (b + 1)].unsqueeze(3).broadcast_to([P, NKC, NS, ST]), ALU.add,
                        r=[self.b_tmpn, self.b_ada], w=[self.b_hT[tt]])
            yield

    def spool_load(self, b):
        nrow = NS * 15
        self.load(self.t2[0:nrow, :], self.d["spool"][b * nrow:(b + 1) * nrow, :], [self.b_t2])

    def prep_gen(self, b):
        tl = self.tile_list(True)
        sg = self.segs(True)
        yield from self.norm_gen(tl, self.gsc1, self.sh1, b, stream=True)
        self.gates(b, True, sg, TB + SS)
        yield

    def hT_bufs(self, tiles):
        return [self.b_hT[t] for t in tiles]

    def block(self, b):
        fw, d, o = self.fw, self.d, self.o
        has_s = True
        self.cb = b
        self.gen_banks = (0, 1, 2, 3)
        tl = self.tile_list(has_s)
        sg = self.segs(has_s)
        t0 = b * NT_BLK
        ntl = len(tl)
        ncol = TB + (SS if has_s else 0)

        fw.barrier()
        self.aoff = 0
        self.gen_banks = (0, 1, 2, 3, 4, 5, 6, 7)
        self.pool_branch(b, has_s, sg, tl)

        fw.barrier()
        self.aoff = 0
        self.gen_banks = (0, 1, 2)
        self.heads(b, has_s, sg, tl)

        self.mixer_out(b, has_s, sg, tl)

        fw.barrier()
        self.aoff = 0
        self.ffn(b, has_s, sg, tl, t0)

    def gates(self, b, has_s, sg, ncol):
        fw, d, o = self.fw, self.d, self.o
        G = [self.b_g]
        gi, lf, gB, gm, nbk, wa = self.g_i, self.g_lf, self.g_B, self.g_m, self.g_nbk, self.g_wa
        for (c0, c1, tiles) in sg:
            w = c1 - c0
            pi, pib = self.bank()
            for kc in range(NKC):
                self.mm(pi[0:4, 0:w], self.wg[:, kc, 0:4], self.hT[:, kc, c0:c1], kc == 0, kc == NKC - 1,
                        r=[self.b_wg] + self.hT_bufs(tiles), w=[pib])
            pf, pfb = self.bank()
            for kc in range(NKC):
                self.mm(pf[0:4, 0:w], self.wg[:, kc, 4:8], self.hT[:, kc, c0:c1], kc == 0, kc == NKC - 1,
                        r=[self.b_wg] + self.hT_bufs(tiles), w=[pfb])
            self.act(gi[:, c0:c1], pi[0:4, 0:w], AF.Identity, r=[pib, self.b_bif], w=G, bias=self.bif[:, 0:1])
            self.act(lf[:, c0:c1], pf[0:4, 0:w], AF.Exp, r=[pfb, self.b_bif], w=G, bias=self.bif[:, 2:3], scale=-1.0)
        self.act(lf[:, 0:ncol], lf[:, 0:ncol], AF.Ln, r=G + [self.b_cols], w=G, bias=self.onec[0:4, :])
        self.ts(lf[:, 0:ncol], lf[:, 0:ncol], -1.0, None, ALU.mult, r=G, w=G)
        NTB = NT_BLK
        sml = self.g_small
        Kx = sml[:, 0:NTB + 1]
        dk = sml[:, 8:8 + NTB]
        dec = sml[:, 16:16 + NTB]
        Ks = sml[:, 32:32 + NS]
        dks = sml[:, 48:48 + NS]
        decs = sml[:, 64:64 + NS]
        cur = sml[:, 80:80 + NS]
        tmp = sml[:, 96:96 + NS]
        fw.op("dve", lambda e: e.tensor_tensor_scan(out=gB[:, 0:TB], data0=self.onesrow[:, 0:TB], data1=lf[:, 0:TB],
                                                    initial=0.0, op0=ALU.mult, op1=ALU.add), G + [self.b_const], G)
        fw.op("dve", lambda e: e.tensor_tensor_scan(out=gm[:, 0:TB], data0=lf[:, 0:TB], data1=gi[:, 0:TB],
                                                    initial=self.mprev[:, 0:1], op0=ALU.add, op1=ALU.max),
              G + [self.b_mprev], G)
        Bv = gB[:, 0:TB].rearrange("p (c t) -> p c t", t=P)
        mv = gm[:, 0:TB].rearrange("p (c t) -> p c t", t=P)
        self.tt(Kx[:, 1:NTB + 1], Bv[:, :, P - 1], mv[:, :, P - 1], ALU.subtract, r=G, w=G)
        self.ts(Kx[:, 0:1], self.mprev[:, 0:1], -1.0, None, ALU.mult, r=[self.b_mprev], w=G)
        self.tt(dk, Kx[:, 1:NTB + 1], Kx[:, 0:NTB], ALU.subtract, r=G, w=G)
        self.act(dec, dk, AF.Exp, r=G, w=G)
        self.tt(nbk[:, 0:TB].rearrange("p (c t) -> p c t", t=P), Kx[:, 1:NTB + 1].unsqueeze(2).broadcast_to([NH, NTB, P]),
                Bv, ALU.subtract, r=G, w=G)
        self.cp(self.mprev[:, 0:1], gm[:, TB - 1:TB], r=G, w=[self.b_mprev])
        nR = NTB
        if has_s:
            l3 = lf[:, TB:TB + SS].rearrange("p (s t) -> p s t", t=ST)
            i3 = gi[:, TB:TB + SS].rearrange("p (s t) -> p s t", t=ST)
            B3 = gB[:, TB:TB + SS].rearrange("p (s t) -> p s t", t=ST)
            m3 = gm[:, TB:TB + SS].rearrange("p (s t) -> p s t", t=ST)
            n3 = nbk[:, TB:TB + SS].rearrange("p (s t) -> p s t", t=ST)
            self.cp(B3[:, :, 0], l3[:, :, 0], r=G, w=G)
            for t in range(1, ST):
                self.tt(B3[:, :, t], B3[:, :, t - 1], l3[:, :, t], ALU.add, r=G, w=G)
            prev = self.m0s[:, NS * b:NS * (b + 1)]
            for t in range(ST):
                self.tt(tmp, l3[:, :, t], prev, ALU.add, r=G + [self.b_m0s], w=G)
                self.tt(m3[:, :, t], tmp, i3[:, :, t], ALU.max, r=G, w=G)
                prev = m3[:, :, t]
            self.tt(Ks, B3[:, :, ST - 1], m3[:, :, ST - 1], ALU.subtract, r=G, w=G)
            self.tt(dks, Ks, self.m0s[:, NS * b:NS * (b + 1)], ALU.add, r=G + [self.b_m0s], w=G)
            self.act(decs, dks, AF.Exp, r=G, w=G)
            self.tt(n3, Ks.unsqueeze(2).broadcast_to([NH, NS, ST]), B3, ALU.subtract, r=G, w=G)
            self.cp(self.small_out[0:4, 0:NS], m3[:, :, ST - 1], r=G, w=[self.b_so])
            self.store(o["nmsT"][:, NS * b:NS * (b + 1)], self.small_out[0:4, 0:NS], [self.b_so])
            nR = NTB + NS
        if b == NBLK - 1:
            self.cp(self.small_out[0:4, 32:33], gm[:, TB - 1:TB], r=G, w=[self.b_so])
            self.store(o["nmp"], self.small_out[0:4, 32:33], [self.b_so])
        self.tt(wa[:, 0:ncol], gi[:, 0:ncol], nbk[:, 0:ncol], ALU.add, r=G, w=G)
        self.act(nbk[:, 0:ncol], nbk[:, 0:ncol], AF.Exp, r=G, w=G)
        self.act(wa[:, 0:ncol], wa[:, 0:ncol], AF.Exp, r=G, w=G)
        idb = self.identf[0:4, 0:4].unsqueeze(2)
        self.tt(self.Rm[:, :, 0:NTB], idb.broadcast_to([NH, NH, NTB]), dec.unsqueeze(1).broadcast_to([NH, NH, NTB]),
                ALU.mult, r=G + [self.b_idf], w=G)
        if has_s:
            self.tt(self.Rm[:, :, NTB:NTB + NS], idb.broadcast_to([NH, NH, NS]), decs.unsqueeze(1).broadcast_to([NH, NH, NS]),
                    ALU.mult, r=G + [self.b_idf], w=G)
        self.gates_pending = (b, has_s)

    def gates_b(self):
        fw = self.fw
        b, has_s = self.gates_pending
        G = [self.b_g]
        wa, nbk = self.g_wa, self.g_nbk
        pr, prb = self.bank()
        nRt = NT_BLK + NS
        self.mm(pr[:, 0:NH * nRt], self.ones4[:, :], self.Rm[:].rearrange("p a b -> p (a b)"), True, True,
                r=G + [self.b_const], w=[prb])
        self.cp(self.decbc[:].rearrange("p a b -> p (a b)"), pr[:, 0:NH * nRt], r=[prb], w=[self.b_dec])
        for (tt, pc, c0) in self.tile_list(has_s):
            pt, ptb = self.bank()
            self.tp(pt[0:pc, 0:4], wa[:, c0:c0 + pc], self.identf[0:4, 0:4], r=G + [self.b_idf], w=[ptb])
            self.tp(pt[0:pc, 4:8], nbk[:, c0:c0 + pc], self.identf[0:4, 0:4], r=G + [self.b_idf], w=[ptb])
            self.ts(self.wsk[0:pc, tt, :], pt[0:pc, 0:4], 1.0 / 16.0, None, ALU.mult, r=[ptb], w=[self.b_wt[tt]])
            self.cp(self.thrT[0:pc, tt, :], pt[0:pc, 4:8], r=[ptb], w=[self.b_wt[tt]])
        if b == 0:
            self.dump("wsk0", self.wsk[:, 0:NT_BLK, :], self.b_wt[:NT_BLK], [P, NT_BLK, NH])
            self.dump("thr0", self.thrT[:, 0:NT_BLK, :], self.b_wt[:NT_BLK], [P, NT_BLK, NH])
            self.dump("dec0", self.decbc[:], [self.b_dec], [P, NH, NT_BLK + NS])
        if has_s:
            self.dump("wskL", self.wsk[:, :, :], self.b_wt, [P, NTL, NH])
            self.dump("thrL", self.thrT[:, :, :], self.b_wt, [P, NTL, NH])
            self.dump("decL", self.decbc[:], [self.b_dec], [P, NH, NT_BLK + NS])

    def pool_branch(self, b, has_s, sg, tl):
        fw, d, o = self.fw, self.d, self.o
        NU = 15 + TB
        uf = self.aview(self.carve(4 * NU * 4), 4 * NU, F32).rearrange("p (g n) -> p g n", g=4)
        b_uf = fw.bufs(4, "uf")
        sA = self.aview(self.carve(NU * 4), NU, F32); b_sA = fw.buf()
        sB = self.aview(self.carve(NU * 4), NU, F32); b_sB = fw.buf()
        pl = self.aview(self.carve(4 * TT * 2), 4 * TT, BF16).rearrange("p (g n) -> p g n", g=4)
        b_pl = [[fw.buf() for _ in sg] for g in range(4)]
        ypT = self.aview(self.carve(4 * TT * 2), 4 * TT, BF16).rearrange("p (g n) -> p g n", g=4)
        b_yp = [[fw.buf() for _ in sg] for g in range(4)]
        t15 = self.aview(self.carve(64), 15, F32); b_t15 = fw.buf()
        if has_s:
            fs = self.aview(self.carve(4 * NS * 19 * 4), 4 * NS * 19, F32).rearrange("p (g s n) -> p g s n", g=4, s=NS)
            b_fs = fw.bufs(4, "fs")
            s3A = self.aview(self.carve(NS * 19 * 4), NS * 19, F32).rearrange("p (s n) -> p s n", s=NS); b_s3A = fw.buf()
            s3B = self.aview(self.carve(NS * 19 * 4), NS * 19, F32).rearrange("p (s n) -> p s n", s=NS); b_s3B = fw.buf()
            us = self.aview(self.carve(4 * SS * 4), 4 * SS, F32).rearrange("p (g n) -> p g n", g=4); b_us = fw.bufs(4, "us")
            nrow = NS * 15
            spsb = self.t2[0:nrow, :]
            b_sp = self.b_t2
            ups = self.aview(self.carve(512 * 4), 512, F32, parts=SS); b_ups = fw.buf()
            b_np = fw.buf()
            fw.dma("sp", o["nps"][NS * b:NS * (b + 1), 0:11, :], d["spool"].rearrange("(s j) c -> s j c", j=15)[NS * b:NS * (b + 1), 4:15, :],
                   writes=[b_np], track=b_np, is_output=True)
        self.cp(uf[:, :, 0:15], self.uhist[:], r=[self.b_uh], w=b_uf)
        wu, wub = self.wpanel(d["w_in"], COL_U, 512)
        for g in range(4):
            for si, (c0, c1, tiles) in enumerate(sg):
                w = c1 - c0
                pu, pub = self.bank()
                for kc in range(NKC):
                    self.mm(pu[:, 0:w], wu[:, kc, g * P:(g + 1) * P], self.hT[:, kc, c0:c1], kc == 0, kc == NKC - 1,
                            r=wub + self.hT_bufs(tiles), w=[pub])
                if c0 < TB:
                    self.act(uf[:, g, 15 + c0:15 + c1], pu[:, 0:w], AF.Copy, r=[pub], w=[b_uf[g]])
                else:
                    self.act(fs[:, g, :, 15:19], pu[:, 0:SS].rearrange("p (s t) -> p s t", t=ST), AF.Copy, r=[pub], w=[b_fs[g]])
                    self.act(us[:, g, :], pu[:, 0:SS], AF.Copy, r=[pub], w=[b_us[g]])
        self.cp(self.uhist[:], uf[:, :, TB:TB + 15], r=b_uf, w=[self.b_uh])
        for g in range(4):
            f = uf[:, g, :]
            lv = [(sA, b_sA), (sB, b_sB)]
            src, srcb = f, b_uf[g]
            sh = 1
            for lvl in range(g + 1):
                dst, dstb = lv[lvl % 2]
                lo = 2 * sh - 1
                self.tt(dst[:, lo:NU], src[:, lo:NU], src[:, lo - sh:NU - sh], ALU.add, r=[srcb], w=[dstb])
                src, srcb = dst, dstb
                sh *= 2
            wsz = float(POOL_W[g])
            self.stt(pl[:, g, 0:TB], src[:, 15:NU], 1.0 / wsz, f[:, 15:NU], ALU.mult, ALU.subtract,
                     r=[srcb, b_uf[g]], w=[b_pl[g][0]])
            if b == 0:
                self.tt(t15, src[:, 15:30], self.icnt[:, g, :], ALU.mult, r=[srcb, self.b_const], w=[b_t15])
                self.tt(pl[:, g, 0:15], t15, f[:, 15:30], ALU.subtract, r=[b_t15, b_uf[g]], w=[b_pl[g][0]])
        self.final_b()
        t0 = b * NT_BLK
        for tt in range(NT_BLK):
            self.load(self.xb[:, tt, :], d["xp"][(t0 + tt) * P:(t0 + tt + 1) * P, :], [self.b_xb[tt]])
        self.load(self.xb[0:SS, NT_BLK, :], d["xs"][SS * b:SS * (b + 1), :], [self.b_xb[NT_BLK]])
        if has_s:
            for g in range(4):
                pt, ptb = self.bank()
                self.tp(pt[:, 0:nrow], spsb[:, g * P:(g + 1) * P], self.identf[0:nrow, 0:nrow], r=[b_sp, self.b_idf], w=[ptb])
                self.cp(fs[:, g, :, 0:15], pt[:, 0:nrow].rearrange("p (s n) -> p s n", n=15), r=[ptb], w=[b_fs[g]])
            for g in range(4):
                f3 = fs[:, g]
                lv = [(s3A, b_s3A), (s3B, b_s3B)]
                src, srcb = f3, b_fs[g]
                sh = 1
                for lvl in range(g + 1):
                    dst, dstb = lv[lvl % 2]
                    lo = 2 * sh - 1
                    self.tt(dst[:, :, lo:19], src[:, :, lo:19], src[:, :, lo - sh:19 - sh], ALU.add, r=[srcb], w=[dstb])
                    src, srcb = dst, dstb
                    sh *= 2
                wsz = float(POOL_W[g])
                self.stt(pl[:, g, TB:TB + SS].rearrange("p (s t) -> p s t", t=ST), src[:, :, 15:19], 1.0 / wsz, f3[:, :, 15:19],
                         ALU.mult, ALU.subtract, r=[srcb, b_fs[g]], w=[b_pl[g][1]])
            pt, ptb = self.bank()
            for g in range(4):
                self.tp(pt[0:SS, g * P:(g + 1) * P], us[:, g, :], self.identf[:, :], r=[b_us[g], self.b_idf], w=[ptb])
            self.cp(ups, pt[0:SS, :], r=[ptb], w=[b_ups])
            for t in range(ST):
                fw.dma("sp", o["nps"][NS * b:NS * (b + 1), 11 + t, :], ups[t:SS:ST, :], reads=[b_ups],
                       writes=[fw.buf()], track=b_ups, is_output=True)
        if b == NBLK - 1:
            pt, ptb = self.bank()
            for g in range(4):
                self.tp(pt[0:15, g * P:(g + 1) * P], uf[:, g, TB:TB + 15], self.identf[:, :], r=[b_uf[g], self.b_idf], w=[ptb])
            self.cp(self.small_out[0:15, :], pt[0:15, :], r=[ptb], w=[self.b_so])
            self.store(o["npp"], self.small_out[0:15, :], [self.b_so])
        if b == 0:
            self.dump("pl0", pl[:, :, 0:TB], [x[0] for x in b_pl], [P, 4, TB])
        for g in range(4):
            for si, (c0, c1, tiles) in enumerate(sg):
                w = c1 - c0
                pg, pgb = self.bank()
                self.mm(pg[:, 0:w], self.wpg[:, g, :], pl[:, g, c0:c1], True, True, r=[self.b_wpg, b_pl[g][si]], w=[pgb])
                self.act(ypT[:, g, c0:c1], pg[:, 0:w], AF.Copy, r=[pgb, self.b_vecT], w=[b_yp[g][si]], scale=self.pscT[:, g:g + 1])
        wpo, wpob = self.wload(d["w_po"].rearrange("(g p) n -> p g n", p=P), (4, D))
        self.b_mT = [[fw.buf() for _ in sg] for fc in range(NKC)]
        wga = None
        for fc in range(NKC):
            if fc % 4 == 0:
                wga, wgab = self.wpanel(d["w_in"], COL_GA + (fc // 4) * 512, 512)
            for si, (c0, c1, tiles) in enumerate(sg):
                w = c1 - c0
                pa, pab = self.bank()
                for g in range(4):
                    self.mm(pa[:, 0:w], wpo[:, g, fc * P:(fc + 1) * P], ypT[:, g, c0:c1], g == 0, g == 3,
                            r=wpob + [b_yp[g][si]], w=[pab])
                pg, pgb = self.bank()
                for kc in range(NKC):
                    self.mm(pg[:, 0:w], wga[:, kc, (fc % 4) * P:(fc % 4 + 1) * P], self.hT[:, kc, c0:c1], kc == 0, kc == NKC - 1,
                            r=wgab + self.hT_bufs(tiles), w=[pgb])
                tg, ti, tgb = self.rot("tg")
                self.act(tg[:, ti, 0:w], pg[:, 0:w], AF.Tanh, r=[pgb], w=[tgb], scale=0.5)
                self.stt(self.mT[:, fc, c0:c1], tg[:, ti, 0:w], 1.0, pa[:, 0:w], ALU.add, ALU.mult, r=[tgb, pab], w=[self.b_mT[fc][si]])
        self.gates_b()

    def heads(self, b, has_s, sg, tl):
        fw, d, o = self.fw, self.d, self.o
        ntl = len(tl)
        qT = self.aview(self.carve(2 * TT * 2), 2 * TT, BF16).rearrange("p (a n) -> p a n", a=2)
        kT = self.aview(self.carve(2 * TT * 2), 2 * TT, BF16).rearrange("p (a n) -> p a n", a=2)
        ktil = self.aview(self.carve(NTL * HD * 2), NTL * HD, BF16).rearrange("p (a n) -> p a n", a=NTL)
        vext = self.aview(self.carve(NTL * 258 * 2), NTL * 258, BF16).rearrange("p (a n) -> p a n", a=NTL)
        og = self.aview(self.carve(NTL * HD * 2), NTL * HD, BF16).rearrange("p (a n) -> p a n", a=NTL)
        hq = self.aview(self.carve(NTL * HD * 4), NTL * HD, F32).rearrange("p (a n) -> p a n", a=NTL)
        hmh = self.aview(self.carve(NTL * HD * 2), NTL * HD, BF16).rearrange("p (a n) -> p a n", a=NTL)
        to = self.aview(self.carve(HD * 4), HD, F32); b_to = fw.buf()
        self.hmT = self.aview(self.carve(NKC * TT * 2), NKC * TT, BF16).rearrange("p (a n) -> p a n", a=NKC)
        self.b_hmT = [fw.bufs(NTL, "hmT") for _ in range(NKC)]
        b_qT = [[fw.buf() for _ in sg] for _ in range(2)]
        b_kT = [fw.bufs(NTL, "kT") for _ in range(2)]
        b_kt = fw.bufs(NTL, "ktil"); b_v = fw.bufs(NTL, "vext"); b_og = fw.bufs(NTL, "og"); b_hq = fw.bufs(NTL, "hq")
        b_hmh = fw.bufs(NTL, "hmh")
        if has_s:
            NCB = 4
            Cin = self.aview(self.carve(NCB * 2 * 258 * 4), NCB * 2 * 258, F32).rearrange("p (i a n) -> p i a n", i=NCB, a=2)
            Cout = self.aview(self.carve(NCB * 2 * 258 * 4), NCB * 2 * 258, F32).rearrange("p (i a n) -> p i a n", i=NCB, a=2)
            Cbs = self.aview(self.carve(NCB * 2 * 258 * 2), NCB * 2 * 258, BF16).rearrange("p (i a n) -> p i a n", i=NCB, a=2)
            b_Cin = fw.bufs(NCB, "Cin"); b_Cout = fw.bufs(NCB, "Cout"); b_Cbs = fw.bufs(NCB, "Cbs")
            kz = self.aview(self.carve(NS * HD * 2), NS * HD, BF16, parts=SS).rearrange("p (s n) -> p s n", s=NS); b_kz = fw.buf()
            qz = self.aview(self.carve(2 * NS * SS * 2), 2 * NS * SS, BF16).rearrange("p (a s n) -> p a s n", a=2, s=NS); b_qz = fw.buf()
            nT = self.aview(self.carve(2 * NBH * 4), 2 * NBH, F32).rearrange("p (a n) -> p a n", a=2); b_nT = fw.buf()
            nTn = self.aview(self.carve(2 * NBH * 4), 2 * NBH, F32).rearrange("p (a n) -> p a n", a=2); b_nTn = fw.buf()
            sn_sb = self.aview(self.carve(HD * 4), HD, F32, parts=NBH); b_sn = fw.buf()
            sms = self.aview(self.carve(SS * 2), SS, BF16, parts=SS); b_sms = fw.buf()
            self.load(sn_sb, d["sn"][NBH * b:NBH * (b + 1), :], [b_sn])
            pt, ptb = self.bank()
            for dc in range(2):
                self.tp(pt[:, dc * NBH:(dc + 1) * NBH], sn_sb[:, dc * P:(dc + 1) * P], self.identf[0:NBH, 0:NBH], r=[b_sn, self.b_idf], w=[ptb])
            self.cp(nT[:].rearrange("p a n -> p (a n)"), pt[:, 0:2 * NBH], r=[ptb], w=[b_nT])
        self.memset(vext[:, :, 256:257], 1.0, w=b_v)
        ci = 0
        def head_norm(hd):
            self.gen_banks_save = self.gen_banks
            self.ts(self.rsq[:, 0:ntl], self.ssq[:, 0:ntl], 1.0 / HD, EPS, ALU.mult, ALU.add, r=[self.b_ssq], w=[self.b_rsq])
            self.act(self.rsq[:, 0:ntl], self.rsq[:, 0:ntl], AF.Sqrt, r=[self.b_rsq], w=[self.b_rsq])
            self.recip(self.rsq[:, 0:ntl], self.rsq[:, 0:ntl], r=[self.b_rsq], w=[self.b_rsq])
            for (tt, pc, c0) in tl:
                self.stt(hmh[0:pc, tt, :], hq[0:pc, tt, :], self.rsq[0:pc, tt:tt + 1], og[0:pc, tt, :], ALU.mult, ALU.mult,
                         r=[b_hq[tt], self.b_rsq, b_og[tt]], w=[b_hmh[tt]])
            for (tt, pc, c0) in tl:
                ptr, ptrb = self.bank()
                pv = ptr.bitcast(BF16)
                for dc in range(2):
                    self.tp(pv[:, dc * P:dc * P + pc], hmh[0:pc, tt, dc * P:(dc + 1) * P], self.identb[0:pc, 0:pc],
                            r=[b_hmh[tt], self.b_idb], w=[ptrb])
                for dc in range(2):
                    self.cp(self.hmT[:, 2 * hd + dc, c0:c0 + pc], pv[:, dc * P:dc * P + pc], r=[ptrb], w=[self.b_hmT[2 * hd + dc][tt]])
            if b == 0 and hd == 0 and False:
                self.dump("hq0", hq[:, 0:NT_BLK, :], b_hq[:NT_BLK], [P, NT_BLK, HD])
                self.dump("hmh0", hmh[:, 0:NT_BLK, :], b_hmh[:NT_BLK], [P, NT_BLK, HD])
        assert NS == NT_BLK
        for hd in range(NH):
            self.gen_banks = (0, 1, 2, 3, 4, 5, 6, 7)
            for s_ in range(NS):
                bhg = (NS * b + s_) * NH + hd
                self.load(Cin[:, s_, :, 0:HD], d["sC"][bhg].rearrange("(a p) e -> p a e", p=P), [b_Cin[s_]])
            wq, wqb = self.wpanel(d["w_in"], COL_Q + hd * HD, HD)
            for dc in range(2):
                for si, (c0, c1, tiles) in enumerate(sg):
                    w = c1 - c0
                    pq, pqb = self.bank()
                    for kc in range(NKC):
                        self.mm(pq[:, 0:w], wq[:, kc, dc * P:(dc + 1) * P], self.hT[:, kc, c0:c1], kc == 0, kc == NKC - 1,
                                r=wqb + self.hT_bufs(tiles), w=[pqb])
                    self.act(qT[:, dc, c0:c1], pq[:, 0:w], AF.Copy, r=[pqb], w=[b_qT[dc][si]])
            wk, wkb = self.wpanel(d["w_in"], COL_K + hd * HD, HD)
            def k_tr(tt, pc, c0):
                ptr, ptrb = self.bank()
                pv = ptr.bitcast(BF16)
                for dc in range(2):
                    self.tp(pv[:, dc * P:dc * P + pc], ktil[0:pc, tt, dc * P:(dc + 1) * P], self.identb[0:pc, 0:pc],
                            r=[b_kt[tt], self.b_idb], w=[ptrb])
                for dc in range(2):
                    self.cp(kT[:, dc, c0:c0 + pc], pv[:, dc * P:dc * P + pc], r=[ptrb], w=[b_kT[dc][tt]])
            prev_k = None
            for (tt, pc, c0) in tl:
                pk, pkb = self.bank()
                for kc in range(NKC):
                    self.mm(pk[0:pc, 0:HD], self.hT[:, kc, c0:c0 + pc], wk[:, kc, :], kc == 0, kc == NKC - 1,
                            r=wkb + [self.b_hT[tt]], w=[pkb])
                self.act(ktil[0:pc, tt, :], pk[0:pc, 0:HD], AF.Copy, r=[pkb, self.b_wt[tt]], w=[b_kt[tt]],
                         scale=self.wsk[0:pc, tt, hd:hd + 1])
                if prev_k is not None:
                    k_tr(*prev_k)
                prev_k = (tt, pc, c0)
            k_tr(*prev_k)
            wv_, wvb = self.wpanel(d["w_in"], COL_V + hd * HD, HD)
            for (tt, pc, c0) in tl:
                pk, pkb = self.bank()
                for kc in range(NKC):
                    self.mm(pk[0:pc, 0:HD], self.hT[:, kc, c0:c0 + pc], wv_[:, kc, :], kc == 0, kc == NKC - 1,
                            r=wvb + [self.b_hT[tt]], w=[pkb])
                self.cp(vext[0:pc, tt, 0:HD], pk[0:pc, 0:HD], r=[pkb], w=[b_v[tt]])
            if hd > 0:
                head_norm(hd - 1)
            wo, wob = self.wpanel(d["w_in"], COL_O + hd * HD, HD)
            for (tt, pc, c0) in tl:
                pk, pkb = self.bank()
                for kc in range(NKC):
                    self.mm(pk[0:pc, 0:HD], self.hT[:, kc, c0:c0 + pc], wo[:, kc, :], kc == 0, kc == NKC - 1,
                            r=wob + [self.b_hT[tt]], w=[pkb])
                self.act(to[0:pc, :], pk[0:pc, 0:HD], AF.Tanh, r=[pkb], w=[b_to], scale=0.5)
                self.stt(og[0:pc, tt, :], to[0:pc, :], 1.0, self.mnh[0:pc, hd * HD:(hd + 1) * HD], ALU.add, ALU.mult,
                         r=[b_to, self.b_mnh], w=[b_og[tt]])
            if b == 0 and hd == 0:
                self.dump("qT0", qT[:, :, 0:TB], [x[0] for x in b_qT], [P, 2, TB])
                self.dump("kT0", kT[:, :, 0:TB], b_kT[0][:NT_BLK] + b_kT[1][:NT_BLK], [P, 2, TB])
                self.dump("vx0", vext[:, 0:NT_BLK, :], b_v[:NT_BLK], [P, NT_BLK, 258])
            self.act(self.Csb[:, hd, :, 0:257], self.Cst[:, hd, :, :], AF.Copy, r=[self.b_C[hd], self.b_dec], w=[self.b_Csb[hd]],
                     scale=self.decbc[:, hd, 0:1])
            self.gen_banks = (7,)
            tts = NT_BLK
            cs0 = TB
            for s_ in range(NS):
                bh = s_ * NH + hd
                self.cp(Cin[:, s_, :, 256:257], nT[:, :, bh:bh + 1], r=[b_nT], w=[b_Cin[s_]])
                dcol = self.decbc[:, hd, NT_BLK + s_:NT_BLK + s_ + 1]
                self.act(Cbs[:, s_, :, 0:257], Cin[:, s_, :, 0:257], AF.Copy, r=[b_Cin[s_], self.b_dec], w=[b_Cbs[s_]], scale=dcol)
            pS, pSb = self.ps[:, 7, :], self.pb[7]
            for dc in range(2):
                self.mm(pS[0:SS, 0:SS], kT[:, dc, cs0:cs0 + SS], qT[:, dc, cs0:cs0 + SS], dc == 0, dc == 1,
                        r=[b_kT[dc][tts], b_qT[dc][1]], w=[pSb])
            self.tt(sms, pS[0:SS, 0:SS], self.masks[:], ALU.mult, r=[pSb, self.b_mk], w=[b_sms])
            for dc in range(2):
                self.tt(qz[:, dc], qT[:, dc, cs0:cs0 + SS].unsqueeze(1).broadcast_to([P, NS, SS]), self.cmb[:], ALU.mult,
                        r=[b_qT[dc][1], self.b_cm], w=[b_qz])
            self.tt(kz, ktil[0:SS, tts, :].unsqueeze(1).broadcast_to([SS, NS, HD]),
                    self.rm[:, :].unsqueeze(2).broadcast_to([SS, NS, HD]), ALU.mult, r=[b_kt[tts], self.b_const], w=[b_kz])
            pNs, pNsb = self.ps[:, 6, :], self.pb[6]
            self.mm(pNs[0:SS, 0:257], sms, vext[0:SS, tts, 0:257], True, False, r=[b_sms, b_v[tts]], w=[pNsb])
            pS2, pS2b = self.ps[:, 2, :], self.pb[2]
            for tt in range(NT_BLK):
                c0 = tt * P
                for dc in range(2):
                    self.mm(pS2[:, tt * P:(tt + 1) * P], kT[:, dc, c0:c0 + P], qT[:, dc, c0:c0 + P], dc == 0, dc == 1,
                            r=[b_kT[dc][tt], b_qT[dc][0]], w=[pS2b])
            sm = self.sm
            for tt in range(NT_BLK):
                self.tt(sm[:, tt, :], pS2[:, tt * P:(tt + 1) * P], self.maskp[:], ALU.mult, r=[pS2b, self.b_mk], w=[self.b_sm[tt]])
            for tt in range(NT_BLK):
                c0 = tt * P
                smi, smb = tt, self.b_sm[tt]
                pC, pCb = self.pair(0)
                for dc in range(2):
                    self.mm(pC[:, dc, 0:257], ktil[:, tt, dc * P:(dc + 1) * P], vext[:, tt, 0:257], True, True,
                            r=[b_kt[tt], b_v[tt]], w=[pCb[dc]])
                nb_ = 3 if tt % 2 == 0 else 2
                pN, pNb = self.ps[:, nb_, :], self.pb[nb_]
                self.mm(pN[:, 0:257], sm[:, smi, :], vext[:, tt, 0:257], True, False, r=[smb, b_v[tt]], w=[pNb])
                for dc in range(2):
                    self.mm(pN[:, 0:257], qT[:, dc, c0:c0 + P], self.Csb[:, hd, dc, 0:257], False, dc == 1,
                            r=[b_qT[dc][0], self.b_Csb[hd]], w=[pNb])
                self.chunk_post(pN, pNb, P, tt, hd, hq, b_hq)
                self.stt(self.Cst[:, hd, :, :], self.Cst[:, hd, :, :], self.decbc[:, hd, tt:tt + 1], pC[:, :, 0:257], ALU.mult, ALU.add,
                         r=[self.b_C[hd], self.b_dec] + pCb, w=[self.b_C[hd]])
                if tt + 1 < NT_BLK:
                    self.act(self.Csb[:, hd, :, 0:257], self.Cst[:, hd, :, :], AF.Copy, r=[self.b_C[hd], self.b_dec],
                             w=[self.b_Csb[hd]], scale=self.decbc[:, hd, tt + 1:tt + 2])
                s_ = tt
                bh = s_ * NH + hd
                bhg = (NS * b + s_) * NH + hd
                dcol = self.decbc[:, hd, NT_BLK + s_:NT_BLK + s_ + 1]
                for dc in range(2):
                    self.mm(pNs[0:SS, 0:257], qz[:, dc, s_, :], Cbs[:, s_, dc, 0:257], False, (s_ == NS - 1 and dc == 1),
                            r=[b_qz, b_Cbs[s_]], w=[pNsb])
                pC2 = self.ps[:, 0:2, :]
                pC2b = [self.pb[0], self.pb[1]]
                for dc in range(2):
                    self.mm(pC2[:, dc, 0:257], kz[:, s_, dc * P:(dc + 1) * P], vext[0:SS, tts, 0:257], True, True,
                            r=[b_kz, b_v[tts]], w=[pC2b[dc]])
                self.stt(Cout[:, s_, :, 0:257], Cin[:, s_, :, 0:257], dcol, pC2[:, :, 0:257], ALU.mult, ALU.add,
                         r=[b_Cin[s_], self.b_dec] + pC2b, w=[b_Cout[s_]])
                fw.dma("sp", o["ncs"][bhg].rearrange("(a p) e -> p a e", p=P), Cout[:, s_, :, 0:HD], reads=[b_Cout[s_]],
                       writes=[fw.buf()], track=b_Cout[s_], is_output=True)
                self.cp(nTn[:, :, bh:bh + 1], Cout[:, s_, :, 256:257], r=[b_Cout[s_]], w=[b_nTn])
            self.chunk_post(pNs, pNsb, SS, tts, hd, hq, b_hq)
            self.gen_banks = (0, 1, 2)
            pass
        head_norm(NH - 1)
        if b == NBLK - 1:
            for hd in range(NH):
                fw.dma("sp", o["ncp"][hd].rearrange("(a p) e -> p a e", p=P), self.Cst[:, hd, :, 0:HD], reads=[self.b_C[hd]],
                       writes=[fw.buf()], track=self.b_C[hd], is_output=True)
            nP = self.small_out[:, 256:264]
            self.cp(nP.rearrange("p (h a) -> p h a", a=2), self.Cst[:, :, :, 256], r=self.b_C, w=[self.b_so])
            pt, ptb = self.bank()
            self.tp(pt[0:8, 0:P], nP, self.identf[:, :], r=[self.b_so, self.b_idf], w=[ptb])
            self.cp(self.small_out[0:8, 300:300 + P], pt[0:8, 0:P], r=[ptb], w=[self.b_so])
            self.store(o["nnp"], self.small_out[0:8, 300:300 + P], [self.b_so])
        if has_s:
            pt, ptb = self.bank()
            for dc in range(2):
                self.tp(pt[0:NBH, dc * P:(dc + 1) * P], nTn[:, dc, :], self.identf[:, :], r=[b_nTn, self.b_idf], w=[ptb])
            self.cp(sn_sb, pt[0:NBH, 0:HD], r=[ptb], w=[b_sn])
            self.store(o["nns"][NBH * b:NBH * (b + 1), :], sn_sb, [b_sn])

    def chunk_post(self, pN, pNb, pc, tt, hd, hq, b_hq):
        rr, ri, rrb = self.rot("rr")
        self.act(rr[0:pc, ri:ri + 1], pN[0:pc, 256:257], AF.Abs, r=[pNb], w=[rrb])
        self.ts(rr[0:pc, ri:ri + 1], rr[0:pc, ri:ri + 1], self.thrT[0:pc, tt, hd:hd + 1], None, ALU.max,
                r=[rrb, self.b_wt[tt]], w=[rrb])
        self.recip(rr[0:pc, ri:ri + 1], rr[0:pc, ri:ri + 1], r=[rrb], w=[rrb])
        self.act(hq[0:pc, tt, :], pN[0:pc, 0:HD], AF.Copy, r=[pNb, rrb], w=[b_hq[tt]], scale=rr[0:pc, ri:ri + 1])
        self.act(self.junk[0:pc, 0:HD], pN[0:pc, 0:HD], AF.Square, r=[pNb, rrb], w=[self.b_junk, self.b_ssq],
                 scale=rr[0:pc, ri:ri + 1], accum_out=self.ssq[0:pc, tt:tt + 1])

    def xupdate_a(self, pM, pMb, c0, c1, tiles, fc, gtT):
        w = c1 - c0
        if c0 < TB:
            gmv, gi, gb = self.rot("gmv")
            g2 = gmv[:, gi, :]
            self.act(g2[:, 0:w], pM[:, 0:w], AF.Copy, r=[pMb, self.b_ada, self.b_der], w=[gb], scale=gtT[:, fc, 0:1])
        else:
            gmv, gi, gb = self.rot("gmvs")
            g2 = gmv[:, gi, :]
            self.tt(g2[:, 0:SS].rearrange("p (s t) -> p s t", t=ST), pM[:, 0:SS].rearrange("p (s t) -> p s t", t=ST),
                    gtT[:, fc, 1 + NS * self.cb:1 + NS * (self.cb + 1)].unsqueeze(2).broadcast_to([P, NS, ST]), ALU.mult, r=[pMb, self.b_ada, self.b_der], w=[gb])
        return (g2, gb, c0, tiles, fc)

    def xupdate_b(self, st):
        g2, gb, c0, tiles, fc = st
        pT, pTb = self.bankN()
        pv = pT.rearrange("p (j f) -> p j f", f=P)
        for j, tt in enumerate(tiles):
            pc = P if tt < NT_BLK else SS
            self.tp(pv[0:pc, j, :], g2[:, j * P:j * P + pc], self.identf[:, :], r=[gb, self.b_idf], w=[pTb])
        if c0 < TB:
            n = len(tiles)
            xv = self.xb[:, tiles[0]:tiles[0] + n, fc * P:(fc + 1) * P]
            self.tt(xv, xv, pv[:, 0:n, :], ALU.add, r=[pTb] + [self.b_xb[t] for t in tiles], w=[self.b_xb[t] for t in tiles])
        else:
            xv = self.xb[0:SS, NT_BLK, fc * P:(fc + 1) * P]
            self.tt(xv, xv, pv[0:SS, 0, :], ALU.add, r=[pTb, self.b_xb[NT_BLK]], w=[self.b_xb[NT_BLK]])

    def mixer_out(self, b, has_s, sg, tl):
        fw, d, o = self.fw, self.d, self.o
        self.gen_banks = (0, 1, 2, 3, 4, 5, 6, 7)
        for fc in range(NKC):
            if fc % 4 == 0:
                wmo, wmob = self.wpanel(d["w_mo"], (fc // 4) * 512, 512)
                wgb_, wgbb = self.wpanel(d["w_in"], COL_GB + (fc // 4) * 512, 512)
            for si, (c0, c1, tiles) in enumerate(sg):
                w = c1 - c0
                pB, pBb = self.bank()
                for kc in range(NKC):
                    self.mm(pB[:, 0:w], wmo[:, kc, (fc % 4) * P:(fc % 4 + 1) * P], self.hmT[:, kc, c0:c1], kc == 0, kc == NKC - 1,
                            r=wmob + [self.b_hmT[kc][t] for t in tiles], w=[pBb])
                pg, pgb = self.bank()
                for kc in range(NKC):
                    self.mm(pg[:, 0:w], wgb_[:, kc, (fc % 4) * P:(fc % 4 + 1) * P], self.hT[:, kc, c0:c1], kc == 0, kc == NKC - 1,
                            r=wgbb + self.hT_bufs(tiles), w=[pgb])
                tg, ti, tgb = self.rot("tg")
                self.act(tg[:, ti, 0:w], pg[:, 0:w], AF.Tanh, r=[pgb], w=[tgb], scale=0.5)
                self.stt(self.t2[:, 0:w], tg[:, ti, 0:w], 1.0, pB[:, 0:w], ALU.add, ALU.mult, r=[tgb, pBb], w=[self.b_t2])
                self.tt(self.mT[:, fc, c0:c1], self.t2[:, 0:w], self.mT[:, fc, c0:c1], ALU.add, r=[self.b_t2, self.b_mT[fc][si]],
                        w=[self.b_mT[fc][si]])
        if b == 0:
            self.dump("mT0", self.mT[:, :, 0:TB], [x[0] for x in self.b_mT], [P, NKC, TB])
        self.gen_banks = (0, 1, 2, 4, 5, 7)
        pend = []
        for fc in range(NKC):
            if fc % 4 == 0:
                wou, woub = self.wpanel(d["w_out"], (fc // 4) * 512, 512)
            cur = []
            for si, (c0, c1, tiles) in enumerate(sg):
                w = c1 - c0
                pM, pMb = self.bank()
                for kc in range(NKC):
                    self.mm(pM[:, 0:w], wou[:, kc, (fc % 4) * P:(fc % 4 + 1) * P], self.mT[:, kc, c0:c1], kc == 0, kc == NKC - 1,
                            r=woub + [self.b_mT[kc][si]], w=[pMb])
                cur.append(self.xupdate_a(pM, pMb, c0, c1, tiles, fc, self.hgt1))
            for st_ in pend:
                self.xupdate_b(st_)
            pend = cur
        for st_ in pend:
            self.xupdate_b(st_)
        if b == 0:
            self.dump("x10", self.xb[:, 0:NT_BLK, :], self.b_xb[:NT_BLK], [P, NT_BLK, D])
        for _ in self.norm_gen(tl, self.gsc2, self.sh2, b):
            pass

    def ffn(self, b, has_s, sg, tl, t0):
        fw, d, o = self.fw, self.d, self.o
        hidT = self.aview(self.carve(32 * TT * 2), 32 * TT, BF16).rearrange("p (j n) -> p j n", j=32)
        b_hid = [[fw.buf() for _ in sg] for _ in range(32)]
        rl = self.aview(self.carve(2 * 512 * 4), 2 * 512, F32).rearrange("p (i n) -> p i n", i=2); b_rl = fw.bufs(2, "rl")
        rli = 0
        self.gen_banks = (0, 1, 2, 3, 4, 5, 6, 7)
        for j in range(32):
            if j % 4 == 0:
                w1p, w1b = self.wpanel(d["w1"], (j // 4) * 512, 512)
            for si, (c0, c1, tiles) in enumerate(sg):
                w = c1 - c0
                pH, pHb = self.bank()
                for kc in range(NKC):
                    self.mm(pH[:, 0:w], w1p[:, kc, (j % 4) * P:(j % 4 + 1) * P], self.hT[:, kc, c0:c1], kc == 0, kc == NKC - 1,
                            r=w1b + self.hT_bufs(tiles), w=[pHb])
                ri = rli % 2
                rli += 1
                self.act(rl[:, ri, 0:w], pH[:, 0:w], AF.Relu, r=[pHb], w=[b_rl[ri]])
                self.tt(hidT[:, j, c0:c1], pH[:, 0:w], rl[:, ri, 0:w], ALU.mult, r=[pHb, b_rl[ri]], w=[b_hid[j][si]])
        self.gen_banks = (0, 1, 2)
        pend = []
        nxt = self.prep_gen(b + 1) if b + 1 < NBLK else iter(())
        if b + 1 < NBLK:
            self.spool_load(b + 1)
        next(nxt, None)
        for fc in range(NKC):
            w2p, w2b = self.wload(d["w2"][:, fc * P:(fc + 1) * P].rearrange("(j p) n -> p j n", p=P), (32, P))
            cur = []
            for si, (c0, c1, tiles) in enumerate(sg):
                w = c1 - c0
                pO, pOb = self.bank()
                for j in range(32):
                    self.mm(pO[:, 0:w], w2p[:, j, :], hidT[:, j, c0:c1], j == 0, j == 31, r=w2b + [b_hid[j][si]], w=[pOb])
                cur.append(self.xupdate_a(pO, pOb, c0, c1, tiles, fc, self.gt2))
            for st_ in pend:
                self.xupdate_b(st_)
            pend = cur
            if fc == NKC - 1:
                for st_ in pend:
                    self.xupdate_b(st_)
                pend = []
            if fc != 0:
                next(nxt, None)
        for st_ in pend:
            self.xupdate_b(st_)
        for _ in nxt:
            pass
        for (tt, pc, c0) in tl:
            self.act(self.junk[0:pc, :], self.xb[0:pc, tt, :], AF.Square, r=[self.b_xb[tt]], w=[self.b_junk, self.b_ss],
                     accum_out=self.ss[0:pc, tt:tt + 1])
        n = len(tl)
        self.ts(self.rstd[:, 0:n], self.ss[:, 0:n], 1.0 / D, EPS, ALU.mult, ALU.add, r=[self.b_ss], w=[self.b_rstd])
        self.act(self.rstd[:, 0:n], self.rstd[:, 0:n], AF.Sqrt, r=[self.b_rstd], w=[self.b_rstd])
        self.recip(self.rstd[:, 0:n], self.rstd[:, 0:n], r=[self.b_rstd], w=[self.b_rstd])
        self.final_pending = (b, tl, t0)
        if b == NBLK - 1:
            self.final_b()

    def final_b(self):
        if self.final_pending is None:
            return
        fw, o = self.fw, self.o
        b, tl, t0 = self.final_pending
        self.final_pending = None
        for i, (tt, pc, c0) in enumerate(tl):
            self.stt(self.xb[0:pc, tt, :], self.xb[0:pc, tt, :], self.rstd[0:pc, tt:tt + 1], self.gfin[0:pc, :], ALU.mult, ALU.mult,
                     r=[self.b_xb[tt], self.b_rstd, self.b_gfin], w=[self.b_xb[tt]])
            if tt < NT_BLK:
                dst = o["yp"][(t0 + tt) * P:(t0 + tt + 1) * P, :]
            else:
                dst = o["ys"][SS * b:SS * (b + 1), :]
            fw.dma("sp", dst, self.xb[0:pc, tt, :], reads=[self.b_xb[tt]], writes=[fw.buf()], track=self.b_xb[tt], is_output=True)


_CACHE = {}


def _program(dbg=()):
    key = tuple(sorted(dbg))
    if key not in _CACHE:
        _CACHE[key] = MK(dbg)
    return _CACHE[key]


def make_in_maps(inputs):
    f = lambda a: np.ascontiguousarray(np.asarray(a, dtype=np.float32))
    I = {k: f(v) for k, v in inputs.items()}
    consts = _consts()
    vec_common = [I["g_mix"][0].reshape(8, P), I["g_ffn"][0].reshape(8, P), I["b_ada"][0].reshape(48, P),
                  I["pool_scale"][0].reshape(4, P)]
    vecs = np.concatenate(vec_common, axis=0)
    bif = np.stack([I["b_i"][0], I["b_f"][0]], axis=1)
    shared = dict(vecs=vecs, bif=bif, m_norm=I["m_norm"][0], g_final=I["g_final"], w_ada=I["w_ada"][0], w_in=I["w_in"][0],
                  w_pg=I["w_pool_group"][0], w_po=I["w_pool_out"][0], w_mo=I["w_m_out"][0], w_out=I["w_out"][0],
                  w1=I["w1"][0], w2=I["w2"][0], **consts)
    maps = []
    for c in range(8):
        s0, s1 = c * NS_CORE, (c + 1) * NS_CORE
        m = dict(shared)
        m["xp"] = I["x_prompt"][c]
        m["xs"] = I["x_sample"][s0:s1].reshape(SS_CORE, D)
        m["call"] = np.concatenate([I["c_prompt"][c:c + 1], I["c_sample"][s0:s1]], axis=0)
        m["spool"] = I["state_pool"][0, s0:s1].reshape(NS_CORE * 15, 512)
        m["sC"] = I["state_C"][0, s0:s1].reshape(NS_CORE * NH, HD, HD)
        m["sn"] = I["state_n"][0, s0:s1].reshape(NS_CORE * NH, HD)
        m["smT"] = np.ascontiguousarray(I["state_m"][0, s0:s1].T)
        maps.append(m)
    return maps


def kernel(**inputs):
    prog = _program()
    maps = make_in_maps(inputs)
    res = run_bass_kernel_spmd(prog.nc, maps, core_ids=list(range(8)))
    R = res.results
    y_prompt = np.stack([R[c]["yp"] for c in range(8)], axis=0)
    y_sample = np.concatenate([R[c]["ys"].reshape(NS_CORE, ST, D) for c in range(8)], axis=0)
    npp = np.stack([R[c]["npp"] for c in range(8)], axis=0)[None]
    ncp = np.stack([R[c]["ncp"] for c in range(8)], axis=0)[None]
    nnp = np.stack([R[c]["nnp"].reshape(NH, HD) for c in range(8)], axis=0)[None]
    nmp = np.stack([R[c]["nmp"].reshape(NH) for c in range(8)], axis=0)[None]
    nps = np.concatenate([R[c]["nps"] for c in range(8)], axis=0)[None]
    ncs = np.concatenate([R[c]["ncs"].reshape(NS_CORE, NH, HD, HD) for c in range(8)], axis=0)[None]
    nns = np.concatenate([R[c]["nns"].reshape(NS_CORE, NH, HD) for c in range(8)], axis=0)[None]
    nms = np.concatenate([np.ascontiguousarray(R[c]["nmsT"].T) for c in range(8)], axis=0)[None]
    outs = (y_prompt, y_sample, npp, ncp, nnp, nmp, nps, ncs, nns, nms)
    return tuple(np.ascontiguousarray(a, dtype=np.float32) for a in outs)
```

```python
import numpy as np
from contextlib import ExitStack
import concourse.bass as bass
import concourse.mybir as mybir
from concourse.bass_utils import run_bass_kernel_spmd

F32 = mybir.dt.float32
BF16 = mybir.dt.bfloat16
AF = mybir.ActivationFunctionType
ALU = mybir.AluOpType

ENGS = ("pe", "act", "dve", "pool", "sp")
ATTACH = True


class Buf:
    __slots__ = ("name", "writer", "readers", "dsem", "dcount")

    def __init__(self, name):
        self.name = name
        self.writer = None
        self.readers = []
        self.dsem = None
        self.dcount = 0


class Ins:
    __slots__ = ("fn", "waits", "needed", "dma", "rank", "attach")

    def __init__(self, fn, waits, dma=None, attach=False):
        self.fn = fn
        self.waits = waits
        self.needed = False
        self.dma = dma
        self.rank = None
        self.attach = attach


class FW:
    def __init__(self, nc, stack, same_engine_sync=True):
        self.nc = nc
        self.stack = stack
        self.streams = {e: [] for e in ENGS}
        self.seen = {e: {} for e in ENGS}
        self.same_engine_sync = same_engine_sync
        self.esem = {}
        for e in ("pe", "act", "dve", "pool"):
            self.esem[e] = stack.enter_context(nc.semaphore("es_" + e))
        self.dsems = []
        self.out_events = []
        self.n_bufs = 0
        self.pend = {}

    def buf(self, name=None):
        self.n_bufs += 1
        return Buf(name or f"b{self.n_bufs}")

    def bufs(self, n, name="b"):
        return [self.buf(f"{name}{i}") for i in range(n)]

    def _collect(self, eng, reads, writes):
        deps = {}

        def add(ev):
            if ev is None:
                return
            key = (ev[0], ev[1])
            if key not in deps or deps[key] < ev[2]:
                deps[key] = ev[2]

        for b in reads:
            add(b.writer)
        for b in writes:
            add(b.writer)
            for r in b.readers:
                add(r)
        waits = []
        seen = self.seen[eng]
        for key, idx in deps.items():
            if key[0] == "e" and key[1] == eng:
                if eng in ("pe", "sp") or not self.same_engine_sync:
                    continue
            if seen.get(key, -1) >= idx:
                continue
            seen[key] = idx
            waits.append((key[0], key[1], idx))
            if key[0] == "e":
                self.streams[key[1]][idx].needed = True
        return waits

    def op(self, eng, fn, reads=(), writes=(), attach=False):
        waits = self._collect(eng, reads, writes)
        idx = len(self.streams[eng])
        self.streams[eng].append(Ins(fn, waits, attach=attach and ATTACH))
        ev = ("e", eng, idx)
        for b in reads:
            b.readers.append(ev)
        for b in writes:
            b.writer = ev
            b.readers = []
        return ev

    def dma(self, q, out, in_, reads=(), writes=(), track=None, is_output=False, **kw):
        if track is None:
            track = writes[0] if writes else reads[0]
        if track.dsem is None:
            track.dsem = len(self.dsems)
            self.dsems.append(self.stack.enter_context(self.nc.semaphore(f"ds{track.dsem}")))
        waits = self._collect(q, reads, writes)
        track.dcount += 16
        ev = ("d", track.dsem, track.dcount)
        fn = lambda e, out=out, in_=in_, kw=kw: e.dma_start(out=out, in_=in_, **kw)
        self.streams[q].append(Ins(fn, waits, dma=(track.dsem, track.dcount)))
        for b in reads:
            b.readers.append(ev)
        for b in writes:
            b.writer = ev
            b.readers = []
        if is_output:
            self.out_events.append(ev)
        if q == "sp":
            self.pend[track.dsem] = track.dcount
        return ev

    def barrier(self, engines=("act", "dve", "sp")):
        last = {}
        for e in ("pe", "act", "dve"):
            s = self.streams[e]
            for i in range(len(s) - 1, -1, -1):
                if s[i].dma is None and s[i].fn is not None:
                    last[e] = i
                    break
        for e in engines:
            waits = []
            for src, idx in last.items():
                if src == e and (e == "pe" or not self.same_engine_sync):
                    continue
                key = ("e", src)
                if self.seen[e].get(key, -1) >= idx:
                    continue
                self.seen[e][key] = idx
                self.streams[src][idx].needed = True
                waits.append(("e", src, idx))
            if e != "sp":
                for sem, cnt in self.pend.items():
                    key = ("d", sem)
                    if self.seen[e].get(key, -1) >= cnt:
                        continue
                    self.seen[e][key] = cnt
                    waits.append(("d", sem, cnt))
            if waits:
                self.streams[e].append(Ins(None, waits))

    def finish(self):
        fin = {}
        for ev in self.out_events:
            key = (ev[0], ev[1])
            fin[key] = max(fin.get(key, 0), ev[2])
        self.streams["sp"].append(Ins(None, [(k[0], k[1], v) for k, v in fin.items()]))
        for e in ("pe", "act", "dve", "pool"):
            r = 0
            for ins in self.streams[e]:
                if ins.needed:
                    r += 1
                    ins.rank = r
        engmap = {"pe": "tensor", "act": "scalar", "dve": "vector", "pool": "gpsimd", "sp": "sync"}

        def replay(eobj, e):
            for ins in self.streams[e]:
                ws = ins.waits
                last = None
                if ins.attach and ins.fn is not None and ws:
                    last = ws[-1]
                    ws = ws[:-1]
                for w in ws:
                    if w[0] == "e":
                        eobj.wait_ge(self.esem[w[1]], self.streams[w[1]][w[2]].rank)
                    else:
                        eobj.wait_ge(self.dsems[w[1]], w[2])
                if ins.fn is None:
                    continue
                bi = ins.fn(eobj)
                if last is not None:
                    if last[0] == "e":
                        bi._wait_ge(self.esem[last[1]], self.streams[last[1]][last[2]].rank)
                    else:
                        bi._wait_ge(self.dsems[last[1]], last[2])
                if ins.dma is not None:
                    bi.then_inc(self.dsems[ins.dma[0]], 16)
                elif ins.needed:
                    bi.then_inc(self.esem[e], 1)

        with self.nc.Block() as block:
            for e in ENGS:
                if self.streams[e]:
                    getattr(block, engmap[e])(lambda eobj, e=e: replay(eobj, e))

    def stats(self):
        return {e: (len(s), sum(1 for i in s if i.needed), sum(len(i.waits) for i in s))
                for e, s in self.streams.items()}


P = 128
D = 1024
NKC = 8
SEQ = 2048
NS_CORE = 16
ST = 4
SS_CORE = NS_CORE * ST
NS = 4
SS = NS * ST
HD = 256
NH = 4
NBH = NS * NH
DFF = 4096
COL_U, COL_Q, COL_K, COL_V, COL_O, COL_I, COL_F, COL_GA, COL_GB = 0, 512, 1536, 2560, 3584, 4608, 4612, 4616, 5640
INW = 6664
EPS = 1e-6
POOL_W = (2, 4, 8, 16)

NT_BLK = 4
NBLK = SEQ // (NT_BLK * P)
TB = NT_BLK * P
TT = TB + SS
NTL = NT_BLK + 1
NWB = 4


def _consts():
    c = {}
    c["identf"] = np.eye(P, dtype=np.float32)
    s = np.arange(P)
    c["maskp"] = (s[:, None] <= s[None, :]).astype(np.float32)
    t = np.arange(SS)
    c["masks"] = ((t[:, None] // ST == t[None, :] // ST) & (t[:, None] <= t[None, :])).astype(np.float32)
    cm = (np.arange(NS)[:, None] == (t[None, :] // ST)).astype(np.float32)
    c["cm"] = np.broadcast_to(cm[None], (P, NS, SS)).reshape(P, NS * SS).copy()
    c["rm"] = (t[:, None] // ST == np.arange(NS)[None, :]).astype(np.float32)
    ic = np.zeros((P, 4, 15), np.float32)
    for g, w in enumerate(POOL_W):
        ic[:, g, :] = 1.0 / np.minimum(np.arange(15) + 1, w)
    c["icnt"] = ic.reshape(P, 60)
    c["ones4"] = np.ones((4, P), np.float32)
    c["onesrow"] = np.ones((4, TT), np.float32)
    return c


class MK:
    def __init__(self, dbg=()):
        self.dbg = set(dbg)
        self.nc = bass.Bass("TRN2", target_bir_lowering=False)
        self.st = ExitStack()
        self.fw = FW(self.nc, self.st)
        self.dbg_out = {}
        self.build()

    def din(self, name, shape):
        return self.nc.dram_tensor(name, list(shape), F32, kind="ExternalInput").ap()

    def dout(self, name, shape):
        return self.nc.dram_tensor(name, list(shape), F32, kind="ExternalOutput").ap()

    def sb(self, name, shape, dt=F32):
        return self.st.enter_context(self.nc.sbuf_tensor("s_" + name, list(shape), dt))

    def carve(self, nbytes):
        off = self.aoff
        self.aoff += (nbytes + 63) // 64 * 64
        assert self.aoff <= self.asize, (self.aoff, self.asize)
        return off

    def aview(self, off, n, dt, parts=P):
        if dt == BF16:
            return self.arena[0:parts, off // 2: off // 2 + n]
        return self.arena[0:parts, off // 2: off // 2 + 2 * n].bitcast(F32)

    def act(self, out, in_, func, r=(), w=(), **kw):
        return self.fw.op("act", lambda e: e.activation(out=out, in_=in_, func=func, **kw), r, w, attach=("accum_out" not in kw))

    def mm(self, out, lhsT, rhs, start, stop, r=(), w=()):
        return self.fw.op("pe", lambda e: e.matmul(out, lhsT=lhsT, rhs=rhs, start=start, stop=stop), r, w)

    def tp(self, out, in_, ident, r=(), w=()):
        return self.fw.op("pe", lambda e: e.transpose(out=out, in_=in_, identity=ident), r, w)

    def tt(self, out, in0, in1, op, r=(), w=(), eng="dve"):
        return self.fw.op(eng, lambda e: e.tensor_tensor(out=out, in0=in0, in1=in1, op=op), r, w, attach=True)

    def ts(self, out, in0, s1, s2, op0, op1=None, r=(), w=(), eng="dve"):
        if op1 is None:
            return self.fw.op(eng, lambda e: e.tensor_scalar(out=out, in0=in0, scalar1=s1, scalar2=None, op0=op0), r, w, attach=True)
        return self.fw.op(eng, lambda e: e.tensor_scalar(out=out, in0=in0, scalar1=s1, scalar2=s2, op0=op0, op1=op1), r, w, attach=True)

    def stt(self, out, in0, scalar, in1, op0, op1, r=(), w=()):
        return self.fw.op("dve", lambda e: e.scalar_tensor_tensor(out=out, in0=in0, scalar=scalar, in1=in1, op0=op0, op1=op1), r, w, attach=True)

    def cp(self, out, in_, r=(), w=(), eng="dve"):
        return self.fw.op(eng, lambda e: e.tensor_copy(out=out, in_=in_), r, w, attach=True)

    def recip(self, out, in_, r=(), w=()):
        return self.fw.op("dve", lambda e: e.reciprocal(out=out, in_=in_), r, w, attach=True)

    def memset(self, ap, val, w=(), eng="dve"):
        return self.fw.op(eng, lambda e: e.memset(ap, val), (), w)

    def load(self, out, in_, w, q="sp", **kw):
        return self.fw.dma(q, out, in_, writes=w, **kw)

    def store(self, out, in_, r, track=None):
        return self.fw.dma("sp", out, in_, reads=r, writes=[self.fw.buf("o")], track=track or r[0], is_output=True)

    def bank(self):
        i = self.gen_banks[self.gen_ptr % len(self.gen_banks)]
        self.gen_ptr += 1
        return self.ps[:, i, :], self.pb[i]

    def bankN(self):
        i = (3, 6)[self.n_ptr % 2]
        self.n_ptr += 1
        return self.ps[:, i, :], self.pb[i]

    def pair(self, which=0):
        i = (4, 6)[which]
        return self.ps[:, i:i + 2, :], [self.pb[i], self.pb[i + 1]]

    def wload(self, src, shape):
        a, b = shape
        n = a * b
        nh = 2 * NWB
        if n <= 2048:
            i = self.w_ptr % nh
            self.w_ptr += 1
            bufs = [self.wb[i]]
        else:
            if self.w_ptr % 2:
                self.w_ptr += 1
            i = self.w_ptr % nh
            self.w_ptr += 2
            bufs = [self.wb[i], self.wb[i + 1]]
        view = self.wring[:, i * 2048:i * 2048 + n].rearrange("p (a b) -> p a b", a=a)
        self.fw.dma("pool", view, src, writes=bufs, track=bufs[0])
        return view, bufs

    def wpanel(self, w_ap, c0, ncols):
        src = w_ap[:, c0:c0 + ncols].rearrange("(kc p) n -> p kc n", p=P)
        return self.wload(src, (NKC, ncols))

    def dump(self, name, ap, bufs, shape, parts=P):
        if name not in self.dbg:
            return
        o = self.nc.dram_tensor("dbg_" + name, [parts] + list(shape[1:]), ap.dtype, kind="ExternalOutput").ap()
        self.fw.dma("sp", o, ap, reads=bufs, writes=[self.fw.buf()], track=bufs[0], is_output=True)
        self.dbg_out[name] = "dbg_" + name

    def build(self):
        nc, fw = self.nc, self.fw
        d = self.d = {}
        for name, shape in [("xp", (SEQ, D)), ("xs", (SS_CORE, D)), ("call", (17, D)), ("spool", (NS_CORE * 15, 512)),
                            ("sC", (NS_CORE * NH, HD, HD)), ("sn", (NS_CORE * NH, HD)), ("smT", (NH, NS_CORE)),
                            ("vecs", (68, P)), ("bif", (NH, 2)), ("m_norm", (NH, HD)), ("g_final", (D,)),
                            ("w_ada", (D, 6 * D)), ("w_in", (D, INW)), ("w_pg", (4, P, P)), ("w_po", (512, D)),
                            ("w_mo", (D, D)), ("w_out", (D, D)), ("w1", (D, DFF)), ("w2", (DFF, D)),
                            ("identf", (P, P)), ("maskp", (P, P)), ("masks", (SS, SS)), ("cm", (P, NS * SS)),
                            ("rm", (SS, NS)), ("icnt", (P, 60)), ("ones4", (4, P)), ("onesrow", (4, TT))]:
            d[name] = self.din(name, shape)
        o = self.o = {}
        for name, shape in [("yp", (SEQ, D)), ("ys", (SS_CORE, D)), ("npp", (15, 512)), ("ncp", (NH, HD, HD)),
                            ("nnp", (8, P)), ("nmp", (NH, 1)), ("nps", (NS_CORE, 15, 512)), ("ncs", (NS_CORE * NH, HD, HD)),
                            ("nns", (NS_CORE * NH, HD)), ("nmsT", (NH, NS_CORE))]:
            o[name] = self.dout(name, shape)

        self.ps = self.st.enter_context(nc.psum_tensor("ps", [P, 8, 512], F32))
        self.pb = fw.bufs(8, "bank")
        self.gen_banks = (0, 1, 2)
        self.gen_ptr = 0
        self.n_ptr = 0

        self.xb = self.sb("xb", [P, NTL, D]); self.b_xb = fw.bufs(NTL, "xb")
        self.hT = self.sb("hT", [P, NKC, TT], BF16); self.b_hT = fw.bufs(NTL, "hT")
        self.mT = self.sb("mT", [P, NKC, TT], BF16)
        self.wring = self.sb("wring", [P, NWB * 4096], BF16); self.wb = fw.bufs(2 * NWB, "wr"); self.w_ptr = 0
        self.Cst = self.sb("Cst", [P, NH, 2, 257]); self.b_C = fw.bufs(NH, "C")
        self.Csb = self.sb("Csb", [P, NH, 2, 258], BF16); self.b_Csb = fw.bufs(NH, "Csb")
        self.asize = 56 * 1024
        self.arena = self.sb("arena", [P, self.asize // 2], BF16)

        self.final_pending = None
        self.setup_consts()
        self.gen_banks = (0, 1, 2, 3)
        self.spool_load(0)
        gen0 = self.prep_gen(0)
        next(gen0)
        self.setup_ada()
        self.gen_banks = (0, 1, 2, 3)
        for _ in gen0:
            pass
        for b in range(NBLK):
            self.block(b)
        fw.finish()

    def setup_consts(self):
        fw, d = self.fw, self.d
        self.identf = self.sb("identf", [P, P]); self.b_idf = fw.buf()
        self.load(self.identf[:], d["identf"], [self.b_idf])
        self.identb = self.sb("identb", [P, P], BF16); self.b_idb = fw.buf()
        self.cp(self.identb[:], self.identf[:], r=[self.b_idf], w=[self.b_idb])
        self.maskp = self.sb("maskp", [P, P]); self.b_mk = fw.buf()
        self.load(self.maskp[:], d["maskp"], [self.b_mk])
        self.masks = self.sb("masks", [SS, SS])
        self.load(self.masks[:], d["masks"], [self.b_mk])
        cmf = self.aview(0, NS * SS, F32)
        b_cmf = fw.buf()
        self.load(cmf, d["cm"], [b_cmf])
        self.cmb = self.sb("cmb", [P, NS, SS], BF16); self.b_cm = fw.buf()
        self.cp(self.cmb[:].rearrange("p a b -> p (a b)"), cmf, r=[b_cmf], w=[self.b_cm])
        self.rm = self.sb("rm", [SS, NS])
        self.load(self.rm[:], d["rm"], [self.b_cm])
        self.icnt = self.sb("icnt", [P, 4, 15])
        self.load(self.icnt[:].rearrange("p a b -> p (a b)"), d["icnt"], [self.b_cm])
        self.ones4 = self.sb("ones4", [4, P])
        self.load(self.ones4[:], d["ones4"], [self.b_cm])
        self.onesrow = self.sb("onesrow", [4, TT])
        self.load(self.onesrow[:], d["onesrow"], [self.b_cm])
        self.b_const = self.b_cm
        self.cols = self.sb("cols", [P, 4]); self.b_cols = fw.buf()
        self.memset(self.cols[:, 0:1], 1.0, w=[self.b_cols])
        self.memset(self.cols[:, 1:2], EPS, w=[self.b_cols])
        self.onec = self.cols[:, 0:1]
        vec_sb = self.sb("vec_sb", [68, P]); b_v = fw.buf()
        self.load(vec_sb[:], d["vecs"], [b_v])
        self.vecT = self.sb("vecT", [P, 68]); self.b_vecT = fw.buf()
        pst, pbk = self.bank()
        self.tp(pst[:, 0:68], vec_sb[:], self.identf[0:68, 0:68], r=[b_v, self.b_idf], w=[pbk])
        self.cp(self.vecT[:], pst[:, 0:68], r=[pbk], w=[self.b_vecT])
        self.gmixT = self.vecT[:, 0:8]
        self.gffnT = self.vecT[:, 8:16]
        self.badaT = self.vecT[:, 16:64]
        self.pscT = self.vecT[:, 64:68]
        self.bif = self.sb("bif", [NH, 4]); self.b_bif = fw.buf()
        self.load(self.bif[:, 0:2], d["bif"], [self.b_bif])
        self.ts(self.bif[:, 2:3], self.bif[:, 1:2], -1.0, None, ALU.mult, r=[self.b_bif], w=[self.b_bif])
        self.mnh = self.sb("mnh", [P, D]); self.b_mnh = fw.buf()
        self.load(self.mnh[:], d["m_norm"].rearrange("a b -> (a b)").partition_broadcast(P), [self.b_mnh])
        self.ts(self.mnh[:], self.mnh[:], 0.5, None, ALU.mult, r=[self.b_mnh], w=[self.b_mnh])
        self.gfin = self.sb("gfin", [P, D]); self.b_gfin = fw.buf()
        self.load(self.gfin[:], d["g_final"].partition_broadcast(P), [self.b_gfin])
        self.wg = self.sb("wg", [P, NKC, 8], BF16); self.b_wg = fw.buf()
        fw.dma("pool", self.wg[:], d["w_in"][:, COL_I:COL_I + 8].rearrange("(kc p) n -> p kc n", p=P), writes=[self.b_wg])
        self.wpg = self.sb("wpg", [P, 4, P], BF16); self.b_wpg = fw.buf()
        fw.dma("pool", self.wpg[:], d["w_pg"].rearrange("g c d -> c g d"), writes=[self.b_wpg])
        self.memset(self.Cst[:].rearrange("p a b c -> p (a b c)"), 0.0, w=self.b_C)
        self.memset(self.Csb[:].rearrange("p a b c -> p (a b c)"), 0.0, w=self.b_Csb)
        self.uhist = self.sb("uhist", [P, 4, 15]); self.b_uh = fw.buf()
        self.memset(self.uhist[:].rearrange("p a b -> p (a b)"), 0.0, w=[self.b_uh])
        self.mprev = self.sb("mprev", [NH, 2]); self.b_mprev = fw.buf()
        self.memset(self.mprev[:], 0.0, w=[self.b_mprev])
        self.m0s = self.sb("m0s", [NH, NS_CORE]); self.b_m0s = fw.buf()
        self.load(self.m0s[:], d["smT"], [self.b_m0s])
        self.g_i = self.sb("g_i", [NH, TT]); self.g_lf = self.sb("g_lf", [NH, TT])
        self.g_B = self.sb("g_B", [NH, TT]); self.g_m = self.sb("g_m", [NH, TT])
        self.g_nbk = self.sb("g_nbk", [NH, TT]); self.g_wa = self.sb("g_wa", [NH, TT])
        self.g_small = self.sb("g_small", [NH, 128])
        self.b_g = fw.buf("gates")
        self.Rm = self.sb("Rm", [NH, NH, NT_BLK + NS])
        self.decbc = self.sb("decbc", [P, NH, NT_BLK + NS]); self.b_dec = fw.buf()
        self.wsk = self.sb("wsk", [P, NTL, NH]); self.thrT = self.sb("thrT", [P, NTL, NH]); self.b_wt = fw.bufs(NTL, "wt")
        self.ss = self.sb("ss", [P, 16]); self.b_ss = fw.buf()
        self.memset(self.ss[:], 1.0, w=[self.b_ss])
        self.rstd = self.sb("rstd", [P, 16]); self.b_rstd = fw.buf()
        self.junk = self.sb("junk", [P, D], BF16); self.b_junk = fw.buf()
        self.xn = self.sb("xn", [P, 2, D]); self.b_xn = fw.bufs(2, "xn"); self.xn_ptr = 0
        self.xr = self.sb("xr", [P, 2, D]); self.b_xr = fw.bufs(2, "xr"); self.xr_ptr = 0
        self.tmpn = self.sb("tmpn", [P, NKC, P]); self.b_tmpn = fw.buf()
        self.gmv = self.sb("gmv", [P, 3, 512]); self.b_gmv = fw.bufs(3, "gmv"); self.gmv_ptr = 0
        self.gmvs = self.sb("gmvs", [P, 3, SS]); self.b_gmvs = fw.bufs(3, "gmvs"); self.gmvs_ptr = 0
        self.tg = self.sb("tg", [P, 2, 512]); self.b_tg = fw.bufs(2, "tg"); self.tg_ptr = 0
        self.t2 = self.sb("t2", [P, 512]); self.b_t2 = fw.buf()
        self.sm = self.sb("sm", [P, NT_BLK, P], BF16); self.b_sm = fw.bufs(NT_BLK, "sm"); self.sm_ptr = 0
        self.rr = self.sb("rr", [P, 4]); self.b_rr = fw.bufs(4, "rr"); self.rr_ptr = 0
        self.ssq = self.sb("ssq", [P, NTL]); self.b_ssq = fw.buf()
        self.memset(self.ssq[:], 1.0, w=[self.b_ssq])
        self.rsq = self.sb("rsq", [P, NTL]); self.b_rsq = fw.buf()
        self.small_out = self.sb("small_out", [P, 512]); self.b_so = fw.buf()
        self.adaT = self.sb("adaT", [P, 48, 17]); self.b_ada = fw.buf()
        self.gsc1 = self.sb("gsc1", [P, NKC, 17]); self.gsc2 = self.sb("gsc2", [P, NKC, 17])
        self.hgt1 = self.sb("hgt1", [P, NKC, 17]); self.b_der = fw.buf(); self.b_der2 = fw.buf()
        self.sh1 = self.adaT[:, 0:8, :]
        self.sh2 = self.adaT[:, 24:32, :]
        self.gt2 = self.adaT[:, 40:48, :]

    def rot(self, name):
        ptr = getattr(self, name + "_ptr")
        setattr(self, name + "_ptr", ptr + 1)
        t = getattr(self, name)
        bl = getattr(self, "b_" + name)
        i = ptr % len(bl)
        return t, i, bl[i]

    def setup_ada(self):
        fw, d = self.fw, self.d
        call = self.aview(8192, D, F32, parts=17); b_c = fw.buf()
        self.load(call, d["call"], [b_c])
        th = self.aview(16384, D, F32, parts=17); b_th = fw.buf()
        self.act(th, call, AF.Tanh, r=[b_c], w=[b_th], scale=0.5)
        self.stt(th, th, 1.0, call, ALU.add, ALU.mult, r=[b_th, b_c], w=[b_th])
        pst, pbk = self.bank()
        for kc in range(NKC):
            self.tp(pst[:, kc * 17:(kc + 1) * 17], th[:, kc * P:(kc + 1) * P], self.identf[0:17, 0:17],
                    r=[b_th, self.b_idf], w=[pbk])
        self.cT = self.sb("cT", [P, NKC, 17], BF16); b_cT = fw.buf()
        self.act(self.cT[:].rearrange("p a b -> p (a b)"), pst[:, 0:NKC * 17], AF.Copy, r=[pbk], w=[b_cT], scale=0.5)
        self.b_cT = b_cT
        self.ada_panels(0, 4)
        A = self.adaT
        self.stt(self.gsc1[:], A[:, 8:16, :], 1.0, self.gmixT.unsqueeze(2).broadcast_to([P, NKC, 17]), ALU.add, ALU.mult,
                 r=[self.b_ada, self.b_vecT], w=[self.b_der])
        fw.barrier()
        self.ada_rest_pending = True

    def ada_rest(self):
        if not self.ada_rest_pending:
            return
        self.ada_rest_pending = False
        self.ada_panels(4, 12)
        A = self.adaT
        self.stt(self.gsc2[:], A[:, 32:40, :], 1.0, self.gffnT.unsqueeze(2).broadcast_to([P, NKC, 17]), ALU.add, ALU.mult,
                 r=[self.b_ada, self.b_vecT], w=[self.b_der2])
        self.ts(self.hgt1[:], A[:, 16:24, :], 0.5, None, ALU.mult, r=[self.b_ada], w=[self.b_der2])

    def ada_panels(self, p0, p1):
        fw, d = self.fw, self.d
        b_cT = self.b_cT
        for pnl in range(p0, p1):
            wv, wbuf = self.wpanel(d["w_ada"], pnl * 512, 512)
            for c4 in range(4):
                ch = pnl * 4 + c4
                pst, pbk = self.bank()
                for kc in range(NKC):
                    self.mm(pst[:, 0:17], wv[:, kc, c4 * P:(c4 + 1) * P], self.cT[:, kc, :], kc == 0, kc == NKC - 1,
                            r=wbuf + [b_cT], w=[pbk])
                self.act(self.adaT[:, ch, :], pst[:, 0:17], AF.Identity, r=[pbk, self.b_vecT], w=[self.b_ada],
                         bias=self.badaT[:, ch:ch + 1])

    def tile_list(self, has_s):
        tl = [(tt, P, tt * P) for tt in range(NT_BLK)]
        if has_s:
            tl.append((NT_BLK, SS, TB))
        return tl

    def segs(self, has_s):
        sg = [(0, TB, list(range(NT_BLK)))]
        if has_s:
            sg.append((TB, TB + SS, [NT_BLK]))
        return sg

    def x_src(self, b, tt, pc):
        if tt < NT_BLK:
            return self.d["xp"][(b * NT_BLK + tt) * P:(b * NT_BLK + tt + 1) * P, :]
        return self.d["xs"][SS * b:SS * (b + 1), :]

    def norm_gen(self, tl, gsc, sh, b, stream=False):
        fw = self.fw
        def fetch(tt, pc):
            if not stream:
                return self.xb[0:pc, tt, :], self.b_xb[tt]
            xr, ri, rb = self.rot("xr")
            self.load(xr[0:pc, ri, :], self.x_src(b, tt, pc), [rb])
            return xr[0:pc, ri, :], rb
        for (tt, pc, c0) in tl:
            src, sb_ = fetch(tt, pc)
            self.act(self.junk[0:pc, :], src, AF.Square, r=[sb_], w=[self.b_junk, self.b_ss],
                     accum_out=self.ss[0:pc, tt:tt + 1])
        n = len(tl)
        self.ts(self.rstd[:, 0:n], self.ss[:, 0:n], 1.0 / D, EPS, ALU.mult, ALU.add, r=[self.b_ss], w=[self.b_rstd])
        self.act(self.rstd[:, 0:n], self.rstd[:, 0:n], AF.Sqrt, r=[self.b_rstd], w=[self.b_rstd])
        self.recip(self.rstd[:, 0:n], self.rstd[:, 0:n], r=[self.b_rstd], w=[self.b_rstd])
        yield
        for i, (tt, pc, c0) in enumerate(tl):
            src, sb_ = fetch(tt, pc)
            xn, xi, xnb = self.rot("xn")
            self.act(xn[0:pc, xi, :], src, AF.Copy, r=[sb_, self.b_rstd], w=[xnb],
                     scale=self.rstd[0:pc, tt:tt + 1])
            pp, ppb = self.pair(i % 2)
            pv = pp.rearrange("p a b -> p (a b)").rearrange("p (k t) -> p k t", t=P)
            for kc in range(NKC):
                self.tp(pv[:, kc, 0:pc], xn[0:pc, xi, kc * P:(kc + 1) * P], self.identf[0:pc, 0:pc],
                        r=[xnb, self.b_idf], w=ppb)
            if pc == P:
                self.tt(self.tmpn[:], pv, gsc[:, :, 0:1].broadcast_to([P, NKC, P]), ALU.mult,
                        r=ppb + [self.b_der, self.b_der2, self.b_ada], w=[self.b_tmpn])
                self.tt(self.hT[:, :, c0:c0 + P], self.tmpn[:], sh[:, :, 0:1].broadcast_to([P, NKC, P]), ALU.add,
                        r=[self.b_tmpn, self.b_ada], w=[self.b_hT[tt]])
            else:
                tv = self.tmpn[:, :, 0:SS].rearrange("p k (s t) -> p k s t", t=ST)
                self.tt(tv, pv[:, :, 0:SS].rearrange("p k (s t) -> p k s t", t=ST),
                        gsc[:, :, 1 + NS * b:1 + NS * (b + 1)].unsqueeze(3).broadcast_to([P, NKC, NS, ST]), ALU.mult,
                        r=ppb + [self.b_der, self.b_der2, self.b_ada], w=[self.b_tmpn])
                self.tt(self.hT[:, :, c0:c0 + SS].rearrange("p k (s t) -> p k s t", t=ST), tv,
                        sh[:, :, 1 + NS * b:1 + NS * (b + 1)].unsqueeze(3).broadcast_to([P, NKC, NS, ST]), ALU.add,
                        r=[self.b_tmpn, self.b_ada], w=[self.b_hT[tt]])
            yield

    def spool_load(self, b):
        nrow = NS * 15
        self.load(self.t2[0:nrow, :], self.d["spool"][b * nrow:(b + 1) * nrow, :], [self.b_t2])

    def prep_gen(self, b):
        tl = self.tile_list(True)
        sg = self.segs(True)
        yield from self.norm_gen(tl, self.gsc1, self.sh1, b, stream=True)
        self.gates(b, True, sg, TB + SS)
        yield

    def hT_bufs(self, tiles):
        return [self.b_hT[t] for t in tiles]

    def block(self, b):
        fw, d, o = self.fw, self.d, self.o
        has_s = True
        self.cb = b
        self.gen_banks = (0, 1, 2, 3)
        tl = self.tile_list(has_s)
        sg = self.segs(has_s)
        t0 = b * NT_BLK
        ntl = len(tl)
        ncol = TB + (SS if has_s else 0)

        fw.barrier()
        self.aoff = 0
        self.gen_banks = (0, 1, 2, 3, 4, 5, 6, 7)
        self.pool_branch(b, has_s, sg, tl)

        fw.barrier()
        self.aoff = 0
        self.gen_banks = (0, 1, 2)
        self.heads(b, has_s, sg, tl)

        self.mixer_out(b, has_s, sg, tl)

        fw.barrier()
        self.aoff = 0
        self.ffn(b, has_s, sg, tl, t0)

    def gates(self, b, has_s, sg, ncol):
        fw, d, o = self.fw, self.d, self.o
        G = [self.b_g]
        gi, lf, gB, gm, nbk, wa = self.g_i, self.g_lf, self.g_B, self.g_m, self.g_nbk, self.g_wa
        for (c0, c1, tiles) in sg:
            w = c1 - c0
            pi, pib = self.bank()
            for kc in range(NKC):
                self.mm(pi[0:4, 0:w], self.wg[:, kc, 0:4], self.hT[:, kc, c0:c1], kc == 0, kc == NKC - 1,
                        r=[self.b_wg] + self.hT_bufs(tiles), w=[pib])
            pf, pfb = self.bank()
            for kc in range(NKC):
                self.mm(pf[0:4, 0:w], self.wg[:, kc, 4:8], self.hT[:, kc, c0:c1], kc == 0, kc == NKC - 1,
                        r=[self.b_wg] + self.hT_bufs(tiles), w=[pfb])
            self.act(gi[:, c0:c1], pi[0:4, 0:w], AF.Identity, r=[pib, self.b_bif], w=G, bias=self.bif[:, 0:1])
            self.act(lf[:, c0:c1], pf[0:4, 0:w], AF.Exp, r=[pfb, self.b_bif], w=G, bias=self.bif[:, 2:3], scale=-1.0)
        self.act(lf[:, 0:ncol], lf[:, 0:ncol], AF.Ln, r=G + [self.b_cols], w=G, bias=self.onec[0:4, :])
        self.ts(lf[:, 0:ncol], lf[:, 0:ncol], -1.0, None, ALU.mult, r=G, w=G)
        NTB = NT_BLK
        sml = self.g_small
        Kx = sml[:, 0:NTB + 1]
        dk = sml[:, 8:8 + NTB]
        dec = sml[:, 16:16 + NTB]
        Ks = sml[:, 32:32 + NS]
        dks = sml[:, 48:48 + NS]
        decs = sml[:, 64:64 + NS]
        cur = sml[:, 80:80 + NS]
        tmp = sml[:, 96:96 + NS]
        fw.op("dve", lambda e: e.tensor_tensor_scan(out=gB[:, 0:TB], data0=self.onesrow[:, 0:TB], data1=lf[:, 0:TB],
                                                    initial=0.0, op0=ALU.mult, op1=ALU.add), G + [self.b_const], G)
        fw.op("dve", lambda e: e.tensor_tensor_scan(out=gm[:, 0:TB], data0=lf[:, 0:TB], data1=gi[:, 0:TB],
                                                    initial=self.mprev[:, 0:1], op0=ALU.add, op1=ALU.max),
              G + [self.b_mprev], G)
        Bv = gB[:, 0:TB].rearrange("p (c t) -> p c t", t=P)
        mv = gm[:, 0:TB].rearrange("p (c t) -> p c t", t=P)
        self.tt(Kx[:, 1:NTB + 1], Bv[:, :, P - 1], mv[:, :, P - 1], ALU.subtract, r=G, w=G)
        self.ts(Kx[:, 0:1], self.mprev[:, 0:1], -1.0, None, ALU.mult, r=[self.b_mprev], w=G)
        self.tt(dk, Kx[:, 1:NTB + 1], Kx[:, 0:NTB], ALU.subtract, r=G, w=G)
        self.act(dec, dk, AF.Exp, r=G, w=G)
        self.tt(nbk[:, 0:TB].rearrange("p (c t) -> p c t", t=P), Kx[:, 1:NTB + 1].unsqueeze(2).broadcast_to([NH, NTB, P]),
                Bv, ALU.subtract, r=G, w=G)
        self.cp(self.mprev[:, 0:1], gm[:, TB - 1:TB], r=G, w=[self.b_mprev])
        nR = NTB
        if has_s:
            l3 = lf[:, TB:TB + SS].rearrange("p (s t) -> p s t", t=ST)
            i3 = gi[:, TB:TB + SS].rearrange("p (s t) -> p s t", t=ST)
            B3 = gB[:, TB:TB + SS].rearrange("p (s t) -> p s t", t=ST)
            m3 = gm[:, TB:TB + SS].rearrange("p (s t) -> p s t", t=ST)
            n3 = nbk[:, TB:TB + SS].rearrange("p (s t) -> p s t", t=ST)
            self.cp(B3[:, :, 0], l3[:, :, 0], r=G, w=G)
            for t in range(1, ST):
                self.tt(B3[:, :, t], B3[:, :, t - 1], l3[:, :, t], ALU.add, r=G, w=G)
            prev = self.m0s[:, NS * b:NS * (b + 1)]
            for t in range(ST):
                self.tt(tmp, l3[:, :, t], prev, ALU.add, r=G + [self.b_m0s], w=G)
                self.tt(m3[:, :, t], tmp, i3[:, :, t], ALU.max, r=G, w=G)
                prev = m3[:, :, t]
            self.tt(Ks, B3[:, :, ST - 1], m3[:, :, ST - 1], ALU.subtract, r=G, w=G)
            self.tt(dks, Ks, self.m0s[:, NS * b:NS * (b + 1)], ALU.add, r=G + [self.b_m0s], w=G)
            self.act(decs, dks, AF.Exp, r=G, w=G)
            self.tt(n3, Ks.unsqueeze(2).broadcast_to([NH, NS, ST]), B3, ALU.subtract, r=G, w=G)
            self.cp(self.small_out[0:4, 0:NS], m3[:, :, ST - 1], r=G, w=[self.b_so])
            self.store(o["nmsT"][:, NS * b:NS * (b + 1)], self.small_out[0:4, 0:NS], [self.b_so])
            nR = NTB + NS
        if b == NBLK - 1:
            self.cp(self.small_out[0:4, 32:33], gm[:, TB - 1:TB], r=G, w=[self.b_so])
            self.store(o["nmp"], self.small_out[0:4, 32:33], [self.b_so])
        self.tt(wa[:, 0:ncol], gi[:, 0:ncol], nbk[:, 0:ncol], ALU.add, r=G, w=G)
        self.act(nbk[:, 0:ncol], nbk[:, 0:ncol], AF.Exp, r=G, w=G)
        self.act(wa[:, 0:ncol], wa[:, 0:ncol], AF.Exp, r=G, w=G)
        idb = self.identf[0:4, 0:4].unsqueeze(2)
        self.tt(self.Rm[:, :, 0:NTB], idb.broadcast_to([NH, NH, NTB]), dec.unsqueeze(1).broadcast_to([NH, NH, NTB]),
                ALU.mult, r=G + [self.b_idf], w=G)
        if has_s:
            self.tt(self.Rm[:, :, NTB:NTB + NS], idb.broadcast_to([NH, NH, NS]), decs.unsqueeze(1).broadcast_to([NH, NH, NS]),
                    ALU.mult, r=G + [self.b_idf], w=G)
        self.gates_pending = (b, has_s)

    def gates_b(self):
        fw = self.fw
        b, has_s = self.gates_pending
        G = [self.b_g]
        wa, nbk = self.g_wa, self.g_nbk
        pr, prb = self.bank()
        nRt = NT_BLK + NS
        self.mm(pr[:, 0:NH * nRt], self.ones4[:, :], self.Rm[:].rearrange("p a b -> p (a b)"), True, True,
                r=G + [self.b_const], w=[prb])
        self.cp(self.decbc[:].rearrange("p a b -> p (a b)"), pr[:, 0:NH * nRt], r=[prb], w=[self.b_dec])
        for (tt, pc, c0) in self.tile_list(has_s):
            pt, ptb = self.bank()
            self.tp(pt[0:pc, 0:4], wa[:, c0:c0 + pc], self.identf[0:4, 0:4], r=G + [self.b_idf], w=[ptb])
            self.tp(pt[0:pc, 4:8], nbk[:, c0:c0 + pc], self.identf[0:4, 0:4], r=G + [self.b_idf], w=[ptb])
            self.ts(self.wsk[0:pc, tt, :], pt[0:pc, 0:4], 1.0 / 16.0, None, ALU.mult, r=[ptb], w=[self.b_wt[tt]])
            self.cp(self.thrT[0:pc, tt, :], pt[0:pc, 4:8], r=[ptb], w=[self.b_wt[tt]])
        if b == 0:
            self.dump("wsk0", self.wsk[:, 0:NT_BLK, :], self.b_wt[:NT_BLK], [P, NT_BLK, NH])
            self.dump("thr0", self.thrT[:, 0:NT_BLK, :], self.b_wt[:NT_BLK], [P, NT_BLK, NH])
            self.dump("dec0", self.decbc[:], [self.b_dec], [P, NH, NT_BLK + NS])
        if has_s:
            self.dump("wskL", self.wsk[:, :, :], self.b_wt, [P, NTL, NH])
            self.dump("thrL", self.thrT[:, :, :], self.b_wt, [P, NTL, NH])
            self.dump("decL", self.decbc[:], [self.b_dec], [P, NH, NT_BLK + NS])

    def pool_branch(self, b, has_s, sg, tl):
        fw, d, o = self.fw, self.d, self.o
        NU = 15 + TB
        uf = self.aview(self.carve(4 * NU * 4), 4 * NU, F32).rearrange("p (g n) -> p g n", g=4)
        b_uf = fw.bufs(4, "uf")
        sA = self.aview(self.carve(NU * 4), NU, F32); b_sA = fw.buf()
        sB = self.aview(self.carve(NU * 4), NU, F32); b_sB = fw.buf()
        pl = self.aview(self.carve(4 * TT * 2), 4 * TT, BF16).rearrange("p (g n) -> p g n", g=4)
        b_pl = [[fw.buf() for _ in sg] for g in range(4)]
        ypT = self.aview(self.carve(4 * TT * 2), 4 * TT, BF16).rearrange("p (g n) -> p g n", g=4)
        b_yp = [[fw.buf() for _ in sg] for g in range(4)]
        t15 = self.aview(self.carve(64), 15, F32); b_t15 = fw.buf()
        if has_s:
            fs = self.aview(self.carve(4 * NS * 19 * 4), 4 * NS * 19, F32).rearrange("p (g s n) -> p g s n", g=4, s=NS)
            b_fs = fw.bufs(4, "fs")
            s3A = self.aview(self.carve(NS * 19 * 4), NS * 19, F32).rearrange("p (s n) -> p s n", s=NS); b_s3A = fw.buf()
            s3B = self.aview(self.carve(NS * 19 * 4), NS * 19, F32).rearrange("p (s n) -> p s n", s=NS); b_s3B = fw.buf()
            us = self.aview(self.carve(4 * SS * 4), 4 * SS, F32).rearrange("p (g n) -> p g n", g=4); b_us = fw.bufs(4, "us")
            nrow = NS * 15
            spsb = self.t2[0:nrow, :]
            b_sp = self.b_t2
            ups = self.aview(self.carve(512 * 4), 512, F32, parts=SS); b_ups = fw.buf()
            b_np = fw.buf()
            fw.dma("sp", o["nps"][NS * b:NS * (b + 1), 0:11, :], d["spool"].rearrange("(s j) c -> s j c", j=15)[NS * b:NS * (b + 1), 4:15, :],
                   writes=[b_np], track=b_np, is_output=True)
        self.cp(uf[:, :, 0:15], self.uhist[:], r=[self.b_uh], w=b_uf)
        wu, wub = self.wpanel(d["w_in"], COL_U, 512)
        for g in range(4):
            for si, (c0, c1, tiles) in enumerate(sg):
                w = c1 - c0
                pu, pub = self.bank()
                for kc in range(NKC):
                    self.mm(pu[:, 0:w], wu[:, kc, g * P:(g + 1) * P], self.hT[:, kc, c0:c1], kc == 0, kc == NKC - 1,
                            r=wub + self.hT_bufs(tiles), w=[pub])
                if c0 < TB:
                    self.act(uf[:, g, 15 + c0:15 + c1], pu[:, 0:w], AF.Copy, r=[pub], w=[b_uf[g]])
                else:
                    self.act(fs[:, g, :, 15:19], pu[:, 0:SS].rearrange("p (s t) -> p s t", t=ST), AF.Copy, r=[pub], w=[b_fs[g]])
                    self.act(us[:, g, :], pu[:, 0:SS], AF.Copy, r=[pub], w=[b_us[g]])
        self.cp(self.uhist[:], uf[:, :, TB:TB + 15], r=b_uf, w=[self.b_uh])
        self.ada_rest()
        for g in range(4):
            f = uf[:, g, :]
            lv = [(sA, b_sA), (sB, b_sB)]
            src, srcb = f, b_uf[g]
            sh = 1
            for lvl in range(g + 1):
                dst, dstb = lv[lvl % 2]
                lo = 2 * sh - 1
                self.tt(dst[:, lo:NU], src[:, lo:NU], src[:, lo - sh:NU - sh], ALU.add, r=[srcb], w=[dstb])
                src, srcb = dst, dstb
                sh *= 2
            wsz = float(POOL_W[g])
            self.stt(pl[:, g, 0:TB], src[:, 15:NU], 1.0 / wsz, f[:, 15:NU], ALU.mult, ALU.subtract,
                     r=[srcb, b_uf[g]], w=[b_pl[g][0]])
            if b == 0:
                self.tt(t15, src[:, 15:30], self.icnt[:, g, :], ALU.mult, r=[srcb, self.b_const], w=[b_t15])
                self.tt(pl[:, g, 0:15], t15, f[:, 15:30], ALU.subtract, r=[b_t15, b_uf[g]], w=[b_pl[g][0]])
        self.final_b()
        t0 = b * NT_BLK
        for tt in range(NT_BLK):
            self.load(self.xb[:, tt, :], d["xp"][(t0 + tt) * P:(t0 + tt + 1) * P, :], [self.b_xb[tt]])
        self.load(self.xb[0:SS, NT_BLK, :], d["xs"][SS * b:SS * (b + 1), :], [self.b_xb[NT_BLK]])
        if has_s:
            for g in range(4):
                pt, ptb = self.bank()
                self.tp(pt[:, 0:nrow], spsb[:, g * P:(g + 1) * P], self.identf[0:nrow, 0:nrow], r=[b_sp, self.b_idf], w=[ptb])
                self.cp(fs[:, g, :, 0:15], pt[:, 0:nrow].rearrange("p (s n) -> p s n", n=15), r=[ptb], w=[b_fs[g]])
            for g in range(4):
                f3 = fs[:, g]
                lv = [(s3A, b_s3A), (s3B, b_s3B)]
                src, srcb = f3, b_fs[g]
                sh = 1
                for lvl in range(g + 1):
                    dst, dstb = lv[lvl % 2]
                    lo = 2 * sh - 1
                    self.tt(dst[:, :, lo:19], src[:, :, lo:19], src[:, :, lo - sh:19 - sh], ALU.add, r=[srcb], w=[dstb])
                    src, srcb = dst, dstb
                    sh *= 2
                wsz = float(POOL_W[g])
                self.stt(pl[:, g, TB:TB + SS].rearrange("p (s t) -> p s t", t=ST), src[:, :, 15:19], 1.0 / wsz, f3[:, :, 15:19],
                         ALU.mult, ALU.subtract, r=[srcb, b_fs[g]], w=[b_pl[g][1]])
            pt, ptb = self.bank()
            for g in range(4):
                self.tp(pt[0:SS, g * P:(g + 1) * P], us[:, g, :], self.identf[:, :], r=[b_us[g], self.b_idf], w=[ptb])
            self.cp(ups, pt[0:SS, :], r=[ptb], w=[b_ups])
            for t in range(ST):
                fw.dma("sp", o["nps"][NS * b:NS * (b + 1), 11 + t, :], ups[t:SS:ST, :], reads=[b_ups],
                       writes=[fw.buf()], track=b_ups, is_output=True)
        if b == NBLK - 1:
            pt, ptb = self.bank()
            for g in range(4):
                self.tp(pt[0:15, g * P:(g + 1) * P], uf[:, g, TB:TB + 15], self.identf[:, :], r=[b_uf[g], self.b_idf], w=[ptb])
            self.cp(self.small_out[0:15, :], pt[0:15, :], r=[ptb], w=[self.b_so])
            self.store(o["npp"], self.small_out[0:15, :], [self.b_so])
        if b == 0:
            self.dump("pl0", pl[:, :, 0:TB], [x[0] for x in b_pl], [P, 4, TB])
        for g in range(4):
            for si, (c0, c1, tiles) in enumerate(sg):
                w = c1 - c0
                pg, pgb = self.bank()
                self.mm(pg[:, 0:w], self.wpg[:, g, :], pl[:, g, c0:c1], True, True, r=[self.b_wpg, b_pl[g][si]], w=[pgb])
                self.act(ypT[:, g, c0:c1], pg[:, 0:w], AF.Copy, r=[pgb, self.b_vecT], w=[b_yp[g][si]], scale=self.pscT[:, g:g + 1])
        wpo, wpob = self.wload(d["w_po"].rearrange("(g p) n -> p g n", p=P), (4, D))
        self.b_mT = [[fw.buf() for _ in sg] for fc in range(NKC)]
        wga = None
        for fc in range(NKC):
            if fc % 4 == 0:
                wga, wgab = self.wpanel(d["w_in"], COL_GA + (fc // 4) * 512, 512)
            for si, (c0, c1, tiles) in enumerate(sg):
                w = c1 - c0
                pa, pab = self.bank()
                for g in range(4):
                    self.mm(pa[:, 0:w], wpo[:, g, fc * P:(fc + 1) * P], ypT[:, g, c0:c1], g == 0, g == 3,
                            r=wpob + [b_yp[g][si]], w=[pab])
                pg, pgb = self.bank()
                for kc in range(NKC):
                    self.mm(pg[:, 0:w], wga[:, kc, (fc % 4) * P:(fc % 4 + 1) * P], self.hT[:, kc, c0:c1], kc == 0, kc == NKC - 1,
                            r=wgab + self.hT_bufs(tiles), w=[pgb])
                tg, ti, tgb = self.rot("tg")
                self.act(tg[:, ti, 0:w], pg[:, 0:w], AF.Tanh, r=[pgb], w=[tgb], scale=0.5)
                self.stt(self.mT[:, fc, c0:c1], tg[:, ti, 0:w], 1.0, pa[:, 0:w], ALU.add, ALU.mult, r=[tgb, pab], w=[self.b_mT[fc][si]])
        self.gates_b()

    def heads(self, b, has_s, sg, tl):
        fw, d, o = self.fw, self.d, self.o
        ntl = len(tl)
        qT = self.aview(self.carve(2 * TT * 2), 2 * TT, BF16).rearrange("p (a n) -> p a n", a=2)
        kT = self.aview(self.carve(2 * TT * 2), 2 * TT, BF16).rearrange("p (a n) -> p a n", a=2)
        ktil = self.aview(self.carve(NTL * HD * 2), NTL * HD, BF16).rearrange("p (a n) -> p a n", a=NTL)
        vext = self.aview(self.carve(NTL * 258 * 2), NTL * 258, BF16).rearrange("p (a n) -> p a n", a=NTL)
        og = self.aview(self.carve(NTL * HD * 2), NTL * HD, BF16).rearrange("p (a n) -> p a n", a=NTL)
        hq = self.aview(self.carve(NTL * HD * 4), NTL * HD, F32).rearrange("p (a n) -> p a n", a=NTL)
        hmh = self.aview(self.carve(NTL * HD * 2), NTL * HD, BF16).rearrange("p (a n) -> p a n", a=NTL)
        to = self.aview(self.carve(HD * 4), HD, F32); b_to = fw.buf()
        self.hmT = self.aview(self.carve(NKC * TT * 2), NKC * TT, BF16).rearrange("p (a n) -> p a n", a=NKC)
        self.b_hmT = [fw.bufs(NTL, "hmT") for _ in range(NKC)]
        b_qT = [[fw.buf() for _ in sg] for _ in range(2)]
        b_kT = [fw.bufs(NTL, "kT") for _ in range(2)]
        b_kt = fw.bufs(NTL, "ktil"); b_v = fw.bufs(NTL, "vext"); b_og = fw.bufs(NTL, "og"); b_hq = fw.bufs(NTL, "hq")
        b_hmh = fw.bufs(NTL, "hmh")
        if has_s:
            NCB = 4
            Cin = self.aview(self.carve(NCB * 2 * 258 * 4), NCB * 2 * 258, F32).rearrange("p (i a n) -> p i a n", i=NCB, a=2)
            Cout = self.aview(self.carve(NCB * 2 * 258 * 4), NCB * 2 * 258, F32).rearrange("p (i a n) -> p i a n", i=NCB, a=2)
            Cbs = self.aview(self.carve(NCB * 2 * 258 * 2), NCB * 2 * 258, BF16).rearrange("p (i a n) -> p i a n", i=NCB, a=2)
            b_Cin = fw.bufs(NCB, "Cin"); b_Cout = fw.bufs(NCB, "Cout"); b_Cbs = fw.bufs(NCB, "Cbs")
            kz = self.aview(self.carve(NS * HD * 2), NS * HD, BF16, parts=SS).rearrange("p (s n) -> p s n", s=NS); b_kz = fw.buf()
            qz = self.aview(self.carve(2 * NS * SS * 2), 2 * NS * SS, BF16).rearrange("p (a s n) -> p a s n", a=2, s=NS); b_qz = fw.buf()
            nT = self.aview(self.carve(2 * NBH * 4), 2 * NBH, F32).rearrange("p (a n) -> p a n", a=2); b_nT = fw.buf()
            nTn = self.aview(self.carve(2 * NBH * 4), 2 * NBH, F32).rearrange("p (a n) -> p a n", a=2); b_nTn = fw.buf()
            sn_sb = self.aview(self.carve(HD * 4), HD, F32, parts=NBH); b_sn = fw.buf()
            sms = self.aview(self.carve(SS * 2), SS, BF16, parts=SS); b_sms = fw.buf()
            self.load(sn_sb, d["sn"][NBH * b:NBH * (b + 1), :], [b_sn])
            pt, ptb = self.bank()
            for dc in range(2):
                self.tp(pt[:, dc * NBH:(dc + 1) * NBH], sn_sb[:, dc * P:(dc + 1) * P], self.identf[0:NBH, 0:NBH], r=[b_sn, self.b_idf], w=[ptb])
            self.cp(nT[:].rearrange("p a n -> p (a n)"), pt[:, 0:2 * NBH], r=[ptb], w=[b_nT])
        self.memset(vext[:, :, 256:257], 1.0, w=b_v)
        ci = 0
        def head_norm(hd):
            self.gen_banks_save = self.gen_banks
            self.ts(self.rsq[:, 0:ntl], self.ssq[:, 0:ntl], 1.0 / HD, EPS, ALU.mult, ALU.add, r=[self.b_ssq], w=[self.b_rsq])
            self.act(self.rsq[:, 0:ntl], self.rsq[:, 0:ntl], AF.Sqrt, r=[self.b_rsq], w=[self.b_rsq])
            self.recip(self.rsq[:, 0:ntl], self.rsq[:, 0:ntl], r=[self.b_rsq], w=[self.b_rsq])
            for (tt, pc, c0) in tl:
                self.stt(hmh[0:pc, tt, :], hq[0:pc, tt, :], self.rsq[0:pc, tt:tt + 1], og[0:pc, tt, :], ALU.mult, ALU.mult,
                         r=[b_hq[tt], self.b_rsq, b_og[tt]], w=[b_hmh[tt]])
            for (tt, pc, c0) in tl:
                ptr, ptrb = self.bank()
                pv = ptr.bitcast(BF16)
                for dc in range(2):
                    self.tp(pv[:, dc * P:dc * P + pc], hmh[0:pc, tt, dc * P:(dc + 1) * P], self.identb[0:pc, 0:pc],
                            r=[b_hmh[tt], self.b_idb], w=[ptrb])
                for dc in range(2):
                    self.cp(self.hmT[:, 2 * hd + dc, c0:c0 + pc], pv[:, dc * P:dc * P + pc], r=[ptrb], w=[self.b_hmT[2 * hd + dc][tt]])
            if b == 0 and hd == 0 and False:
                self.dump("hq0", hq[:, 0:NT_BLK, :], b_hq[:NT_BLK], [P, NT_BLK, HD])
                self.dump("hmh0", hmh[:, 0:NT_BLK, :], b_hmh[:NT_BLK], [P, NT_BLK, HD])
        assert NS == NT_BLK
        for hd in range(NH):
            self.gen_banks = (0, 1, 2, 3, 4, 5, 6, 7)
            for s_ in range(NS):
                bhg = (NS * b + s_) * NH + hd
                self.load(Cin[:, s_, :, 0:HD], d["sC"][bhg].rearrange("(a p) e -> p a e", p=P), [b_Cin[s_]])
            wq, wqb = self.wpanel(d["w_in"], COL_Q + hd * HD, HD)
            for dc in range(2):
                for si, (c0, c1, tiles) in enumerate(sg):
                    w = c1 - c0
                    pq, pqb = self.bank()
                    for kc in range(NKC):
                        self.mm(pq[:, 0:w], wq[:, kc, dc * P:(dc + 1) * P], self.hT[:, kc, c0:c1], kc == 0, kc == NKC - 1,
                                r=wqb + self.hT_bufs(tiles), w=[pqb])
                    self.act(qT[:, dc, c0:c1], pq[:, 0:w], AF.Copy, r=[pqb], w=[b_qT[dc][si]])
            wk, wkb = self.wpanel(d["w_in"], COL_K + hd * HD, HD)
            def k_tr(tt, pc, c0):
                ptr, ptrb = self.bank()
                pv = ptr.bitcast(BF16)
                for dc in range(2):
                    self.tp(pv[:, dc * P:dc * P + pc], ktil[0:pc, tt, dc * P:(dc + 1) * P], self.identb[0:pc, 0:pc],
                            r=[b_kt[tt], self.b_idb], w=[ptrb])
                for dc in range(2):
                    self.cp(kT[:, dc, c0:c0 + pc], pv[:, dc * P:dc * P + pc], r=[ptrb], w=[b_kT[dc][tt]])
            prev_k = None
            for (tt, pc, c0) in tl:
                pk, pkb = self.bank()
                for kc in range(NKC):
                    self.mm(pk[0:pc, 0:HD], self.hT[:, kc, c0:c0 + pc], wk[:, kc, :], kc == 0, kc == NKC - 1,
                            r=wkb + [self.b_hT[tt]], w=[pkb])
                self.act(ktil[0:pc, tt, :], pk[0:pc, 0:HD], AF.Copy, r=[pkb, self.b_wt[tt]], w=[b_kt[tt]],
                         scale=self.wsk[0:pc, tt, hd:hd + 1])
                if prev_k is not None:
                    k_tr(*prev_k)
                prev_k = (tt, pc, c0)
            k_tr(*prev_k)
            wv_, wvb = self.wpanel(d["w_in"], COL_V + hd * HD, HD)
            for (tt, pc, c0) in tl:
                pk, pkb = self.bank()
                for kc in range(NKC):
                    self.mm(pk[0:pc, 0:HD], self.hT[:, kc, c0:c0 + pc], wv_[:, kc, :], kc == 0, kc == NKC - 1,
                            r=wvb + [self.b_hT[tt]], w=[pkb])
                self.cp(vext[0:pc, tt, 0:HD], pk[0:pc, 0:HD], r=[pkb], w=[b_v[tt]])
            if hd > 0:
                head_norm(hd - 1)
            wo, wob = self.wpanel(d["w_in"], COL_O + hd * HD, HD)
            for (tt, pc, c0) in tl:
                pk, pkb = self.bank()
                for kc in range(NKC):
                    self.mm(pk[0:pc, 0:HD], self.hT[:, kc, c0:c0 + pc], wo[:, kc, :], kc == 0, kc == NKC - 1,
                            r=wob + [self.b_hT[tt]], w=[pkb])
                self.act(to[0:pc, :], pk[0:pc, 0:HD], AF.Tanh, r=[pkb], w=[b_to], scale=0.5)
                self.stt(og[0:pc, tt, :], to[0:pc, :], 1.0, self.mnh[0:pc, hd * HD:(hd + 1) * HD], ALU.add, ALU.mult,
                         r=[b_to, self.b_mnh], w=[b_og[tt]])
            if b == 0 and hd == 0:
                self.dump("qT0", qT[:, :, 0:TB], [x[0] for x in b_qT], [P, 2, TB])
                self.dump("kT0", kT[:, :, 0:TB], b_kT[0][:NT_BLK] + b_kT[1][:NT_BLK], [P, 2, TB])
                self.dump("vx0", vext[:, 0:NT_BLK, :], b_v[:NT_BLK], [P, NT_BLK, 258])
            self.act(self.Csb[:, hd, :, 0:257], self.Cst[:, hd, :, :], AF.Copy, r=[self.b_C[hd], self.b_dec], w=[self.b_Csb[hd]],
                     scale=self.decbc[:, hd, 0:1])
            self.gen_banks = (7,)
            tts = NT_BLK
            cs0 = TB
            for s_ in range(NS):
                bh = s_ * NH + hd
                self.cp(Cin[:, s_, :, 256:257], nT[:, :, bh:bh + 1], r=[b_nT], w=[b_Cin[s_]])
                dcol = self.decbc[:, hd, NT_BLK + s_:NT_BLK + s_ + 1]
                self.act(Cbs[:, s_, :, 0:257], Cin[:, s_, :, 0:257], AF.Copy, r=[b_Cin[s_], self.b_dec], w=[b_Cbs[s_]], scale=dcol)
            pS, pSb = self.ps[:, 7, :], self.pb[7]
            for dc in range(2):
                self.mm(pS[0:SS, 0:SS], kT[:, dc, cs0:cs0 + SS], qT[:, dc, cs0:cs0 + SS], dc == 0, dc == 1,
                        r=[b_kT[dc][tts], b_qT[dc][1]], w=[pSb])
            self.tt(sms, pS[0:SS, 0:SS], self.masks[:], ALU.mult, r=[pSb, self.b_mk], w=[b_sms])
            for dc in range(2):
                self.tt(qz[:, dc], qT[:, dc, cs0:cs0 + SS].unsqueeze(1).broadcast_to([P, NS, SS]), self.cmb[:], ALU.mult,
                        r=[b_qT[dc][1], self.b_cm], w=[b_qz])
            self.tt(kz, ktil[0:SS, tts, :].unsqueeze(1).broadcast_to([SS, NS, HD]),
                    self.rm[:, :].unsqueeze(2).broadcast_to([SS, NS, HD]), ALU.mult, r=[b_kt[tts], self.b_const], w=[b_kz])
            pNs, pNsb = self.ps[:, 6, :], self.pb[6]
            self.mm(pNs[0:SS, 0:257], sms, vext[0:SS, tts, 0:257], True, False, r=[b_sms, b_v[tts]], w=[pNsb])
            pS2, pS2b = self.ps[:, 2, :], self.pb[2]
            for tt in range(NT_BLK):
                c0 = tt * P
                for dc in range(2):
                    self.mm(pS2[:, tt * P:(tt + 1) * P], kT[:, dc, c0:c0 + P], qT[:, dc, c0:c0 + P], dc == 0, dc == 1,
                            r=[b_kT[dc][tt], b_qT[dc][0]], w=[pS2b])
            sm = self.sm
            for tt in range(NT_BLK):
                self.tt(sm[:, tt, :], pS2[:, tt * P:(tt + 1) * P], self.maskp[:], ALU.mult, r=[pS2b, self.b_mk], w=[self.b_sm[tt]])
            for tt in range(NT_BLK):
                c0 = tt * P
                smi, smb = tt, self.b_sm[tt]
                pC, pCb = self.pair(0)
                for dc in range(2):
                    self.mm(pC[:, dc, 0:257], ktil[:, tt, dc * P:(dc + 1) * P], vext[:, tt, 0:257], True, True,
                            r=[b_kt[tt], b_v[tt]], w=[pCb[dc]])
                nb_ = 3 if tt % 2 == 0 else 2
                pN, pNb = self.ps[:, nb_, :], self.pb[nb_]
                self.mm(pN[:, 0:257], sm[:, smi, :], vext[:, tt, 0:257], True, False, r=[smb, b_v[tt]], w=[pNb])
                for dc in range(2):
                    self.mm(pN[:, 0:257], qT[:, dc, c0:c0 + P], self.Csb[:, hd, dc, 0:257], False, dc == 1,
                            r=[b_qT[dc][0], self.b_Csb[hd]], w=[pNb])
                self.chunk_post(pN, pNb, P, tt, hd, hq, b_hq)
                self.stt(self.Cst[:, hd, :, :], self.Cst[:, hd, :, :], self.decbc[:, hd, tt:tt + 1], pC[:, :, 0:257], ALU.mult, ALU.add,
                         r=[self.b_C[hd], self.b_dec] + pCb, w=[self.b_C[hd]])
                if tt + 1 < NT_BLK:
                    self.act(self.Csb[:, hd, :, 0:257], self.Cst[:, hd, :, :], AF.Copy, r=[self.b_C[hd], self.b_dec],
                             w=[self.b_Csb[hd]], scale=self.decbc[:, hd, tt + 1:tt + 2])
                s_ = tt
                bh = s_ * NH + hd
                bhg = (NS * b + s_) * NH + hd
                dcol = self.decbc[:, hd, NT_BLK + s_:NT_BLK + s_ + 1]
                for dc in range(2):
                    self.mm(pNs[0:SS, 0:257], qz[:, dc, s_, :], Cbs[:, s_, dc, 0:257], False, (s_ == NS - 1 and dc == 1),
                            r=[b_qz, b_Cbs[s_]], w=[pNsb])
                pC2 = self.ps[:, 0:2, :]
                pC2b = [self.pb[0], self.pb[1]]
                for dc in range(2):
                    self.mm(pC2[:, dc, 0:257], kz[:, s_, dc * P:(dc + 1) * P], vext[0:SS, tts, 0:257], True, True,
                            r=[b_kz, b_v[tts]], w=[pC2b[dc]])
                self.stt(Cout[:, s_, :, 0:257], Cin[:, s_, :, 0:257], dcol, pC2[:, :, 0:257], ALU.mult, ALU.add,
                         r=[b_Cin[s_], self.b_dec] + pC2b, w=[b_Cout[s_]])
                fw.dma("sp", o["ncs"][bhg].rearrange("(a p) e -> p a e", p=P), Cout[:, s_, :, 0:HD], reads=[b_Cout[s_]],
                       writes=[fw.buf()], track=b_Cout[s_], is_output=True)
                self.cp(nTn[:, :, bh:bh + 1], Cout[:, s_, :, 256:257], r=[b_Cout[s_]], w=[b_nTn])
            self.chunk_post(pNs, pNsb, SS, tts, hd, hq, b_hq)
            self.gen_banks = (0, 1, 2)
            pass
        head_norm(NH - 1)
        if b == NBLK - 1:
            for hd in range(NH):
                fw.dma("sp", o["ncp"][hd].rearrange("(a p) e -> p a e", p=P), self.Cst[:, hd, :, 0:HD], reads=[self.b_C[hd]],
                       writes=[fw.buf()], track=self.b_C[hd], is_output=True)
            nP = self.small_out[:, 256:264]
            self.cp(nP.rearrange("p (h a) -> p h a", a=2), self.Cst[:, :, :, 256], r=self.b_C, w=[self.b_so])
            pt, ptb = self.bank()
            self.tp(pt[0:8, 0:P], nP, self.identf[:, :], r=[self.b_so, self.b_idf], w=[ptb])
            self.cp(self.small_out[0:8, 300:300 + P], pt[0:8, 0:P], r=[ptb], w=[self.b_so])
            self.store(o["nnp"], self.small_out[0:8, 300:300 + P], [self.b_so])
        if has_s:
            pt, ptb = self.bank()
            for dc in range(2):
                self.tp(pt[0:NBH, dc * P:(dc + 1) * P], nTn[:, dc, :], self.identf[:, :], r=[b_nTn, self.b_idf], w=[ptb])
            self.cp(sn_sb, pt[0:NBH, 0:HD], r=[ptb], w=[b_sn])
            self.store(o["nns"][NBH * b:NBH * (b + 1), :], sn_sb, [b_sn])

    def chunk_post(self, pN, pNb, pc, tt, hd, hq, b_hq):
        rr, ri, rrb = self.rot("rr")
        self.act(rr[0:pc, ri:ri + 1], pN[0:pc, 256:257], AF.Abs, r=[pNb], w=[rrb])
        self.ts(rr[0:pc, ri:ri + 1], rr[0:pc, ri:ri + 1], self.thrT[0:pc, tt, hd:hd + 1], None, ALU.max,
                r=[rrb, self.b_wt[tt]], w=[rrb])
        self.recip(rr[0:pc, ri:ri + 1], rr[0:pc, ri:ri + 1], r=[rrb], w=[rrb])
        self.act(hq[0:pc, tt, :], pN[0:pc, 0:HD], AF.Copy, r=[pNb, rrb], w=[b_hq[tt]], scale=rr[0:pc, ri:ri + 1])
        self.act(self.junk[0:pc, 0:HD], pN[0:pc, 0:HD], AF.Square, r=[pNb, rrb], w=[self.b_junk, self.b_ssq],
                 scale=rr[0:pc, ri:ri + 1], accum_out=self.ssq[0:pc, tt:tt + 1])

    def xupdate_a(self, pM, pMb, c0, c1, tiles, fc, gtT):
        w = c1 - c0
        if c0 < TB:
            gmv, gi, gb = self.rot("gmv")
            g2 = gmv[:, gi, :]
            self.act(g2[:, 0:w], pM[:, 0:w], AF.Copy, r=[pMb, self.b_ada, self.b_der, self.b_der2], w=[gb], scale=gtT[:, fc, 0:1])
        else:
            gmv, gi, gb = self.rot("gmvs")
            g2 = gmv[:, gi, :]
            self.tt(g2[:, 0:SS].rearrange("p (s t) -> p s t", t=ST), pM[:, 0:SS].rearrange("p (s t) -> p s t", t=ST),
                    gtT[:, fc, 1 + NS * self.cb:1 + NS * (self.cb + 1)].unsqueeze(2).broadcast_to([P, NS, ST]), ALU.mult, r=[pMb, self.b_ada, self.b_der, self.b_der2], w=[gb])
        return (g2, gb, c0, tiles, fc)

    def xupdate_b(self, st):
        g2, gb, c0, tiles, fc = st
        pT, pTb = self.bankN()
        pv = pT.rearrange("p (j f) -> p j f", f=P)
        for j, tt in enumerate(tiles):
            pc = P if tt < NT_BLK else SS
            self.tp(pv[0:pc, j, :], g2[:, j * P:j * P + pc], self.identf[:, :], r=[gb, self.b_idf], w=[pTb])
        if c0 < TB:
            n = len(tiles)
            xv = self.xb[:, tiles[0]:tiles[0] + n, fc * P:(fc + 1) * P]
            self.tt(xv, xv, pv[:, 0:n, :], ALU.add, r=[pTb] + [self.b_xb[t] for t in tiles], w=[self.b_xb[t] for t in tiles])
        else:
            xv = self.xb[0:SS, NT_BLK, fc * P:(fc + 1) * P]
            self.tt(xv, xv, pv[0:SS, 0, :], ALU.add, r=[pTb, self.b_xb[NT_BLK]], w=[self.b_xb[NT_BLK]])

    def mixer_out(self, b, has_s, sg, tl):
        fw, d, o = self.fw, self.d, self.o
        self.gen_banks = (0, 1, 2, 3, 4, 5, 6, 7)
        for fc in range(NKC):
            if fc % 4 == 0:
                wmo, wmob = self.wpanel(d["w_mo"], (fc // 4) * 512, 512)
                wgb_, wgbb = self.wpanel(d["w_in"], COL_GB + (fc // 4) * 512, 512)
            for si, (c0, c1, tiles) in enumerate(sg):
                w = c1 - c0
                pB, pBb = self.bank()
                for kc in range(NKC):
                    self.mm(pB[:, 0:w], wmo[:, kc, (fc % 4) * P:(fc % 4 + 1) * P], self.hmT[:, kc, c0:c1], kc == 0, kc == NKC - 1,
                            r=wmob + [self.b_hmT[kc][t] for t in tiles], w=[pBb])
                pg, pgb = self.bank()
                for kc in range(NKC):
                    self.mm(pg[:, 0:w], wgb_[:, kc, (fc % 4) * P:(fc % 4 + 1) * P], self.hT[:, kc, c0:c1], kc == 0, kc == NKC - 1,
                            r=wgbb + self.hT_bufs(tiles), w=[pgb])
                tg, ti, tgb = self.rot("tg")
                self.act(tg[:, ti, 0:w], pg[:, 0:w], AF.Tanh, r=[pgb], w=[tgb], scale=0.5)
                self.stt(self.t2[:, 0:w], tg[:, ti, 0:w], 1.0, pB[:, 0:w], ALU.add, ALU.mult, r=[tgb, pBb], w=[self.b_t2])
                self.tt(self.mT[:, fc, c0:c1], self.t2[:, 0:w], self.mT[:, fc, c0:c1], ALU.add, r=[self.b_t2, self.b_mT[fc][si]],
                        w=[self.b_mT[fc][si]])
        if b == 0:
            self.dump("mT0", self.mT[:, :, 0:TB], [x[0] for x in self.b_mT], [P, NKC, TB])
        self.gen_banks = (0, 1, 2, 4, 5, 7)
        pend = []
        for fc in range(NKC):
            if fc % 4 == 0:
                wou, woub = self.wpanel(d["w_out"], (fc // 4) * 512, 512)
            cur = []
            for si, (c0, c1, tiles) in enumerate(sg):
                w = c1 - c0
                pM, pMb = self.bank()
                for kc in range(NKC):
                    self.mm(pM[:, 0:w], wou[:, kc, (fc % 4) * P:(fc % 4 + 1) * P], self.mT[:, kc, c0:c1], kc == 0, kc == NKC - 1,
                            r=woub + [self.b_mT[kc][si]], w=[pMb])
                cur.append(self.xupdate_a(pM, pMb, c0, c1, tiles, fc, self.hgt1))
            for st_ in pend:
                self.xupdate_b(st_)
            pend = cur
        for st_ in pend:
            self.xupdate_b(st_)
        if b == 0:
            self.dump("x10", self.xb[:, 0:NT_BLK, :], self.b_xb[:NT_BLK], [P, NT_BLK, D])
        for _ in self.norm_gen(tl, self.gsc2, self.sh2, b):
            pass

    def ffn(self, b, has_s, sg, tl, t0):
        fw, d, o = self.fw, self.d, self.o
        hidT = self.aview(self.carve(32 * TT * 2), 32 * TT, BF16).rearrange("p (j n) -> p j n", j=32)
        b_hid = [[fw.buf() for _ in sg] for _ in range(32)]
        rl = self.aview(self.carve(2 * 512 * 4), 2 * 512, F32).rearrange("p (i n) -> p i n", i=2); b_rl = fw.bufs(2, "rl")
        rli = 0
        self.gen_banks = (0, 1, 2, 3, 4, 5, 6, 7)
        for j in range(32):
            if j % 4 == 0:
                w1p, w1b = self.wpanel(d["w1"], (j // 4) * 512, 512)
            for si, (c0, c1, tiles) in enumerate(sg):
                w = c1 - c0
                pH, pHb = self.bank()
                for kc in range(NKC):
                    self.mm(pH[:, 0:w], w1p[:, kc, (j % 4) * P:(j % 4 + 1) * P], self.hT[:, kc, c0:c1], kc == 0, kc == NKC - 1,
                            r=w1b + self.hT_bufs(tiles), w=[pHb])
                ri = rli % 2
                rli += 1
                self.act(rl[:, ri, 0:w], pH[:, 0:w], AF.Relu, r=[pHb], w=[b_rl[ri]])
                self.tt(hidT[:, j, c0:c1], pH[:, 0:w], rl[:, ri, 0:w], ALU.mult, r=[pHb, b_rl[ri]], w=[b_hid[j][si]])
        self.gen_banks = (0, 1, 2)
        pend = []
        nxt = self.prep_gen(b + 1) if b + 1 < NBLK else iter(())
        if b + 1 < NBLK:
            self.spool_load(b + 1)
        next(nxt, None)
        for fc in range(NKC):
            w2p, w2b = self.wload(d["w2"][:, fc * P:(fc + 1) * P].rearrange("(j p) n -> p j n", p=P), (32, P))
            cur = []
            for si, (c0, c1, tiles) in enumerate(sg):
                w = c1 - c0
                pO, pOb = self.bank()
                for j in range(32):
                    self.mm(pO[:, 0:w], w2p[:, j, :], hidT[:, j, c0:c1], j == 0, j == 31, r=w2b + [b_hid[j][si]], w=[pOb])
                cur.append(self.xupdate_a(pO, pOb, c0, c1, tiles, fc, self.gt2))
            for st_ in pend:
                self.xupdate_b(st_)
            pend = cur
            if fc == NKC - 1:
                for st_ in pend:
                    self.xupdate_b(st_)
                pend = []
            if fc != 0:
                next(nxt, None)
        for st_ in pend:
            self.xupdate_b(st_)
        for _ in nxt:
            pass
        for (tt, pc, c0) in tl:
            self.act(self.junk[0:pc, :], self.xb[0:pc, tt, :], AF.Square, r=[self.b_xb[tt]], w=[self.b_junk, self.b_ss],
                     accum_out=self.ss[0:pc, tt:tt + 1])
        n = len(tl)
        self.ts(self.rstd[:, 0:n], self.ss[:, 0:n], 1.0 / D, EPS, ALU.mult, ALU.add, r=[self.b_ss], w=[self.b_rstd])
        self.act(self.rstd[:, 0:n], self.rstd[:, 0:n], AF.Sqrt, r=[self.b_rstd], w=[self.b_rstd])
        self.recip(self.rstd[:, 0:n], self.rstd[:, 0:n], r=[self.b_rstd], w=[self.b_rstd])
        self.final_pending = (b, tl, t0)
        if b == NBLK - 1:
            self.final_b()

    def final_b(self):
        if self.final_pending is None:
            return
        fw, o = self.fw, self.o
        b, tl, t0 = self.final_pending
        self.final_pending = None
        for i, (tt, pc, c0) in enumerate(tl):
            self.stt(self.xb[0:pc, tt, :], self.xb[0:pc, tt, :], self.rstd[0:pc, tt:tt + 1], self.gfin[0:pc, :], ALU.mult, ALU.mult,
                     r=[self.b_xb[tt], self.b_rstd, self.b_gfin], w=[self.b_xb[tt]])
            if tt < NT_BLK:
                dst = o["yp"][(t0 + tt) * P:(t0 + tt + 1) * P, :]
            else:
                dst = o["ys"][SS * b:SS * (b + 1), :]
            fw.dma("sp", dst, self.xb[0:pc, tt, :], reads=[self.b_xb[tt]], writes=[fw.buf()], track=self.b_xb[tt], is_output=True)


_CACHE = {}


def _program(dbg=()):
    key = tuple(sorted(dbg))
    if key not in _CACHE:
        _CACHE[key] = MK(dbg)
    return _CACHE[key]


def make_in_maps(inputs):
    f = lambda a: np.ascontiguousarray(np.asarray(a, dtype=np.float32))
    I = {k: f(v) for k, v in inputs.items()}
    consts = _consts()
    vec_common = [I["g_mix"][0].reshape(8, P), I["g_ffn"][0].reshape(8, P), I["b_ada"][0].reshape(48, P),
                  I["pool_scale"][0].reshape(4, P)]
    vecs = np.concatenate(vec_common, axis=0)
    bif = np.stack([I["b_i"][0], I["b_f"][0]], axis=1)
    shared = dict(vecs=vecs, bif=bif, m_norm=I["m_norm"][0], g_final=I["g_final"], w_ada=I["w_ada"][0], w_in=I["w_in"][0],
                  w_pg=I["w_pool_group"][0], w_po=I["w_pool_out"][0], w_mo=I["w_m_out"][0], w_out=I["w_out"][0],
                  w1=I["w1"][0], w2=I["w2"][0], **consts)
    maps = []
    for c in range(8):
        s0, s1 = c * NS_CORE, (c + 1) * NS_CORE
        m = dict(shared)
        m["xp"] = I["x_prompt"][c]
        m["xs"] = I["x_sample"][s0:s1].reshape(SS_CORE, D)
        m["call"] = np.concatenate([I["c_prompt"][c:c + 1], I["c_sample"][s0:s1]], axis=0)
        m["spool"] = I["state_pool"][0, s0:s1].reshape(NS_CORE * 15, 512)
        m["sC"] = I["state_C"][0, s0:s1].reshape(NS_CORE * NH, HD, HD)
        m["sn"] = I["state_n"][0, s0:s1].reshape(NS_CORE * NH, HD)
        m["smT"] = np.ascontiguousarray(I["state_m"][0, s0:s1].T)
        maps.append(m)
    return maps


def kernel(**inputs):
    prog = _program()
    maps = make_in_maps(inputs)
    res = run_bass_kernel_spmd(prog.nc, maps, core_ids=list(range(8)))
    R = res.results
    y_prompt = np.stack([R[c]["yp"] for c in range(8)], axis=0)
    y_sample = np.concatenate([R[c]["ys"].reshape(NS_CORE, ST, D) for c in range(8)], axis=0)
    npp = np.stack([R[c]["npp"] for c in range(8)], axis=0)[None]
    ncp = np.stack([R[c]["ncp"] for c in range(8)], axis=0)[None]
    nnp = np.stack([R[c]["nnp"].reshape(NH, HD) for c in range(8)], axis=0)[None]
    nmp = np.stack([R[c]["nmp"].reshape(NH) for c in range(8)], axis=0)[None]
    nps = np.concatenate([R[c]["nps"] for c in range(8)], axis=0)[None]
    ncs = np.concatenate([R[c]["ncs"].reshape(NS_CORE, NH, HD, HD) for c in range(8)], axis=0)[None]
    nns = np.concatenate([R[c]["nns"].reshape(NS_CORE, NH, HD) for c in range(8)], axis=0)[None]
    nms = np.concatenate([np.ascontiguousarray(R[c]["nmsT"].T) for c in range(8)], axis=0)[None]
    outs = (y_prompt, y_sample, npp, ncp, nnp, nmp, nps, ncs, nns, nms)
    return tuple(np.ascontiguousarray(a, dtype=np.float32) for a in outs)
```
